# Optimizing a Trainium2 kernel written in Bass

```python
import math
import jax, jax.numpy as jnp
from jax import lax
import numpy as np

D_MODEL = 1024
BATCH = 16
SEQ = 4096
DEPTH = 2
DEC_BATCH = 8
DEC_SEQ = 64
PAST_LEN = 1024

CHUNK = 64
N_AB = (DEPTH + 1) // 2
N_CD = DEPTH // 2
RMS_EPS = 1e-5
LN_EPS = 1e-5
A_WIDTH = D_MODEL // 2
A_CONV = 3
B_WIDTH = D_MODEL // 2
B_GROUPS = 4
B_GROUP_DIM = B_WIDTH // B_GROUPS
B_CHUNK = 128
C_HEADS = 8
C_HEAD_DIM = 64
C_WIDTH = C_HEADS * C_HEAD_DIM
C_PREV_CHUNKS = 8
C_BAND = C_PREV_CHUNKS * CHUNK
C_KEYS = C_BAND + CHUNK
C_MAX_REL = 128
D_HEADS = 8
D_HEAD_DIM = 64
D_INNER = D_HEADS * D_HEAD_DIM
D_GROUPS = 2
D_STATE = 128
D_CONV = 4
D_XBC = D_INNER + 2 * D_GROUPS * D_STATE
DT_MIN = 0.001
DT_MAX = 0.1
AB_IN = 3 * A_WIDTH + 2 * B_WIDTH
CD_IN = 3 * C_WIDTH + D_INNER + D_XBC + D_HEADS
MIX_WIDTH = A_WIDTH + B_WIDTH
FFN_HIDDEN = -(-(8 * D_MODEL) // (3 * 256)) * 256

kernel_name = 'hybrid_streaming_encoder_step'

F32 = jnp.float32


def rmsnorm(x, g):
    xf = x.astype(F32)
    y = xf * lax.rsqrt(jnp.mean(xf * xf, axis=-1, keepdims=True) + RMS_EPS)
    return (y * g.astype(F32)).astype(x.dtype)


def layernorm(x, g, b):
    xf = x.astype(F32)
    mu = jnp.mean(xf, axis=-1, keepdims=True)
    xc = xf - mu
    var = jnp.mean(xc * xc, axis=-1, keepdims=True)
    return (xc * lax.rsqrt(var + LN_EPS) * g.astype(F32) + b.astype(F32)).astype(x.dtype)


def swiglu(x, wg, wu, wd):
    return (jax.nn.silu(x @ wg) * (x @ wu)) @ wd


def causal_dwconv(u_ext, w):
    width = w.shape[0]
    L = u_ext.shape[1] - (width - 1)
    return sum(w[k] * u_ext[:, k:k + L] for k in range(width))


def mixer_a(xa, gate_b, gate_c, conv_hist, conv_w):
    u = gate_c * xa
    u_ext = jnp.concatenate([conv_hist.astype(u.dtype), u], axis=1)
    y = gate_b * causal_dwconv(u_ext, conv_w)
    return y, u_ext[:, -(A_CONV - 1):]


def mixer_b(u, v, ln_g, ln_b, w_s, b_s):
    b, L, _ = u.shape
    blk = min(L, B_CHUNK)
    n = L // blk
    v = layernorm(v, ln_g, ln_b)
    tri = jnp.tril(jnp.ones((blk, blk), dtype=bool))
    w = jnp.where(tri[None], w_s[:, :blk, :blk], 0)
    vb = v.reshape(b, n, blk, B_GROUPS, B_GROUP_DIM)
    f = jnp.einsum('gts,bnsgc->bntgc', w, vb) + b_s[:, :blk].T[None, None, :, :, None]
    return u * f.reshape(b, L, B_WIDTH), v


def ab_layer(h, conv_hist, w_in, conv_w, ln_g, ln_b, w_s, b_s, w_out):
    proj = h @ w_in
    xa, gate_b, gate_c, u, v = jnp.split(
        proj, [A_WIDTH, 2 * A_WIDTH, 3 * A_WIDTH, 3 * A_WIDTH + B_WIDTH], axis=-1)
    ya, new_hist = mixer_a(xa, gate_b, gate_c, conv_hist, conv_w)
    yb, v_rows = mixer_b(jax.nn.gelu(u, approximate=False), jax.nn.gelu(v, approximate=False),
                         ln_g, ln_b, w_s, b_s)
    return jnp.concatenate([ya, yb], axis=-1) @ w_out, new_hist, v_rows


def rel_bias_gather(table, q_pos, k_pos):
    rel = jnp.clip(q_pos[:, None] - k_pos[None, :], -C_MAX_REL, C_MAX_REL) + C_MAX_REL
    return table[:, rel]


def attn_core(q, k, v, bias, mask):
    s = jnp.einsum('bqhd,bkhd->bhqk', q.astype(F32), k.astype(F32)) * (C_HEAD_DIM ** -0.5)
    s = s + bias.astype(F32)[None]
    if mask is not None:
        s = jnp.where(mask, s, -1e30)
    p = jax.nn.softmax(s, axis=-1)
    return jnp.einsum('bhqk,bkhd->bqhd', p, v.astype(F32)).astype(q.dtype)


def mixer_c_prompt(q, k, v, table):
    b, L, H, dh = q.shape
    nc = L // CHUNK
    pad = ((0, 0), (C_BAND, 0), (0, 0), (0, 0))
    kp = jnp.pad(k, pad)
    vp = jnp.pad(v, pad)
    kj = jnp.arange(C_KEYS)
    bias = rel_bias_gather(table, jnp.arange(CHUNK) + C_BAND, kj)

    def one_chunk(c):
        start = c * CHUNK
        qc = lax.dynamic_slice_in_dim(q, start, CHUNK, axis=1)
        kc = lax.dynamic_slice_in_dim(kp, start, C_KEYS, axis=1)
        vc = lax.dynamic_slice_in_dim(vp, start, C_KEYS, axis=1)
        mask = (start - C_BAND + kj >= 0)[None, :]
        return attn_core(qc, kc, vc, bias, mask)

    out = lax.map(one_chunk, jnp.arange(nc))
    out = jnp.moveaxis(out, 0, 1).reshape(b, L, H * dh)
    keep = min(C_BAND, L)
    return out, k[:, L - keep:], v[:, L - keep:]


def mixer_c_sample(q, k, v, cache_k, cache_v, table):
    b, T, H, dh = q.shape
    Lc = cache_k.shape[1]
    kk = jnp.concatenate([cache_k.astype(k.dtype), k], axis=1)
    vv = jnp.concatenate([cache_v.astype(v.dtype), v], axis=1)
    bias = rel_bias_gather(table, Lc + jnp.arange(T), jnp.arange(Lc + T))
    out = attn_core(q, kk, vv, bias, None)
    return out.reshape(b, T, H * dh)


def ssd(x, dt, A, Bm, Cm, h0, q_len):
    b, L, H, P = x.shape
    nc = L // q_len
    Hg = H // D_GROUPS
    xs = x.astype(F32).reshape(b, nc, q_len, D_GROUPS, Hg, P)
    dts = dt.astype(F32).reshape(b, nc, q_len, D_GROUPS, Hg)
    Bs = Bm.astype(F32).reshape(b, nc, q_len, D_GROUPS, D_STATE)
    Cs = Cm.astype(F32).reshape(b, nc, q_len, D_GROUPS, D_STATE)
    cum = jnp.cumsum(dts * A.astype(F32).reshape(D_GROUPS, Hg), axis=2)
    tri = jnp.tril(jnp.ones((q_len, q_len), dtype=bool))[:, :, None, None]
    seg = cum[:, :, :, None] - cum[:, :, None, :]
    decay = jnp.exp(jnp.where(tri, seg, -jnp.inf))
    xdt = xs * dts[..., None]
    cb = jnp.einsum('bctgn,bcsgn->bctsg', Cs, Bs)
    y_intra = jnp.einsum('bctsgh,bcsghp->bctghp', cb[..., None] * decay, xdt)
    decay_end = jnp.exp(cum[:, :, -1:] - cum)
    states = jnp.einsum('bcsgn,bcsghp->bcghpn', Bs, xdt * decay_end[..., None])
    chunk_decay = jnp.exp(cum[:, :, -1])

    def step(h, inp):
        dec, st = inp
        return dec[..., None, None] * h + st, h

    h_init = h0.astype(F32).reshape(b, D_GROUPS, Hg, P, D_STATE)
    h_last, h_in = lax.scan(step, h_init,
                            (jnp.moveaxis(chunk_decay, 1, 0), jnp.moveaxis(states, 1, 0)))
    h_in = jnp.moveaxis(h_in, 0, 1)
    y_inter = jnp.einsum('bctgn,bcghpn->bctghp', Cs, h_in) * jnp.exp(cum)[..., None]
    y = (y_intra + y_inter).reshape(b, L, H, P)
    return y, h_last.reshape(b, H, P, D_STATE).astype(h0.dtype)


def mixer_d(z, xbc, dt_raw, conv_hist, conv_w, conv_b, dt_bias, a_log, d_skip, norm_g, h0, q_len):
    b, L, _ = xbc.shape
    xbc_ext = jnp.concatenate([conv_hist.astype(xbc.dtype), xbc], axis=1)
    xbc_c = jax.nn.silu(causal_dwconv(xbc_ext, conv_w) + conv_b)
    new_hist = xbc_ext[:, -(D_CONV - 1):]
    gn = D_GROUPS * D_STATE
    xh = xbc_c[..., :D_INNER].reshape(b, L, D_HEADS, D_HEAD_DIM)
    Bm = xbc_c[..., D_INNER:D_INNER + gn].reshape(b, L, D_GROUPS, D_STATE)
    Cm = xbc_c[..., D_INNER + gn:].reshape(b, L, D_GROUPS, D_STATE)
    dt = jax.nn.softplus(dt_raw.astype(F32) + dt_bias.astype(F32))
    A = -jnp.exp(a_log.astype(F32))
    y, h = ssd(xh, dt, A, Bm, Cm, h0, q_len)
    y = y + d_skip.astype(F32)[:, None] * xh.astype(F32)
    y = y.reshape(b, L, D_INNER) * jax.nn.silu(z.astype(F32))
    yg = y.reshape(b, L, D_GROUPS, D_INNER // D_GROUPS)
    yg = yg * lax.rsqrt(jnp.mean(yg * yg, axis=-1, keepdims=True) + RMS_EPS)
    y = yg.reshape(b, L, D_INNER) * norm_g.astype(F32)
    return y.astype(z.dtype), new_hist, h


def cd_split(h, w_in):
    b, L, _ = h.shape
    proj = h @ w_in
    q, k, v, z, xbc, dt_raw = jnp.split(
        proj, [C_WIDTH, 2 * C_WIDTH, 3 * C_WIDTH, 3 * C_WIDTH + D_INNER,
               3 * C_WIDTH + D_INNER + D_XBC], axis=-1)
    shp = (b, L, C_HEADS, C_HEAD_DIM)
    return q.reshape(shp), k.reshape(shp), v.reshape(shp), z, xbc, dt_raw


def setup_inputs(seed: int = 0) -> dict:
    key = jax.random.key(seed)
    ks = jax.random.split(key, 32)

    def nrm(i, shape, scale):
        return scale * jax.random.normal(ks[i], shape, jnp.float32)

    c_len = min(C_BAND, PAST_LEN)
    u = jax.random.uniform(ks[21], (N_CD, D_HEADS), jnp.float32)
    dt0 = jnp.exp(u * (math.log(DT_MAX) - math.log(DT_MIN)) + math.log(DT_MIN))
    dt_bias = dt0 + jnp.log(-jnp.expm1(-dt0))
    a_log = jnp.log(jax.random.uniform(ks[22], (N_CD, D_HEADS), jnp.float32, 1.0, 16.0))
    return {
        'x_prompt': nrm(0, (BATCH, SEQ, D_MODEL), 1.0),
        'x_sample': nrm(1, (DEC_BATCH, DEC_SEQ, D_MODEL), 1.0),
        'cache_k_c': nrm(2, (N_CD, DEC_BATCH, c_len, C_HEADS, C_HEAD_DIM), 1.0),
        'cache_v_c': nrm(3, (N_CD, DEC_BATCH, c_len, C_HEADS, C_HEAD_DIM), 1.0),
        'state_conv_a': nrm(4, (N_AB, DEC_BATCH, A_CONV - 1, A_WIDTH), 1.0),
        'state_conv_d': nrm(5, (N_CD, DEC_BATCH, D_CONV - 1, D_XBC), 1.0),
        'state_ssm_d': nrm(6, (N_CD, DEC_BATCH, D_HEADS, D_HEAD_DIM, D_STATE), 0.1),
        'norm_mix': 1.0 + nrm(7, (DEPTH, D_MODEL), 0.1),
        'norm_ffn': 1.0 + nrm(8, (DEPTH, D_MODEL), 0.1),
        'norm_final': 1.0 + nrm(9, (D_MODEL,), 0.1),
        'w_in_ab': nrm(10, (N_AB, D_MODEL, AB_IN), D_MODEL ** -0.5),
        'conv_w_a': nrm(11, (N_AB, A_CONV, A_WIDTH), A_CONV ** -0.5),
        'ln_g_b': 1.0 + nrm(12, (N_AB, B_WIDTH), 0.1),
        'ln_b_b': nrm(13, (N_AB, B_WIDTH), 0.02),
        'w_s_b': nrm(14, (N_AB, B_GROUPS, B_CHUNK, B_CHUNK), B_CHUNK ** -0.5),
        'b_s_b': 1.0 + nrm(15, (N_AB, B_GROUPS, B_CHUNK), 0.1),
        'w_out_ab': nrm(16, (N_AB, MIX_WIDTH, D_MODEL), MIX_WIDTH ** -0.5),
        'w_in_cd': nrm(17, (N_CD, D_MODEL, CD_IN), D_MODEL ** -0.5),
        'rel_bias_c': nrm(18, (N_CD, C_HEADS, 2 * C_MAX_REL + 1), 0.5),
        'conv_w_d': nrm(19, (N_CD, D_CONV, D_XBC), D_CONV ** -0.5),
        'conv_b_d': nrm(20, (N_CD, D_XBC), 0.02),
        'dt_bias_d': dt_bias,
        'a_log_d': a_log,
        'd_skip_d': 1.0 + nrm(23, (N_CD, D_HEADS), 0.1),
        'norm_g_d': 1.0 + nrm(24, (N_CD, D_INNER), 0.1),
        'w_out_cd': nrm(25, (N_CD, MIX_WIDTH, D_MODEL), MIX_WIDTH ** -0.5),
        'w_gate': nrm(26, (DEPTH, D_MODEL, FFN_HIDDEN), D_MODEL ** -0.5),
        'w_up': nrm(27, (DEPTH, D_MODEL, FFN_HIDDEN), D_MODEL ** -0.5),
        'w_down': nrm(28, (DEPTH, FFN_HIDDEN, D_MODEL), FFN_HIDDEN ** -0.5),
    }


def reference(x_prompt, x_sample, cache_k_c, cache_v_c, state_conv_a, state_conv_d, state_ssm_d,
              norm_mix, norm_ffn, norm_final, w_in_ab, conv_w_a, ln_g_b, ln_b_b, w_s_b, b_s_b,
              w_out_ab, w_in_cd, rel_bias_c, conv_w_d, conv_b_d, dt_bias_d, a_log_d, d_skip_d,
              norm_g_d, w_out_cd, w_gate, w_up, w_down):
    hp, hs = x_prompt, x_sample
    bp, Lp, _ = hp.shape
    bs, Ls, _ = hs.shape
    conv_a_p, conv_a_s, v_b_s = [], [], []
    k_c_p, v_c_p, k_c_s, v_c_s = [], [], [], []
    conv_d_p, conv_d_s, ssm_d_p, ssm_d_s = [], [], [], []
    for layer in range(DEPTH):
        i = layer // 2
        hp_n = rmsnorm(hp, norm_mix[layer])
        hs_n = rmsnorm(hs, norm_mix[layer])
        if layer % 2 == 0:
            w = (w_in_ab[i], conv_w_a[i], ln_g_b[i], ln_b_b[i], w_s_b[i], b_s_b[i], w_out_ab[i])
            zero_hist = jnp.zeros((bp, A_CONV - 1, A_WIDTH), hp.dtype)
            mp, hist_p, _ = ab_layer(hp_n, zero_hist, *w)
            ms, hist_s, vrows_s = ab_layer(hs_n, state_conv_a[i], *w)
            conv_a_p.append(hist_p)
            conv_a_s.append(hist_s)
            v_b_s.append(vrows_s)
        else:
            dw = (conv_w_d[i], conv_b_d[i], dt_bias_d[i], a_log_d[i], d_skip_d[i], norm_g_d[i])
            q, k, v, z, xbc, dt_raw = cd_split(hp_n, w_in_cd[i])
            ap, kkeep, vkeep = mixer_c_prompt(q, k, v, rel_bias_c[i])
            dp, dhist_p, ssm_p = mixer_d(
                z, xbc, dt_raw, jnp.zeros((bp, D_CONV - 1, D_XBC), hp.dtype), *dw,
                jnp.zeros((bp, D_HEADS, D_HEAD_DIM, D_STATE), hp.dtype), CHUNK)
            mp = jnp.concatenate([ap, dp], axis=-1) @ w_out_cd[i]
            q, k, v, z, xbc, dt_raw = cd_split(hs_n, w_in_cd[i])
            a_s = mixer_c_sample(q, k, v, cache_k_c[i], cache_v_c[i], rel_bias_c[i])
            ds, dhist_s, ssm_s = mixer_d(z, xbc, dt_raw, state_conv_d[i], *dw, state_ssm_d[i], Ls)
            ms = jnp.concatenate([a_s, ds], axis=-1) @ w_out_cd[i]
            k_c_p.append(kkeep)
            v_c_p.append(vkeep)
            k_c_s.append(k)
            v_c_s.append(v)
            conv_d_p.append(dhist_p)
            conv_d_s.append(dhist_s)
            ssm_d_p.append(ssm_p)
            ssm_d_s.append(ssm_s)
        hp = hp + mp
        hs = hs + ms
        hp = hp + swiglu(rmsnorm(hp, norm_ffn[layer]), w_gate[layer], w_up[layer], w_down[layer])
        hs = hs + swiglu(rmsnorm(hs, norm_ffn[layer]), w_gate[layer], w_up[layer], w_down[layer])
    y_prompt = rmsnorm(hp, norm_final)
    y_sample = rmsnorm(hs, norm_final)
    new_conv_a_prompt = jnp.stack(conv_a_p)
    new_conv_a_sample = jnp.stack(conv_a_s)
    new_v_b_sample = jnp.stack(v_b_s)
    new_k_c_prompt = jnp.stack(k_c_p)
    new_v_c_prompt = jnp.stack(v_c_p)
    new_k_c_sample = jnp.stack(k_c_s)
    new_v_c_sample = jnp.stack(v_c_s)
    new_conv_d_prompt = jnp.stack(conv_d_p)
    new_conv_d_sample = jnp.stack(conv_d_s)
    new_ssm_d_prompt = jnp.stack(ssm_d_p)
    new_ssm_d_sample = jnp.stack(ssm_d_s)
    return (y_prompt, y_sample, new_conv_a_prompt, new_conv_a_sample, new_v_b_sample,
            new_k_c_prompt, new_v_c_prompt, new_k_c_sample, new_v_c_sample,
            new_conv_d_prompt, new_conv_d_sample, new_ssm_d_prompt, new_ssm_d_sample)
```

```python
import bisect
from contextlib import ExitStack
import numpy as np
import concourse.bass as bass
import concourse.mybir as mybir
from concourse.bass_utils import run_bass_kernel_spmd

F32 = mybir.dt.float32
BF16 = mybir.dt.bfloat16
AF = mybir.ActivationFunctionType
ALU = mybir.AluOpType

D = 1024
FF = 2816
EPS = 1e-5
NSLOT = 4
SLOT_E = 4096


class _Op:
    __slots__ = ("eng", "fn", "deps", "dma", "semkey", "sig", "need_sig", "idx")


class Prog:
    ENGS = ("pe", "act", "dve", "pool", "sp")

    def __init__(self, nc):
        self.nc = nc
        self.ops = []
        self.last_w = {}
        self.readers = {}
        self.total_keys = set()
        self.sig_idx = {}

    def op(self, eng, fn, reads=(), writes=(), dma=False, semkey=None):
        o = _Op()
        o.eng, o.fn, o.dma, o.semkey = eng, fn, dma, ((semkey, eng) if dma else None)
        o.sig = None
        o.need_sig = dma
        o.idx = len(self.ops)
        psr = [b for b in reads if isinstance(b, tuple) and b[0] == "ps"]
        if psr:
            writes = list(writes) + [b for b in psr if b not in writes]
        deps = {}
        for b in reads:
            w = self.last_w.get(b)
            if w is not None:
                deps[w.idx] = w
        for b in writes:
            w = self.last_w.get(b)
            if w is not None:
                deps[w.idx] = w
            for r in self.readers.get(b, ()):
                if r.eng != eng or r.dma:
                    deps[r.idx] = r
        pruned = {}
        for di in sorted(deps, reverse=True):
            d = deps[di]
            if d.eng == eng and not d.dma and not dma and eng == "pe":
                continue
            if not d.dma:
                lst = self.sig_idx.setdefault(d.eng, [])
                j = bisect.bisect_left(lst, d.idx)
                if j < len(lst):
                    d = self.ops[lst[j]]
                else:
                    lst.append(d.idx)
                    d.need_sig = True
            else:
                d.need_sig = True
            pruned[d.idx] = d
        o.deps = list(pruned.values())
        for b in reads:
            self.readers.setdefault(b, []).append(o)
        for b in writes:
            self.last_w[b] = o
            self.readers[b] = []
        self.ops.append(o)
        return o

    def emit(self, stack, final_wait_eng="sp"):
        nc = self.nc
        eng_sem = {e: stack.enter_context(nc.semaphore("s_" + e)) for e in self.ENGS}
        dma_sem, cnt = {}, {}
        for o in self.ops:
            if o.dma:
                k = o.semkey
                if k not in dma_sem:
                    dma_sem[k] = stack.enter_context(nc.semaphore("d_%d" % len(dma_sem)))
                cnt[k] = cnt.get(k, 0) + 16
                o.sig = (dma_sem[k], cnt[k])
            elif o.need_sig:
                k = ("e", o.eng)
                cnt[k] = cnt.get(k, 0) + 1
                o.sig = (eng_sem[o.eng], cnt[k])
        for o in self.ops:
            if o.dma and o.semkey[0] in self.total_keys:
                o.sig = (dma_sem[o.semkey], cnt[o.semkey])
        self.n_sems = len(dma_sem) + len(eng_sem)
        print('sems', self.n_sems, {k: v for k, v in cnt.items() if isinstance(k, tuple) and k[0] == 'e'}, 'maxdma', max(v for k, v in cnt.items()))
        final = [(dma_sem[k], cnt[k]) for k in dma_sem]
        per_eng = {e: [o for o in self.ops if o.eng == e] for e in self.ENGS}
        block = stack.enter_context(nc.Block())

        def run(e, h):
            waited = {}
            for o in per_eng[e]:
                need = {}
                for d in o.deps:
                    s, v = d.sig
                    if need.get(id(s), (None, 0))[1] < v:
                        need[id(s)] = (s, v)
                for s, v in need.values():
                    if waited.get(id(s), 0) >= v:
                        continue
                    waited[id(s)] = v
                    h.wait_ge(s, v)
                ins = o.fn(h)
                if o.sig is not None:
                    ins.then_inc(o.sig[0], 16 if o.dma else 1)
            if e == final_wait_eng:
                for s, v in final:
                    if waited.get(id(s), 0) < v:
                        h.wait_ge(s, v)

        block.tensor(lambda h: run("pe", h))
        block.scalar(lambda h: run("act", h))
        block.vector(lambda h: run("dve", h))
        block.gpsimd(lambda h: run("pool", h))
        block.sync(lambda h: run("sp", h))


def build(NPS, SEQ):
    NTILE = SEQ // 512
    nc = bass.Bass("TRN2", target_bir_lowering=False)

    def din(name, shape):
        return nc.dram_tensor(name, list(shape), F32, kind="ExternalInput").ap()

    def dout(name, shape):
        return nc.dram_tensor(name, list(shape), F32, kind="ExternalOutput").ap()

    xp = din("xp", [NPS * SEQ, D]); xs = din("xs", [64, D])
    ck = din("ck", [512, 512]); cv = din("cv", [512, 512])
    sca = din("sca", [2, 512]); scd = din("scd", [3, 1024]); ssm = din("ssm", [512, 128])
    norm_mix = din("norm_mix", [2, D]); norm_ffn = din("norm_ffn", [2, D]); norm_final = din("norm_final", [1, D])
    w_in_ab = din("w_in_ab", [D, 2560]); conv_w_a = din("conv_w_a", [3, 512])
    ln_g_b = din("ln_g_b", [512]); ln_b_b = din("ln_b_b", [512])
    w_s_b = din("w_s_b", [4, 128, 128]); b_s_b = din("b_s_b", [512])
    w_out_ab = din("w_out_ab", [D, D]); w_in_cd = din("w_in_cd", [D, 3080])
    rel_bias = din("rel_bias_c", [8, 257]); conv_w_d = din("conv_w_d", [4, 1024]); conv_b_d = din("conv_b_d", [1024])
    dt_bias = din("dt_bias_d", [8]); a_log = din("a_log_d", [8]); d_skip = din("d_skip_d", [8])
    norm_g_d = din("norm_g_d", [512]); w_out_cd = din("w_out_cd", [D, D])
    w_gate = din("w_gate", [2, D, FF]); w_up = din("w_up", [2, D, FF]); w_down = din("w_down", [2, FF, D])

    yp = dout("yp", [NPS * SEQ, D]); ys = dout("ys", [64, D])
    cap = dout("cap", [NPS, 2, 512]); cas = dout("cas", [1, 2, 512]); vbs = dout("vbs", [64, 512])
    kcp = dout("kcp", [NPS * 512, 512]); vcp = dout("vcp", [NPS * 512, 512])
    kcs = dout("kcs", [64, 512]); vcs = dout("vcs", [64, 512])
    cdp = dout("cdp", [NPS, 3, 1024]); cds = dout("cds", [1, 3, 1024])
    ssp = dout("ssp", [NPS * 512, 128]); sss = dout("sss", [512, 128])

    gran = []

    def add_gran(w2d, K, c0, cols, grp):
        gran.append((w2d[:, c0:c0 + cols], K // 128, cols, grp))
        return len(gran) - 1

    G = {}
    for l in range(2):
        if l == 0:
            G["in_ab"] = [add_gran(w_in_ab, D, i * 512, 512, "in_ab") for i in range(5)]
            G["out_ab"] = [add_gran(w_out_ab, D, i * 512, 512, "out_ab") for i in range(2)]
        else:
            G["in_cd"] = [add_gran(w_in_cd, D, i * 512, 512, "in_cd") for i in range(6)]
            G["out_cd"] = [add_gran(w_out_cd, D, i * 512, 512, "out_cd") for i in range(2)]
        G["gate%d" % l] = [add_gran(w_gate[l], D, i * 512, min(512, FF - i * 512), "gate%d" % l) for i in range(6)]
        G["up%d" % l] = [add_gran(w_up[l], D, i * 512, min(512, FF - i * 512), "up%d" % l) for i in range(6)]
        G["down%d" % l] = [add_gran(w_down[l], FF, i * 128, 128, "down%d" % l) for i in range(8)]
    scr = nc.dram_tensor("wscr", [len(gran), 128, SLOT_E], BF16, kind="Internal").ap()
    tpad = nc.dram_tensor("tpad", [8, 768], F32, kind="Internal").ap()

    st = ExitStack()
    with st:
        P = Prog(nc)
        P.total_keys.add("const")
        P.total_keys.add("haloA")
        P.total_keys.add("haloD")
        for g in gran:
            P.total_keys.add(("scr", g[3]))

        def T(name, shape, dt=F32):
            return st.enter_context(nc.sbuf_tensor(name, list(shape), dt))

        hT = T("hT", [128, 8, 512]); hnT = T("hnT", [128, 8, 512], BF16); mixT = T("mixT", [128, 8, 512], BF16)
        slots = [T("slot%d" % i, [128, SLOT_E], BF16) for i in range(NSLOT)]
        xin = [T("xin%d" % i, [128, D]) for i in range(2)]
        yst = [T("yst%d" % i, [128, D]) for i in range(2)]
        sq = [T("sq%d" % i, [128, 512], BF16) for i in range(2)]
        rstd = T("rstd", [128, 512])
        identb = T("identb", [128, 128], BF16); identf = T("identf", [128, 128])
        onesD = T("onesD", [128, 128], BF16); onesf = T("onesf", [128, 128])
        gcol = T("gcol", [128, 5, 8])
        cwa = T("cwa", [128, 4, 3]); cwd = T("cwd", [128, 8, 4]); cbd = T("cbd", [128, 8])
        lngB = T("lngB", [128, 512]); lnbB = T("lnbB", [128, 512]); ngdB = T("ngdB", [128, 512])
        wsT = T("wsT", [128, 4, 128], BF16); bsB = T("bsB", [128, 512])
        EB = T("EB", [128, 8, 640], BF16)
        Ublk = T("Ublk", [128, 128]); SLblk = T("SLblk", [128, 128]); Uc = T("Uc", [128, 64]); SLc = T("SLc", [128, 64])
        dtbB = T("dtbB", [128, 8]); AB = T("AB", [128, 8]); dskB = T("dskB", [128, 8])
        wdt = T("wdt", [128, 8, 128], BF16)
        haloA = T("haloA", [128, 4, 2]); haloD = T("haloD", [128, 8, 3], BF16)
        kT = [T("kT%d" % i, [128, 4, 512], BF16) for i in range(2)]
        Vx = [T("Vx%d" % i, [128, 4, 4, 192], BF16) for i in range(2)]
        H = T("H", [128, 512]); Hbf = [T("Hbf%d" % i, [128, 512], BF16) for i in range(2)]
        small = T("small", [128, 64])
        oca = T("oca", [128, 4, 2]); ocd = T("ocd", [128, 8, 3]); ost = T("ost", [128, 4, 128])
        kvst = [T("kvst%d" % i, [128, 512]) for i in range(2)]
        RW = 16128
        R = T("R", [128, RW])
        R16 = R.bitcast(BF16)
        ps = [st.enter_context(nc.psum_tensor("ps%d" % i, [128, 512], F32)) for i in range(8)]
        psb = [p.bitcast(BF16) for p in ps]

        class RB:
            def __init__(self, off_b, dt, shape):
                self.esz = 2 if dt == BF16 else 4
                self.off = off_b
                n = int(np.prod(shape))
                base = R16 if dt == BF16 else R
                e0 = off_b // self.esz
                v = base[:, e0:e0 + n]
                if len(shape) == 2:
                    v = v.rearrange("p (a b) -> p a b", b=shape[1])
                elif len(shape) == 3:
                    v = v.rearrange("p (a b c) -> p a b c", b=shape[1], c=shape[2])
                self.ap = v
                self.shape = shape
                self.n = n
                assert off_b + n * self.esz <= RW * 4, (off_b, n)

            def k(self, lo=0, hi=None):
                hi = self.n if hi is None else hi
                b0 = (self.off + lo * self.esz) // 1024
                b1 = (self.off + hi * self.esz - 1) // 1024
                return [("R", u) for u in range(b0, b1 + 1)]

            def kr(self, a, lo=0, hi=None):
                w = int(np.prod(self.shape[1:]))
                hi = w if hi is None else hi
                return self.k(a * w + lo, a * w + hi)

        KB = 1024
        uu = RB(0, F32, [4, 516]); gbuf = RB(9 * KB, BF16, [4, 512]); gu = RB(13 * KB, BF16, [4, 512])
        vtok = RB(17 * KB, F32, [4, 512]); vln = RB(25 * KB, BF16, [4, 512]); ftmp = RB(29 * KB, F32, [2, 512])
        cacc = RB(33 * KB, F32, [2, 512])
        actT = RB(0, BF16, [22, 512]); sgb = RB(22 * KB, BF16, [2, 512])
        qT = RB(0, BF16, [4, 512]); PT = RB(4 * KB, BF16, [2, 2560]); zs = RB(14 * KB, BF16, [4, 512])
        xbc = RB(18 * KB, BF16, [8, 516]); xc = RB(27 * KB, BF16, [8, 512]); tmpE = RB(35 * KB, F32, [2, 512])
        xtok = RB(39 * KB, BF16, [512]); btok = RB(40 * KB, BF16, [256]); Lhi = RB(41 * KB, BF16, [1024])
        decT = RB(43 * KB, F32, [512]); cbm = RB(45 * KB, F32, [2, 128]); MTb = RB(46 * KB, BF16, [8, 128])
        xdt = RB(48 * KB, BF16, [512]); xddz = RB(49 * KB, BF16, [2, 512]); t1 = RB(51 * KB, F32, [512])
        t2 = RB(53 * KB, F32, [512]); yb = RB(55 * KB, F32, [512]); ynb = RB(57 * KB, BF16, [512])
        rden = RB(58 * KB, F32, [512]); Llo = RB(61 * KB, BF16, [1024]); cacd = RB(61 * KB, F32, [512])
        ckf = RB(0, F32, [4, 512]); ckb = RB(8 * KB, BF16, [4, 512]); cvf = RB(12 * KB, F32, [4, 512])
        ssf = RB(20 * KB, F32, [4, 128]); scdf = RB(60 * KB, F32, [8, 3])

        bank_ctr = [0]

        ring = list(range(8))

        def bank():
            b = ring[bank_ctr[0] % len(ring)]
            bank_ctr[0] += 1
            return b

        def MM(out, lhsT, rhs, r, w, start=True, stop=True):
            P.op("pe", lambda e: e.matmul(out, lhsT=lhsT, rhs=rhs, start=start, stop=stop), reads=r, writes=w)

        def TR(out, in_, ident, r, w):
            P.op("pe", lambda e: e.transpose(out, in_, ident), reads=r, writes=w)

        def ACT(out, in_, func, r, w, **kw):
            P.op("act", lambda e: e.activation(out=out, in_=in_, func=func, **kw), reads=r, writes=w)

        def TT(eng, out, in0, in1, op, r, w):
            P.op(eng, lambda e: e.tensor_tensor(out=out, in0=in0, in1=in1, op=op), reads=r, writes=w)

        def TS(eng, out, in0, s1, s2, op0, op1, r, w):
            if s2 is None:
                P.op(eng, lambda e: e.tensor_scalar(out=out, in0=in0, scalar1=s1, scalar2=None, op0=op0), reads=r, writes=w)
            else:
                P.op(eng, lambda e: e.tensor_scalar(out=out, in0=in0, scalar1=s1, scalar2=s2, op0=op0, op1=op1), reads=r, writes=w)

        def STT(out, in0, sc, in1, op0, op1, r, w):
            P.op("dve", lambda e: e.scalar_tensor_tensor(out=out, in0=in0, scalar=sc, in1=in1, op0=op0, op1=op1), reads=r, writes=w)

        def CP(eng, out, in_, r, w):
            if eng == "act":
                P.op("act", lambda e: e.copy(out=out, in_=in_), reads=r, writes=w)
            else:
                P.op(eng, lambda e: e.tensor_copy(out=out, in_=in_), reads=r, writes=w)

        def RCP(out, in_, r, w):
            P.op("dve", lambda e: e.reciprocal(out=out, in_=in_), reads=r, writes=w)

        def MS(eng, ap, val, r, w):
            P.op(eng, lambda e: e.memset(ap, val), reads=r, writes=w)

        def DMA(eng, out, in_, r, w, semkey, slow=False):
            def f(e):
                if slow:
                    with nc.allow_non_contiguous_dma(reason="tiny strided"):
                        return e.dma_start(out=out, in_=in_)
                return e.dma_start(out=out, in_=in_)
            P.op(eng, f, reads=r, writes=w, dma=True, semkey=semkey)

        def ASEL(out, pattern, cmp, fill, base, cm, r, w):
            P.op("pool", lambda e: e.affine_select(out=out, in_=out, pattern=pattern, compare_op=cmp, fill=fill, base=base, channel_multiplier=cm), reads=r, writes=w)

        CK = "const"
        MS("pool", wdt[:], 0.0, [], ["wdt"])
        DMA("pool", wdt[:, :, 0:8], w_in_cd[:, 3072:3080].rearrange("(k p) c -> p k c", p=128), ["wdt"], ["wdt"], "const", slow=True)
        MS("pool", identf[:], 0.0, [], ["identf"])
        ASEL(identf[:], [[-1, 128]], ALU.not_equal, 1.0, 0, 1, ["identf"], ["identf"])
        CP("pool", identb[:], identf[:], ["identf"], ["identb"])
        MS("pool", onesD[:], 1.0 / D, [], ["onesD"])
        MS("pool", onesf[:], 1.0, [], ["onesf"])
        MS("pool", Ublk[:], 1.0, [], ["Ublk"])
        ASEL(Ublk[:], [[1, 128]], ALU.is_ge, 0.0, 0, -1, ["Ublk"], ["Ublk"])
        MS("pool", Ublk[0:64, 64:128], 0.0, ["Ublk"], ["Ublk"])
        MS("pool", SLblk[:], 1.0, [], ["SLblk"])
        ASEL(SLblk[:], [[-1, 128]], ALU.is_gt, 0.0, 0, 1, ["SLblk"], ["SLblk"])
        MS("pool", SLblk[64:128, 0:64], 0.0, ["SLblk"], ["SLblk"])
        CP("pool", Uc[0:64, :], Ublk[0:64, 0:64], ["Ublk"], ["Uc"])
        CP("pool", Uc[64:128, :], Ublk[64:128, 64:128], ["Ublk", "Uc"], ["Uc"])
        CP("pool", SLc[0:64, :], SLblk[0:64, 0:64], ["SLblk"], ["SLc"])
        CP("pool", SLc[64:128, :], SLblk[64:128, 64:128], ["SLblk", "SLc"], ["SLc"])
        Ucb = T("Ucb", [128, 64], BF16); Ublkb = T("Ublkb", [128, 128], BF16); onesb = T("onesb", [128, 128], BF16)
        dtb = T("dtb", [128, 2, 32], BF16)
        SLblkb = T("SLblkb", [128, 128], BF16); onesAB = T("onesAB", [128, 2, 128], BF16)
        CP("pool", SLblkb[:], SLblk[:], ["SLblk"], ["SLblkb"])
        MS("pool", onesAB[:], 0.0, [], ["onesAB"])
        MS("pool", onesAB[0:64, 0, :], 1.0, ["onesAB"], ["onesAB"])
        MS("pool", onesAB[64:128, 1, :], 1.0, ["onesAB"], ["onesAB"])
        CP("pool", Ucb[:], Uc[:], ["Uc"], ["Ucb"])
        CP("pool", Ublkb[:], Ublk[:], ["Ublk"], ["Ublkb"])
        MS("pool", onesb[:], 1.0, [], ["onesb"])
        for i in range(2):
            MS("pool", Vx[i][:], 1.0, [], [("Vx", i, b) for b in range(4)])
        for n, src in enumerate([norm_mix[0:1], norm_ffn[0:1], norm_mix[1:2], norm_ffn[1:2], norm_final]):
            DMA("sp", gcol[:, n, :], src[0].rearrange("(k p) -> p k", p=128), [], [("gcol", n)], CK, slow=True)
        GC = [("gcol", n) for n in range(5)]
        for c in range(4):
            DMA("sp", cwa[:, c, :], conv_w_a[:, c * 128:(c + 1) * 128].rearrange("k p -> p k"), [], [("cwa", c)], CK, slow=True)
        for c in range(8):
            DMA("sp", cwd[:, c, :], conv_w_d[:, c * 128:(c + 1) * 128].rearrange("k p -> p k"), [], [("cwd", c)], CK, slow=True)
        DMA("sp", cbd[:], conv_b_d.rearrange("(k p) -> p k", p=128), [], ["cbd"], CK, slow=True)
        CW = [("cwa", c) for c in range(4)] + [("cwd", c) for c in range(8)] + ["cbd"]
        DMA("sp", lngB[:], ln_g_b.partition_broadcast(128), [], ["lngB"], CK)
        DMA("sp", lnbB[:], ln_b_b.partition_broadcast(128), [], ["lnbB"], CK)
        DMA("sp", ngdB[:], norm_g_d.partition_broadcast(128), [], ["ngdB"], CK)
        DMA("sp", bsB[:], b_s_b.partition_broadcast(128), [], ["bsB"], CK)
        DMA("sp", dtbB[:], dt_bias.partition_broadcast(128), [], ["dtbB"], CK)
        DMA("sp", AB[:], a_log.partition_broadcast(128), [], ["AB"], CK)
        DMA("sp", dskB[:], d_skip.partition_broadcast(128), [], ["dskB"], CK)
        ACT(AB[:], AB[:], AF.Exp, ["AB"], ["AB"])
        P.op("act", lambda e: e.mul(out=AB[:], in_=AB[:], mul=-1.0), reads=["AB"], writes=["AB"])
        wsfb = RB(24 * KB, F32, [4, 128])
        wsf = wsfb.ap
        DMA("sp", wsf, w_s_b.rearrange("g t s -> t g s"), [], ["wsf"] + wsfb.k(), CK)
        for g in range(4):
            ASEL(wsf[:, g, :], [[-1, 128]], ALU.is_ge, 0.0, 0, 1, ["wsf"], ["wsf"])
        b0 = bank()
        for g in range(4):
            TR(ps[b0][:, g * 128:(g + 1) * 128], wsf[:, g, :], identf[:], ["wsf", "identf"] + wsfb.k(), [("ps", b0)] + wsfb.k())
        CP("act", wsT[:], ps[b0][:, :].rearrange("p (g t) -> p g t", t=128), [("ps", b0)], ["wsT"])
        tbb = RB(28 * KB, F32, [768])
        tb = tbb.ap[0:8, :]
        DMA("sp", tb[:, 0:257], rel_bias, [], ["tb"] + tbb.k(), CK)
        CP("act", tb[:, 257:768], tb[:, 256:257].broadcast_to([8, 511]), ["tb"], ["tb2"])
        DMA("sp", tpad, tb, ["tb", "tb2"], ["tpadA", "tpadB"] + tbb.k(), "tpad")
        bm = RB(0, F32, [8, 640])
        MS("pool", bm.ap, -30000.0, [], bm.k())
        for p in range(128):
            a, j = p // 64, p % 64
            DMA("sp" if p % 2 == 0 else "pool", bm.ap[p:p + 1, :, 64 * a:64 * a + 576], tpad[:, 128 - j:128 - j + 576].unsqueeze(0),
                ["tpadA", "tpadB"] + bm.k(), [("bmrow", p)], "bm")
        P.total_keys.add("bm")
        CP("act", EB[:], bm.ap, [("bmrow", p) for p in range(128)] + bm.k(), ["EB"] + bm.k())

        for gi, (src, nkc, cols, grp) in enumerate(gran):
            dst = scr[gi][:, 0:nkc * cols].rearrange("p (k c) -> p k c", c=cols)
            DMA("pool", dst, src.rearrange("(k p) c -> p k c", p=128), [], [("scr", gi)], ("scr", grp))

        slot_ctr = [0]

        def wload(gi):
            s = slot_ctr[0] % NSLOT
            slot_ctr[0] += 1
            src, nkc, cols, grp = gran[gi]
            n = nkc * cols
            DMA("sp", slots[s][:, 0:n], scr[gi][:, 0:n], [("scr", gi)], [("slot", s)], ("slot", s))
            return slots[s][:, 0:n].rearrange("p (k c) -> p k c", c=cols), ("slot", s)

        def mixk(kc):
            return [("mixT", kc, 0), ("mixT", kc, 1)]

        def hk(kc, nb):
            return [("hT", kc, b) for b in range(nb)]

        def norm(nidx, nt, nb, final_out=None):
            pb = bank()
            for kc in range(8):
                s = sq[kc % 2]
                ACT(s[:, :nt], hT[:, kc, :nt], AF.Square, hk(kc, nb), [("sq", kc % 2)])
                MM(ps[pb][:, :nt], onesD[:], s[:, :nt], ["onesD", ("sq", kc % 2)], [("ps", pb)], start=(kc == 0), stop=(kc == 7))
            ACT(rstd[:, :nt], ps[pb][:, :nt], AF.Sqrt, [("ps", pb)], ["rstd"], bias=EPS, scale=1.0)
            RCP(rstd[:, :nt], rstd[:, :nt], ["rstd"], ["rstd"])
            for kc in range(8):
                if final_out is None:
                    STT(hnT[:, kc, :nt], hT[:, kc, :nt], gcol[:, nidx, kc:kc + 1], rstd[:, :nt], ALU.mult, ALU.mult,
                        hk(kc, nb) + ["rstd"] + GC, [("hnT", kc)])
                else:
                    STT(hT[:, kc, :nt], hT[:, kc, :nt], gcol[:, nidx, kc:kc + 1], rstd[:, :nt], ALU.mult, ALU.mult,
                        hk(kc, nb) + ["rstd"] + GC, hk(kc, nb))

        HN = [("hnT", kc) for kc in range(8)]

        def proj_fm(slot, sk, mc, nt, rhsbuf=None, rk=None, nk=8):
            pb = bank()
            for kc in range(nk):
                if rhsbuf is None:
                    rhs, rkey = hnT[:, kc, :nt], [("hnT", kc)]
                else:
                    rhs, rkey = rhsbuf(kc), rk(kc)
                MM(ps[pb][:, :nt], slot[:, kc, mc * 128:(mc + 1) * 128], rhs, [sk] + rkey, [("ps", pb)], start=(kc == 0), stop=(kc == nk - 1))
            return pb

        def proj_tm(slot, sk, b, cols=512):
            pb = bank()
            for kc in range(8):
                MM(ps[pb][:, :cols], hnT[:, kc, b * 128:(b + 1) * 128], slot[:, kc, 0:cols], [sk, ("hnT", kc)], [("ps", pb)], start=(kc == 0), stop=(kc == 7))
            return pb

        def out_proj(gids, src_ap, src_keys, nt, nb, nk):
            for gi_i, gi in enumerate(gids):
                slot, sk = wload(gi)
                ncm = gran[gi][2] // 128
                for m in range(ncm):
                    mc = gi_i * ncm + m
                    pb = proj_fm(slot, sk, m, nt, rhsbuf=lambda kc: src_ap(kc, nt), rk=src_keys, nk=nk)
                    TT("dve", hT[:, mc, :nt], hT[:, mc, :nt], ps[pb][:, :nt], ALU.add, hk(mc, nb) + [("ps", pb)], hk(mc, nb))

        def ffn(l, nt, nb):
            norm(1 + 2 * l, nt, nb)
            for i in range(6):
                sg, skg = wload(G["gate%d" % l][i])
                su, sku = wload(G["up%d" % l][i])
                for m in range(gran[G["gate%d" % l][i]][2] // 128):
                    j = i * 4 + m
                    pg = proj_fm(sg, skg, m, nt)
                    pu = proj_fm(su, sku, m, nt)
                    ACT(sgb.ap[:, j % 2, :nt], ps[pg][:, :nt], AF.Silu, [("ps", pg)], sgb.kr(j % 2))
                    TT("dve", actT.ap[:, j, :nt], sgb.ap[:, j % 2, :nt], ps[pu][:, :nt], ALU.mult, sgb.kr(j % 2) + [("ps", pu)], actT.kr(j))
            out_proj(G["down%d" % l], lambda kc, nt: actT.ap[:, kc, :nt], lambda kc: actT.kr(kc), nt, nb, 22)

        small_ctr = [0]

        def sm(n):
            o = small_ctr[0] % (64 // 8) * 8
            small_ctr[0] += 1
            return small[:, o:o + n], ("small", o)

        def layer0(ti, nt, nb, first, last, sample, seq):
            norm(0, nt, nb)
            if first and not sample:
                MS("dve", uu.ap[:, :, 0:2], 0.0, [], uu.k())
            elif first and sample:
                for c in range(4):
                    DMA("sp", uu.ap[:, c, 0:2], sca[:, c * 128:(c + 1) * 128].rearrange("k p -> p k"), [], uu.kr(c, 0, 2), "haloA", slow=True)
            else:
                CP("dve", uu.ap[:, :, 0:2], haloA[:], ["haloA"], uu.k())
            slot, sk = wload(G["in_ab"][4])
            for b in range(nb):
                pb = proj_tm(slot, sk, b)
                vs, vsk = sm(4)
                MS("dve", vs, 0.0, [], [vsk])
                ACT(vtok.ap[:, b, :], ps[pb][:, :], AF.Gelu, [("ps", pb), vsk], vtok.kr(b) + [vsk], accum_out=vs[:, 0:1])
                TS("dve", vs[:, 1:2], vs[:, 0:1], -1.0 / 512, None, ALU.mult, None, [vsk], [vsk])
                ACT(ftmp.ap[:, b % 2, :], vtok.ap[:, b, :], AF.Square, vtok.kr(b) + [vsk], ftmp.kr(b % 2) + [vsk], bias=vs[:, 1:2], scale=1.0, accum_out=vs[:, 2:3])
                ACT(vs[:, 3:4], vs[:, 2:3], AF.Sqrt, [vsk], [vsk], bias=EPS, scale=1.0 / 512)
                RCP(vs[:, 3:4], vs[:, 3:4], [vsk], [vsk])
                TS("dve", vtok.ap[:, b, :], vtok.ap[:, b, :], vs[:, 1:2], vs[:, 3:4], ALU.add, ALU.mult, vtok.kr(b) + [vsk], vtok.kr(b))
                TT("dve", vtok.ap[:, b, :], vtok.ap[:, b, :], lngB[:], ALU.mult, vtok.kr(b) + ["lngB"], vtok.kr(b))
                TT("dve", vtok.ap[:, b, :], vtok.ap[:, b, :], lnbB[:], ALU.add, vtok.kr(b) + ["lnbB"], vtok.kr(b))
                CP("act", vln.ap[:, b, :], vtok.ap[:, b, :], vtok.kr(b), vln.kr(b))
                if sample:
                    DMA("act", vbs, vtok.ap[0:64, 0, :], vtok.kr(0), [], "o_vbs")
            for gi_i, nm in enumerate(["xa", "gb", "gc", "u"]):
                slot, sk = wload(G["in_ab"][gi_i])
                for mc in range(4):
                    pb = proj_fm(slot, sk, mc, nt)
                    if nm == "xa":
                        CP("act", uu.ap[:, mc, 2:2 + nt], ps[pb][:, :nt], [("ps", pb)], uu.kr(mc))
                    elif nm == "gb":
                        CP("act", gbuf.ap[:, mc, :nt], ps[pb][:, :nt], [("ps", pb)], gbuf.kr(mc))
                    elif nm == "gc":
                        TT("dve", uu.ap[:, mc, 2:2 + nt], uu.ap[:, mc, 2:2 + nt], ps[pb][:, :nt], ALU.mult, uu.kr(mc) + [("ps", pb)], uu.kr(mc))
                    else:
                        ACT(gu.ap[:, mc, :nt], ps[pb][:, :nt], AF.Gelu, [("ps", pb)], gu.kr(mc))
            nv = 64 if sample else nt
            CP("dve", haloA[:], uu.ap[:, :, nv:nv + 2], uu.k(), ["haloA"])
            for c in range(4):
                a = cacc.ap[:, c % 2, :nt]
                ak = cacc.kr(c % 2)
                TS("dve", a, uu.ap[:, c, 0:nt], cwa[:, c, 0:1], None, ALU.mult, None, uu.kr(c) + CW, ak)
                STT(a, uu.ap[:, c, 1:1 + nt], cwa[:, c, 1:2], a, ALU.mult, ALU.add, uu.kr(c) + CW + ak, ak)
                STT(a, uu.ap[:, c, 2:2 + nt], cwa[:, c, 2:3], a, ALU.mult, ALU.add, uu.kr(c) + CW + ak, ak)
                TT("dve", mixT[:, c, :nt], a, gbuf.ap[:, c, :nt], ALU.mult, ak + gbuf.kr(c), mixk(c))
            for g in range(4):
                pb = bank()
                for b in range(nb):
                    MM(ps[pb][:, b * 128:(b + 1) * 128], vln.ap[:, b, g * 128:(g + 1) * 128], wsT[:, g, :], vln.kr(b) + ["wsT"], [("ps", pb)])
                f = ftmp.ap[:, g % 2, :nt]
                TT("dve", f.rearrange("p (b t) -> p b t", t=128), ps[pb][:, :nt].rearrange("p (b t) -> p b t", t=128),
                   bsB[:, g * 128:(g + 1) * 128].unsqueeze(1).broadcast_to([128, nb, 128]), ALU.add, [("ps", pb), "bsB"], ftmp.kr(g % 2))
                TT("dve", mixT[:, 4 + g, :nt], f, gu.ap[:, g, :nt], ALU.mult, ftmp.kr(g % 2) + gu.kr(g), mixk(4 + g))
            if last or sample:
                CP("dve", oca[:], haloA[:], ["haloA"], ["oca"])
                dst = cas[0] if sample else cap[seq]
                for c in range(4):
                    DMA("sp", dst[:, c * 128:(c + 1) * 128].rearrange("k p -> p k"), oca[:, c, :], ["oca"], [], "o_oca", slow=True)
            out_proj(G["out_ab"], lambda kc, nt: mixT[:, kc, :nt], mixk, nt, nb, 8)

        def layer1(ti, nt, nb, first, last, sample, seq):
            import os
            cur, prv = ti % 2, (ti + 1) % 2
            norm(2, nt, nb)
            has_hist = (not first) or sample
            slot, sk = wload(G["in_cd"][0])
            for mc in range(4):
                pb = proj_fm(slot, sk, mc, nt)
                ACT(qT.ap[:, mc, :nt], ps[pb][:, :nt], AF.Copy, [("ps", pb)], qT.kr(mc), scale=0.125)
            slot, sk = wload(G["in_cd"][1])
            for mc in range(4):
                pb = proj_fm(slot, sk, mc, nt)
                CP("act", kT[cur][:, mc, :nt], ps[pb][:, :nt], [("ps", pb)], [("kT", cur, mc)])
            if last or sample:
                for b in range(nb):
                    pb = proj_tm(slot, sk, b)
                    kv = kvst[b % 2]
                    CP("act", kv[:], ps[pb][:, :], [("ps", pb)], [("kvst", b % 2)])
                    if sample:
                        DMA("act", kcs, kv[0:64, :], [("kvst", b % 2)], [], ("o_kv", b % 2))
                    else:
                        DMA("act", kcp[seq * 512 + b * 128: seq * 512 + (b + 1) * 128, :], kv[:], [("kvst", b % 2)], [], ("o_kv", b % 2))
            slot, sk = wload(G["in_cd"][2])
            for b in range(nb):
                pb = proj_tm(slot, sk, b)
                CP("dve", bass.AP(Vx[cur], b * 768, [[4 * 768, 128], [192, 4], [128, 2], [1, 64]]),
                   ps[pb][:, :].rearrange("p (r two c) -> p r two c", two=2, c=64), [("ps", pb)], [("Vx", cur, b)])
                if last or sample:
                    kv = kvst[b % 2]
                    CP("act", kv[:], ps[pb][:, :], [("ps", pb)], [("kvst", b % 2)])
                    if sample:
                        DMA("act", vcs, kv[0:64, :], [("kvst", b % 2)], [], ("o_kv", b % 2))
                    else:
                        DMA("act", vcp[seq * 512 + b * 128: seq * 512 + (b + 1) * 128, :], kv[:], [("kvst", b % 2)], [], ("o_kv", b % 2))
            slot, sk = wload(G["in_cd"][3])
            for b in range(nb):
                pb = proj_tm(slot, sk, b)
                ACT(zs.ap[:, b, :], ps[pb][:, :], AF.Silu, [("ps", pb)], zs.kr(b))
            if first and not sample:
                MS("dve", xbc.ap[:, :, 0:3], 0.0, [], xbc.k())
            elif first and sample:
                for c in range(8):
                    DMA("sp", scdf.ap[:, c, :], scd[:, c * 128:(c + 1) * 128].rearrange("k p -> p k"), [], [("scdf", c)], "haloD", slow=True)
                CP("dve", xbc.ap[:, :, 0:3], scdf.ap, [("scdf", c) for c in range(8)], xbc.k())
            else:
                CP("dve", xbc.ap[:, :, 0:3], haloD[:], ["haloD"], xbc.k())
            for gg in range(2):
                slot, sk = wload(G["in_cd"][4 + gg])
                for mc in range(4):
                    pb = proj_fm(slot, sk, mc, nt)
                    CP("act", xbc.ap[:, gg * 4 + mc, 3:3 + nt], ps[pb][:, :nt], [("ps", pb)], xbc.kr(gg * 4 + mc))
            nv = 64 if sample else nt
            CP("dve", haloD[:], xbc.ap[:, :, nv:nv + 3], xbc.k(), ["haloD"])
            if last or sample:
                CP("dve", ocd[:], xbc.ap[:, :, nv:nv + 3], xbc.k(), ["ocd"])
                dst = cds[0] if sample else cdp[seq]
                for c in range(8):
                    DMA("sp", dst[:, c * 128:(c + 1) * 128].rearrange("k p -> p k"), ocd[:, c, :], ["ocd"], [], "o_ocd", slow=True)
            if os.environ.get("KPRE"):
                for _i in range(4):
                    _pb = bank()
                    MM(ps[_pb][:, :512], hnT[:, 0, 0:128], hnT[:, 1, 0:512], [("hnT", 0), ("hnT", 1)], [("ps", _pb)])
            pd = bank()
            for b in range(nb):
                for kc in range(8):
                    MM(ps[pd][:, b * 128:(b + 1) * 128], hnT[:, kc, b * 128:(b + 1) * 128], wdt[:, kc, :], [("hnT", kc), "wdt"], [("ps", pd)], start=(kc == 0), stop=(kc == 7))
            if os.environ.get("KPOST") and ti == int(os.environ.get("KTILE", ti)):
                for _i in range(4):
                    _pb = bank()
                    MM(ps[_pb][:, :512], hnT[:, 0, 0:128], hnT[:, 1, 0:512], [("hnT", 0), ("hnT", 1)], [("ps", _pb)])
            nb8 = nb * 8
            xd = dts[:, 0, :nb8]; ax = dts[:, 1, :nb8]; dtv = dts[:, 2, :nb8]; dta = dts[:, 3, :nb8]
            DK = ["dts"]
            TT("dve", xd.rearrange("p (b h) -> p b h", h=8), ps[pd][:, :nb * 128].rearrange("p (b h) -> p b h", h=128)[:, :, 0:8],
               dtbB[:].unsqueeze(1).broadcast_to([128, nb, 8]), ALU.add, [("ps", pd), "dtbB"], DK)
            TS("dve", ax, xd, -1.0, None, ALU.mult, None, DK, DK)
            TT("dve", ax, ax, xd, ALU.max, DK, DK)
            ACT(ax, ax, AF.Exp, DK, DK, scale=-1.0)
            ACT(ax, ax, AF.Ln, DK, DK, bias=1.0, scale=1.0)
            TS("dve", dtv, xd, 0.0, None, ALU.max, None, DK, DK)
            TT("dve", dtv, dtv, ax, ALU.add, DK, DK)
            TT("dve", dta.rearrange("p (b h) -> p b h", h=8), dtv.rearrange("p (b h) -> p b h", h=8),
               AB[:].unsqueeze(1).broadcast_to([128, nb, 8]), ALU.mult, DK + ["AB"], DK)
            CP("dve", dtb[:, 0, :nb8], dta, DK, ["dtb"])
            TT("dve", dtb[:, 1, :nb8], dta, dtb[:, 0, :nb8], ALU.subtract, DK + ["dtb"], ["dtb"])
            import os
            st1 = int(os.environ.get('KSTOP1', '99'))
            if st1 < 1: return
            for c in range(8):
                a = cacd.ap[:, :nt]
                ak = cacd.k()
                TS("dve", a, xbc.ap[:, c, 0:nt], cwd[:, c, 0:1], None, ALU.mult, None, xbc.kr(c) + CW, ak)
                for k in range(1, 4):
                    STT(a, xbc.ap[:, c, k:k + nt], cwd[:, c, k:k + 1], a, ALU.mult, ALU.add, xbc.kr(c) + CW + ak, ak)
                ACT(xc.ap[:, c, :nt], a, AF.Silu, ak + CW, xc.kr(c), bias=cbd[:, c:c + 1], scale=1.0)

            if st1 < 2: return
            nqc = 1 if sample else 8
            att_state = {}

            def attA(h):
                pr, hh = h // 2, h % 2
                rows = slice(hh * 64, hh * 64 + 64)
                pt = PT.ap[:, h % 2, :]
                segs = {}
                off = 0
                for bb in range(-4, nb):
                    if bb < 0 and not has_hist:
                        continue
                    qlo, qhi = max(0, 2 * bb), min(nqc - 1, 2 * bb + 9)
                    if qhi < qlo:
                        continue
                    ncols = 128 if sample else (qhi - qlo + 1) * 64
                    kbuf, kkey, kcol = (kT[prv], ("kT", prv, pr), (4 + bb) * 128) if bb < 0 else (kT[cur], ("kT", cur, pr), bb * 128)
                    pb = bank()
                    for kh in range(2):
                        MM(ps[pb][kh * 64:(kh + 1) * 64, :ncols], kbuf[rows, pr, kcol + kh * 64:kcol + kh * 64 + 64],
                           qT.ap[rows, pr, qlo * 64:qlo * 64 + ncols], [kkey] + qT.kr(pr), [("ps", pb)], start=True, stop=False)
                    qrel = qlo - 2 * bb
                    assert qrel * 64 + ncols <= 640
                    MM(ps[pb][:, :ncols], identb[:], EB[:, h, qrel * 64:qrel * 64 + ncols], ["identb", "EB"], [("ps", pb)], start=False, stop=True)
                    ACT(pt[:, off:off + ncols], ps[pb][:, :ncols], AF.Exp, [("ps", pb)], PT.kr(h % 2))
                    segs[bb] = (off, qlo)
                    off += ncols
                    att_state[h] = segs
                    yield
                att_state[h] = segs

            def attB(h):
                pr, hh = h // 2, h % 2
                rows = slice(hh * 64, hh * 64 + 64)
                pt = PT.ap[:, h % 2, :]
                segs = att_state[h]
                po = bank()
                for j in range(max(1, nqc // 2)):
                    bbs = [bb for bb in range(j - 4, j + 1) if bb in segs]
                    for i, bb in enumerate(bbs):
                        o, qlo = segs[bb]
                        vbuf, vkey = (Vx[prv], ("Vx", prv, 4 + bb)) if bb < 0 else (Vx[cur], ("Vx", cur, bb))
                        vb = (4 + bb) if bb < 0 else bb
                        MM(ps[po][:, j * 128:(j + 1) * 128], vbuf[:, vb, pr, hh * 64:hh * 64 + 128], pt[:, o + (2 * j - qlo) * 64:o + (2 * j - qlo) * 64 + 128],
                           [vkey] + PT.kr(h % 2), [("ps", po)], start=(i == 0), stop=(i == len(bbs) - 1))
                    yield
                drows = slice(64, 128) if hh == 0 else slice(0, 64)
                ACT(rden.ap[rows, :nt], ps[po][drows, :nt], AF.Ln, [("ps", po)], rden.k())
                ACT(rden.ap[rows, :nt], rden.ap[rows, :nt], AF.Exp, rden.k(), rden.k(), scale=-1.0)
                TT("dve", mixT[rows, pr, :nt], ps[po][rows, :nt], rden.ap[rows, :nt], ALU.mult, [("ps", po)] + rden.k(), [("mixT", pr, hh)])

            if st1 < 3: return
            if first and not sample:
                MS("dve", H[:], 0.0, [], ["H"])
                MS("dve", Hbf[0][:], 0.0, [], [("Hbf", 0)])
            hb = [0]
            kssd = int(os.environ.get('KSSD', '99'))

            def ssd_block(b):
                tok = slice(b * 128, (b + 1) * 128)
                pbt = 4
                for c in range(4):
                    TR(psb[pbt][:, c * 128:(c + 1) * 128], xc.ap[:, c, tok], identb[:], xc.kr(c) + ["identb"], [("ps", pbt)])
                CP("act", xtok.ap, psb[pbt][:, 0:512], [("ps", pbt)], xtok.k())
                pbt = 4
                for g in range(2):
                    TR(psb[pbt][:, g * 128:(g + 1) * 128], xc.ap[:, 4 + g, tok], identb[:], xc.kr(4 + g) + ["identb"], [("ps", pbt)])
                CP("act", btok.ap, psb[pbt][:, 0:256], [("ps", pbt)], btok.k())
                yield
                dt_b = dts[:, 2, b * 8:(b + 1) * 8]
                DK = ["dts"]
                Lh3 = Lhi.ap.rearrange("p (h s) -> p h s", s=128)
                Ll3 = Llo.ap.rearrange("p (h s) -> p h s", s=128)
                TT("dve", Lh3, dtb[:, 0, b * 8:(b + 1) * 8].unsqueeze(2).broadcast_to([128, 8, 128]), SLblkb[:].unsqueeze(1).broadcast_to([128, 8, 128]),
                   ALU.mult, ["dtb", "SLblkb"], Lhi.k())
                TT("dve", Ll3, dtb[:, 1, b * 8:(b + 1) * 8].unsqueeze(2).broadcast_to([128, 8, 128]), SLblkb[:].unsqueeze(1).broadcast_to([128, 8, 128]),
                   ALU.mult, ["dtb", "SLblkb"], Llo.k())
                yield
                pbs = 5
                for h in range(8):
                    MM(ps[pbs][:, h * 64:(h + 1) * 64], Lh3[:, h, :], Ucb[:], Lhi.k() + ["Ucb"], [("ps", pbs)], start=True, stop=False)
                    MM(ps[pbs][:, h * 64:(h + 1) * 64], Ll3[:, h, :], Ucb[:], Llo.k() + ["Ucb"], [("ps", pbs)], start=False, stop=True)
                yield
                pbc = 6
                MM(ps[pbc][:, 256:264], Ublkb[:], dtb[:, 0, b * 8:(b + 1) * 8], ["dtb", "Ublkb"], [("ps", pbc)], start=True, stop=False)
                MM(ps[pbc][:, 256:264], Ublkb[:], dtb[:, 1, b * 8:(b + 1) * 8], ["dtb", "Ublkb"], [("ps", pbc)], start=False, stop=True)
                for ch in range(2):
                    MM(ps[pbc][:, 264 + ch * 8:272 + ch * 8], onesAB[:, ch, :], dtb[:, 0, b * 8:(b + 1) * 8], ["dtb", "onesAB"], [("ps", pbc)], start=True, stop=False)
                    MM(ps[pbc][:, 264 + ch * 8:272 + ch * 8], onesAB[:, ch, :], dtb[:, 1, b * 8:(b + 1) * 8], ["dtb", "onesAB"], [("ps", pbc)], start=False, stop=True)
                for g in range(2):
                    MM(ps[pbc][:, g * 128:(g + 1) * 128], xc.ap[:, 4 + g, tok], xc.ap[:, 6 + g, tok], xc.kr(4 + g) + xc.kr(6 + g), [("ps", pbc)])
                ACT(decT.ap, ps[pbs][:, :], AF.Exp, [("ps", pbs), ("ps", pbc)], decT.k())
                ACT(ecd[:, 0:24], ps[pbc][:, 256:280], AF.Exp, [("ps", pbc)], ["ecd"])
                TT("dve", cbm.ap, ps[pbc][:, 0:256].rearrange("p (g t) -> p g t", t=128),
                   Ublk[:].unsqueeze(1).broadcast_to([128, 2, 128]), ALU.mult, [("ps", pbc), "Ublk"], cbm.k())
                yield
                P.op("act", lambda e: e.memzero(MTb.ap), reads=[], writes=MTb.k())
                for ch in range(2):
                    rw = slice(ch * 64, ch * 64 + 64)
                    for g in range(2):
                        TT("dve", MTb.ap[rw, g * 4:(g + 1) * 4, ch * 64:(ch + 1) * 64],
                           decT.ap[rw, g * 256:(g + 1) * 256].rearrange("p (h t) -> p h t", t=64),
                           cbm.ap[rw, g, ch * 64:(ch + 1) * 64].unsqueeze(1).broadcast_to([64, 4, 64]), ALU.mult, decT.k() + cbm.k() + MTb.k(), MTb.k())
                dd, ddk = sm(8)
                TT("dve", dd, dt_b, decT.ap.rearrange("p (h t) -> p h t", t=64)[:, :, 63], ALU.mult, DK + decT.k(), [ddk])
                x3 = xtok.ap.rearrange("p (h q) -> p h q", q=64)
                TT("dve", xdt.ap.rearrange("p (h q) -> p h q", q=64), x3, dt_b.unsqueeze(2).broadcast_to([128, 8, 64]), ALU.mult, xtok.k() + DK, xdt.k())
                P.op("act", lambda e: e.memzero(xddz.ap), reads=[], writes=xddz.k())
                for ch in range(2):
                    rw = slice(ch * 64, ch * 64 + 64)
                    TT("dve", xddz.ap[rw, ch, :].rearrange("p (h q) -> p h q", q=64), x3[rw], dd[rw].unsqueeze(2).broadcast_to([64, 8, 64]), ALU.mult,
                       xtok.k() + [ddk] + xddz.k(), xddz.k())
                yield
                pby = 5
                for h in range(8):
                    MM(ps[pby][:, h * 64:(h + 1) * 64], MTb.ap[:, h, :], xdt.ap[:, h * 64:(h + 1) * 64], MTb.k() + xdt.k(), [("ps", pby)])
                yield
                pbi = [None, None]
                for ch in range(2):
                    hcur = hb[0] % 2
                    pbi[ch] = 6 if ch == 0 else 7
                    for g in range(2):
                        MM(ps[pbi[ch]][:, g * 256:(g + 1) * 256], xc.ap[:, 6 + g, tok], Hbf[hcur][:, g * 256:(g + 1) * 256], xc.kr(6 + g) + [("Hbf", hcur)], [("ps", pbi[ch])])
                    rw = slice(ch * 64, ch * 64 + 64)
                    TT("dve", t1.ap[rw].rearrange("p (h q) -> p h q", q=64), ps[pbi[ch]][rw, :].rearrange("p (h q) -> p h q", q=64),
                       ecd[rw, 0:8].unsqueeze(2).broadcast_to([64, 8, 64]), ALU.mult, [("ps", pbi[ch]), "ecd"] + t1.k(), t1.k())
                    if sample and ch == 1:
                        continue
                    pS = 4
                    for g in range(2):
                        MM(ps[pS][:, g * 256:(g + 1) * 256], btok.ap[:, g * 128:(g + 1) * 128], xddz.ap[:, ch, g * 256:(g + 1) * 256], btok.k() + xddz.k(), [("ps", pS)])
                    H3 = H[:].rearrange("p (h q) -> p h q", q=64)
                    TT("dve", H3, H3, ecd[:, 8 + ch * 8:16 + ch * 8].unsqueeze(2).broadcast_to([128, 8, 64]), ALU.mult, ["H", "ecd"], ["H"])
                    TT("dve", H[:], H[:], ps[pS][:, :], ALU.add, ["H", ("ps", pS)], ["H"])
                    hb[0] += 1
                    CP("act", Hbf[hb[0] % 2][:], H[:], ["H"], [("Hbf", hb[0] % 2)])
                yield
                TT("dve", t2.ap.rearrange("p (h q) -> p h q", q=64), x3, dskB[:].unsqueeze(2).broadcast_to([128, 8, 64]), ALU.mult, xtok.k() + ["dskB"], t2.k())
                TT("dve", t1.ap, t1.ap, t2.ap, ALU.add, t1.k() + t2.k(), t1.k())
                TT("dve", yb.ap, ps[pby][:, :], t1.ap, ALU.add, [("ps", pby), ("ps", pbi[1])] + t1.k(), yb.k())
                TT("dve", yb.ap, yb.ap, zs.ap[:, b, :], ALU.mult, yb.k() + zs.kr(b), yb.k())
                yield
                ss_, ssk = sm(4)
                MS("dve", ss_, 0.0, [], [ssk])
                for g in range(2):
                    ACT(t2.ap[:, g * 256:(g + 1) * 256], yb.ap[:, g * 256:(g + 1) * 256], AF.Square, yb.k() + [ssk], t2.k() + [ssk], accum_out=ss_[:, g:g + 1])
                ACT(ss_[:, 2:4], ss_[:, 0:2], AF.Sqrt, [ssk], [ssk], bias=EPS, scale=1.0 / 256)
                RCP(ss_[:, 2:4], ss_[:, 2:4], [ssk], [ssk])
                TT("dve", yb.ap.rearrange("p (g q) -> p g q", q=256), yb.ap.rearrange("p (g q) -> p g q", q=256),
                   ss_[:, 2:4].unsqueeze(2).broadcast_to([128, 2, 256]), ALU.mult, yb.k() + [ssk], yb.k())
                TT("dve", ynb.ap, yb.ap, ngdB[:], ALU.mult, yb.k() + ["ngdB"], ynb.k())
                pbt = 4
                for c in range(4):
                    TR(psb[pbt][:, c * 128:(c + 1) * 128], ynb.ap[:, c * 128:(c + 1) * 128], identb[:], ynb.k() + ["identb"], [("ps", pbt)])
                CP("act", mixT[:, 4:8, tok], psb[pbt][:, 0:512].rearrange("p (c t) -> p c t", t=128), [("ps", pbt)], [k_ for c_ in range(4, 8) for k_ in mixk(c_)])
            att_units = [("A", 0), ("A", 1)]
            for h in range(8):
                att_units.append(("B", h))
                if h + 2 < 8:
                    att_units.append(("A", h + 2))

            def ssd_all():
                for b_ in range(nb):
                    yield from ssd_block(b_)
                    yield

            ring[:] = [0, 1, 2, 3]
            gen = ssd_all()

            def att_all():
                for kind, i in att_units:
                    yield from (attA(i) if kind == "A" else attB(i))
                    yield

            agen = att_all()
            ssd_done = att_done = False
            while not (ssd_done and att_done):
                if not ssd_done:
                    try:
                        next(gen)
                    except StopIteration:
                        ssd_done = True
                for _ in range(3):
                    if not att_done:
                        try:
                            next(agen)
                        except StopIteration:
                            att_done = True
            ring[:] = list(range(8))
            if st1 < 4: return
            if last or sample:
                pbt = bank()
                for c in range(4):
                    TR(ps[pbt][:, c * 128:(c + 1) * 128], H[:, c * 128:(c + 1) * 128], identf[:], ["H", "identf"], [("ps", pbt)])
                CP("act", ost[:], ps[pbt][:, :].rearrange("p (c n) -> p c n", n=128), [("ps", pbt)], ["ost"])
                dst = sss if sample else ssp[seq * 512:(seq + 1) * 512, :]
                DMA("act", dst.rearrange("(c p) n -> p c n", p=128), ost[:], ["ost"], [], "o_ost")
            out_proj(G["out_cd"], lambda kc, nt: mixT[:, kc, :nt], mixk, nt, nb, 8)

        dts = T("dts", [128, 4, 32])
        ecd = T("ecd", [128, 24])

        def load_x(src_rows, nb, sample):
            for b in range(nb):
                xs_ = xin[b % 2]
                if sample:
                    MS("dve", xs_[:], 0.0, [], [("xin", b % 2)])
                    DMA("sp", xs_[0:64, :], src_rows, [("xin", b % 2)], [("xin", b % 2)], ("xin", b % 2))
                else:
                    DMA("sp", xs_[:], src_rows[b * 128:(b + 1) * 128, :], [], [("xin", b % 2)], ("xin", b % 2))
                for half in range(2):
                    pb = bank()
                    for j in range(4):
                        kc = half * 4 + j
                        TR(ps[pb][:, j * 128:(j + 1) * 128], xs_[:, kc * 128:(kc + 1) * 128], identf[:], [("xin", b % 2), "identf"], [("ps", pb)])
                    CP("act", hT[:, half * 4:half * 4 + 4, b * 128:(b + 1) * 128], ps[pb][:, :].rearrange("p (j t) -> p j t", t=128),
                       [("ps", pb)], [("hT", half * 4 + j, b) for j in range(4)])

        def store_y(dst_rows, nt, nb, sample):
            norm(4, nt, nb, final_out=True)
            for b in range(nb):
                y_ = yst[b % 2]
                for half in range(2):
                    pb = bank()
                    for j in range(4):
                        kc = half * 4 + j
                        TR(ps[pb][:, j * 128:(j + 1) * 128], hT[:, kc, b * 128:(b + 1) * 128], identf[:], hk(kc, nb) + ["identf"], [("ps", pb)])
                    CP("act", y_[:, half * 512:(half + 1) * 512], ps[pb][:, :], [("ps", pb)], [("yst", b % 2, half)])
                yk = [("yst", b % 2, 0), ("yst", b % 2, 1)]
                if sample:
                    DMA("act", dst_rows, y_[0:64, :], yk, [], ("o_y", b % 2))
                else:
                    DMA("act", dst_rows[b * 128:(b + 1) * 128, :], y_[:], yk, [], ("o_y", b % 2))

        def tile(ti, x_rows, y_rows, nt, nb, first, last, sample, seq):
            import os
            stop = int(os.environ.get("KSTOP", "99"))
            if stop < 1: return
            load_x(x_rows, nb, sample)
            if stop < 2: return
            layer0(ti, nt, nb, first, last, sample, seq)
            if stop < 3: return
            ffn(0, nt, nb)
            if stop < 4: return
            layer1(ti, nt, nb, first, last, sample, seq)
            if stop < 5: return
            ffn(1, nt, nb)
            if stop < 6: return
            store_y(y_rows, nt, nb, sample)

        gt = 0
        try:
          for s in range(NPS):
              for t in range(NTILE):
                  r0 = s * SEQ + t * 512
                  tile(gt, xp[r0:r0 + 512, :], yp[r0:r0 + 512, :], 512, 4, t == 0, t == NTILE - 1, False, s)
                  gt += 1
          cur, prv = gt % 2, (gt + 1) % 2
          DMA("sp", ckf.ap, ck.rearrange("(b p) c -> p b c", p=128), [], ckf.k(), "smp0")
          CP("dve", ckb.ap, ckf.ap, ckf.k(), ckb.k())
          for b in range(4):
              pbt = bank()
              for pr in range(4):
                  TR(psb[pbt][:, pr * 128:(pr + 1) * 128], ckb.ap[:, b, pr * 128:(pr + 1) * 128], identb[:], ckb.k() + ["identb"], [("ps", pbt)])
              CP("act", kT[prv][:, :, b * 128:(b + 1) * 128], psb[pbt][:, 0:512].rearrange("p (r t) -> p r t", t=128), [("ps", pbt)], [("kT", prv, pr) for pr in range(4)])
          DMA("sp", cvf.ap, cv.rearrange("(b p) c -> p b c", p=128), [], cvf.k(), "smp1")
          for b in range(4):
              CP("dve", bass.AP(Vx[prv], b * 768, [[4 * 768, 128], [192, 4], [128, 2], [1, 64]]),
                 cvf.ap[:, b, :].rearrange("p (r two c) -> p r two c", two=2, c=64), cvf.k(), [("Vx", prv, b)])
          DMA("sp", ssf.ap, ssm.rearrange("(c p) n -> p c n", p=128), [], ssf.k(), "smp2")
          pbt = bank()
          for c in range(4):
              TR(ps[pbt][:, c * 128:(c + 1) * 128], ssf.ap[:, c, :], identf[:], ssf.k() + ["identf"], [("ps", pbt)])
          CP("act", H[:], ps[pbt][:, :], [("ps", pbt)], ["H"])
          CP("act", Hbf[0][:], H[:], ["H"], [("Hbf", 0)])
          tile(gt, xs, ys, 128, 1, True, True, True, 0)


        except StopIteration:
            pass
        print('n_ops', len(P.ops))
        P.emit(st)
    return nc


_CACHE = {}


def _get_nc(NPS, SEQ):
    key = (NPS, SEQ)
    if key not in _CACHE:
        _CACHE[key] = build(NPS, SEQ)
    return _CACHE[key]


def run_cores(inputs, n_cores, NPS, SEQ):
    f = lambda a: np.ascontiguousarray(np.asarray(a, dtype=np.float32))
    I = {k: f(v) for k, v in inputs.items()}
    nc = _get_nc(NPS, SEQ)
    in_maps = []
    for c in range(n_cores):
        m = {
            "xp": I["x_prompt"][c * NPS:(c + 1) * NPS].reshape(NPS * SEQ, D),
            "xs": I["x_sample"][c],
            "ck": I["cache_k_c"][0, c].reshape(512, 512), "cv": I["cache_v_c"][0, c].reshape(512, 512),
            "sca": I["state_conv_a"][0, c], "scd": I["state_conv_d"][0, c], "ssm": I["state_ssm_d"][0, c].reshape(512, 128),
            "norm_mix": I["norm_mix"], "norm_ffn": I["norm_ffn"], "norm_final": I["norm_final"].reshape(1, D),
            "w_in_ab": I["w_in_ab"][0], "conv_w_a": I["conv_w_a"][0], "ln_g_b": I["ln_g_b"][0], "ln_b_b": I["ln_b_b"][0],
            "w_s_b": I["w_s_b"][0], "b_s_b": I["b_s_b"][0].reshape(512), "w_out_ab": I["w_out_ab"][0], "w_in_cd": I["w_in_cd"][0],
            "rel_bias_c": I["rel_bias_c"][0], "conv_w_d": I["conv_w_d"][0], "conv_b_d": I["conv_b_d"][0],
            "dt_bias_d": I["dt_bias_d"][0], "a_log_d": I["a_log_d"][0], "d_skip_d": I["d_skip_d"][0], "norm_g_d": I["norm_g_d"][0],
            "w_out_cd": I["w_out_cd"][0], "w_gate": I["w_gate"], "w_up": I["w_up"], "w_down": I["w_down"],
        }
        in_maps.append({k: np.ascontiguousarray(v) for k, v in m.items()})
    res = run_bass_kernel_spmd(nc, in_maps, core_ids=list(range(n_cores)))
    R_ = res.results
    cat = lambda k: np.concatenate([r[k] for r in R_], axis=0)
    B = n_cores * NPS
    outs = (
        cat("yp").reshape(B, SEQ, D),
        cat("ys").reshape(n_cores, 64, D),
        cat("cap").reshape(1, B, 2, 512),
        cat("cas").reshape(1, n_cores, 2, 512),
        cat("vbs").reshape(1, n_cores, 64, 512),
        cat("kcp").reshape(1, B, 512, 8, 64),
        cat("vcp").reshape(1, B, 512, 8, 64),
        cat("kcs").reshape(1, n_cores, 64, 8, 64),
        cat("vcs").reshape(1, n_cores, 64, 8, 64),
        cat("cdp").reshape(1, B, 3, 1024),
        cat("cds").reshape(1, n_cores, 3, 1024),
        cat("ssp").reshape(1, B, 8, 64, 128),
        cat("sss").reshape(1, n_cores, 8, 64, 128),
    )
    return tuple(np.ascontiguousarray(o, dtype=np.float32) for o in outs)


def kernel(**inputs):
    return run_cores(inputs, 8, 2, 4096)
```

```python
import bisect
from contextlib import ExitStack
import numpy as np
import concourse.bass as bass
import concourse.mybir as mybir
from concourse.bass_utils import run_bass_kernel_spmd

F32 = mybir.dt.float32
BF16 = mybir.dt.bfloat16
AF = mybir.ActivationFunctionType
ALU = mybir.AluOpType

D = 1024
FF = 2816
EPS = 1e-5
NSLOT = 4
SLOT_E = 4096


class _Op:
    __slots__ = ("eng", "fn", "deps", "dma", "semkey", "sig", "need_sig", "idx")


class Prog:
    ENGS = ("pe", "act", "dve", "pool", "sp")

    def __init__(self, nc):
        self.nc = nc
        self.ops = []
        self.last_w = {}
        self.readers = {}
        self.total_keys = set()
        self.sig_idx = {}

    def op(self, eng, fn, reads=(), writes=(), dma=False, semkey=None):
        o = _Op()
        o.eng, o.fn, o.dma, o.semkey = eng, fn, dma, ((semkey, eng) if dma else None)
        o.sig = None
        o.need_sig = dma
        o.idx = len(self.ops)
        psr = [b for b in reads if isinstance(b, tuple) and b[0] == "ps"]
        if psr:
            writes = list(writes) + [b for b in psr if b not in writes]
        deps = {}
        for b in reads:
            w = self.last_w.get(b)
            if w is not None:
                deps[w.idx] = w
        for b in writes:
            w = self.last_w.get(b)
            if w is not None:
                deps[w.idx] = w
            for r in self.readers.get(b, ()):
                if r.eng != eng or r.dma:
                    deps[r.idx] = r
        pruned = {}
        for di in sorted(deps, reverse=True):
            d = deps[di]
            if d.eng == eng and not d.dma and not dma and eng == "pe":
                continue
            if not d.dma:
                lst = self.sig_idx.setdefault(d.eng, [])
                j = bisect.bisect_left(lst, d.idx)
                if j < len(lst):
                    d = self.ops[lst[j]]
                else:
                    lst.append(d.idx)
                    d.need_sig = True
            else:
                d.need_sig = True
            pruned[d.idx] = d
        o.deps = list(pruned.values())
        for b in reads:
            self.readers.setdefault(b, []).append(o)
        for b in writes:
            self.last_w[b] = o
            self.readers[b] = []
        self.ops.append(o)
        return o

    def emit(self, stack, final_wait_eng="sp"):
        nc = self.nc
        eng_sem = {e: stack.enter_context(nc.semaphore("s_" + e)) for e in self.ENGS}
        dma_sem, cnt = {}, {}
        for o in self.ops:
            if o.dma:
                k = o.semkey
                if k not in dma_sem:
                    dma_sem[k] = stack.enter_context(nc.semaphore("d_%d" % len(dma_sem)))
                cnt[k] = cnt.get(k, 0) + 16
                o.sig = (dma_sem[k], cnt[k])
            elif o.need_sig:
                k = ("e", o.eng)
                cnt[k] = cnt.get(k, 0) + 1
                o.sig = (eng_sem[o.eng], cnt[k])
        for o in self.ops:
            if o.dma and o.semkey[0] in self.total_keys:
                o.sig = (dma_sem[o.semkey], cnt[o.semkey])
        self.n_sems = len(dma_sem) + len(eng_sem)
        print('sems', self.n_sems, {k: v for k, v in cnt.items() if isinstance(k, tuple) and k[0] == 'e'}, 'maxdma', max(v for k, v in cnt.items()))
        final = [(dma_sem[k], cnt[k]) for k in dma_sem]
        per_eng = {e: [o for o in self.ops if o.eng == e] for e in self.ENGS}
        block = stack.enter_context(nc.Block())

        def run(e, h):
            waited = {}
            for o in per_eng[e]:
                need = {}
                for d in o.deps:
                    s, v = d.sig
                    if need.get(id(s), (None, 0))[1] < v:
                        need[id(s)] = (s, v)
                for s, v in need.values():
                    if waited.get(id(s), 0) >= v:
                        continue
                    waited[id(s)] = v
                    h.wait_ge(s, v)
                ins = o.fn(h)
                if o.sig is not None:
                    ins.then_inc(o.sig[0], 16 if o.dma else 1)
            if e == final_wait_eng:
                for s, v in final:
                    if waited.get(id(s), 0) < v:
                        h.wait_ge(s, v)

        block.tensor(lambda h: run("pe", h))
        block.scalar(lambda h: run("act", h))
        block.vector(lambda h: run("dve", h))
        block.gpsimd(lambda h: run("pool", h))
        block.sync(lambda h: run("sp", h))


def build(NPS, SEQ):
    NTILE = SEQ // 512
    nc = bass.Bass("TRN2", target_bir_lowering=False)

    def din(name, shape):
        return nc.dram_tensor(name, list(shape), F32, kind="ExternalInput").ap()

    def dout(name, shape):
        return nc.dram_tensor(name, list(shape), F32, kind="ExternalOutput").ap()

    xp = din("xp", [NPS * SEQ, D]); xs = din("xs", [64, D])
    ck = din("ck", [512, 512]); cv = din("cv", [512, 512])
    sca = din("sca", [2, 512]); scd = din("scd", [3, 1024]); ssm = din("ssm", [512, 128])
    norm_mix = din("norm_mix", [2, D]); norm_ffn = din("norm_ffn", [2, D]); norm_final = din("norm_final", [1, D])
    w_in_ab = din("w_in_ab", [D, 2560]); conv_w_a = din("conv_w_a", [3, 512])
    ln_g_b = din("ln_g_b", [512]); ln_b_b = din("ln_b_b", [512])
    w_s_b = din("w_s_b", [4, 128, 128]); b_s_b = din("b_s_b", [512])
    w_out_ab = din("w_out_ab", [D, D]); w_in_cd = din("w_in_cd", [D, 3080])
    rel_bias = din("rel_bias_c", [8, 257]); conv_w_d = din("conv_w_d", [4, 1024]); conv_b_d = din("conv_b_d", [1024])
    dt_bias = din("dt_bias_d", [8]); a_log = din("a_log_d", [8]); d_skip = din("d_skip_d", [8])
    norm_g_d = din("norm_g_d", [512]); w_out_cd = din("w_out_cd", [D, D])
    w_gate = din("w_gate", [2, D, FF]); w_up = din("w_up", [2, D, FF]); w_down = din("w_down", [2, FF, D])

    yp = dout("yp", [NPS * SEQ, D]); ys = dout("ys", [64, D])
    cap = dout("cap", [NPS, 2, 512]); cas = dout("cas", [1, 2, 512]); vbs = dout("vbs", [64, 512])
    kcp = dout("kcp", [NPS * 512, 512]); vcp = dout("vcp", [NPS * 512, 512])
    kcs = dout("kcs", [64, 512]); vcs = dout("vcs", [64, 512])
    cdp = dout("cdp", [NPS, 3, 1024]); cds = dout("cds", [1, 3, 1024])
    ssp = dout("ssp", [NPS * 512, 128]); sss = dout("sss", [512, 128])

    gran = []

    def add_gran(w2d, K, c0, cols, grp):
        gran.append((w2d[:, c0:c0 + cols], K // 128, cols, grp))
        return len(gran) - 1

    G = {}
    for l in range(2):
        if l == 0:
            G["in_ab"] = [add_gran(w_in_ab, D, i * 512, 512, "in_ab") for i in range(5)]
            G["out_ab"] = [add_gran(w_out_ab, D, i * 512, 512, "out_ab") for i in range(2)]
        else:
            G["in_cd"] = [add_gran(w_in_cd, D, i * 512, 512, "in_cd") for i in range(6)]
            G["out_cd"] = [add_gran(w_out_cd, D, i * 512, 512, "out_cd") for i in range(2)]
        G["gate%d" % l] = [add_gran(w_gate[l], D, i * 512, min(512, FF - i * 512), "gate%d" % l) for i in range(6)]
        G["up%d" % l] = [add_gran(w_up[l], D, i * 512, min(512, FF - i * 512), "up%d" % l) for i in range(6)]
        G["down%d" % l] = [add_gran(w_down[l], FF, i * 128, 128, "down%d" % l) for i in range(8)]
    scr = nc.dram_tensor("wscr", [len(gran), 128, SLOT_E], BF16, kind="Internal").ap()
    tpad = nc.dram_tensor("tpad", [8, 768], F32, kind="Internal").ap()

    st = ExitStack()
    with st:
        P = Prog(nc)
        P.total_keys.add("const")
        P.total_keys.add("haloA")
        P.total_keys.add("haloD")
        for g in gran:
            P.total_keys.add(("scr", g[3]))

        def T(name, shape, dt=F32):
            return st.enter_context(nc.sbuf_tensor(name, list(shape), dt))

        hT = T("hT", [128, 8, 512]); hnT = T("hnT", [128, 8, 512], BF16); mixT = T("mixT", [128, 8, 512], BF16)
        slots = [T("slot%d" % i, [128, SLOT_E], BF16) for i in range(NSLOT)]
        xin = [T("xin%d" % i, [128, D]) for i in range(2)]
        yst = [T("yst%d" % i, [128, D]) for i in range(2)]
        sq = [T("sq%d" % i, [128, 512], BF16) for i in range(2)]
        rstd = T("rstd", [128, 512])
        identb = T("identb", [128, 128], BF16); identf = T("identf", [128, 128])
        onesD = T("onesD", [128, 128], BF16); onesf = T("onesf", [128, 128])
        gcol = T("gcol", [128, 5, 8])
        cwa = T("cwa", [128, 4, 3]); cwd = T("cwd", [128, 8, 4]); cbd = T("cbd", [128, 8])
        lngB = T("lngB", [128, 512]); lnbB = T("lnbB", [128, 512]); ngdB = T("ngdB", [128, 512])
        wsT = T("wsT", [128, 4, 128], BF16); bsB = T("bsB", [128, 512])
        EB = T("EB", [128, 8, 640], BF16)
        Ublk = T("Ublk", [128, 128]); SLblk = T("SLblk", [128, 128]); Uc = T("Uc", [128, 64]); SLc = T("SLc", [128, 64])
        dtbB = T("dtbB", [128, 8]); AB = T("AB", [128, 8]); dskB = T("dskB", [128, 8])
        wdt = T("wdt", [128, 8, 128], BF16)
        haloA = T("haloA", [128, 4, 2]); haloD = T("haloD", [128, 8, 3], BF16)
        kT = [T("kT%d" % i, [128, 4, 512], BF16) for i in range(2)]
        Vx = [T("Vx%d" % i, [128, 4, 4, 192], BF16) for i in range(2)]
        H = T("H", [128, 512]); Hbf = [T("Hbf%d" % i, [128, 512], BF16) for i in range(2)]
        small = T("small", [128, 64])
        oca = T("oca", [128, 4, 2]); ocd = T("ocd", [128, 8, 3]); ost = T("ost", [128, 4, 128])
        kvst = [T("kvst%d" % i, [128, 512]) for i in range(2)]
        RW = 16128
        R = T("R", [128, RW])
        R16 = R.bitcast(BF16)
        ps = [st.enter_context(nc.psum_tensor("ps%d" % i, [128, 512], F32)) for i in range(8)]
        psb = [p.bitcast(BF16) for p in ps]

        class RB:
            def __init__(self, off_b, dt, shape):
                self.esz = 2 if dt == BF16 else 4
                self.off = off_b
                n = int(np.prod(shape))
                base = R16 if dt == BF16 else R
                e0 = off_b // self.esz
                v = base[:, e0:e0 + n]
                if len(shape) == 2:
                    v = v.rearrange("p (a b) -> p a b", b=shape[1])
                elif len(shape) == 3:
                    v = v.rearrange("p (a b c) -> p a b c", b=shape[1], c=shape[2])
                self.ap = v
                self.shape = shape
                self.n = n
                assert off_b + n * self.esz <= RW * 4, (off_b, n)

            def k(self, lo=0, hi=None):
                hi = self.n if hi is None else hi
                b0 = (self.off + lo * self.esz) // 1024
                b1 = (self.off + hi * self.esz - 1) // 1024
                return [("R", u) for u in range(b0, b1 + 1)]

            def kr(self, a, lo=0, hi=None):
                w = int(np.prod(self.shape[1:]))
                hi = w if hi is None else hi
                return self.k(a * w + lo, a * w + hi)

        KB = 1024
        uu = RB(0, F32, [4, 516]); gbuf = RB(9 * KB, BF16, [4, 512]); gu = RB(13 * KB, BF16, [4, 512])
        vtok = RB(17 * KB, F32, [4, 512]); vln = RB(25 * KB, BF16, [4, 512]); ftmp = RB(29 * KB, F32, [2, 512])
        cacc = RB(33 * KB, F32, [2, 512])
        actT = RB(0, BF16, [22, 512]); sgb = RB(22 * KB, BF16, [2, 512])
        qT = RB(0, BF16, [4, 512]); PT = RB(4 * KB, BF16, [2, 2560]); zs = RB(14 * KB, BF16, [4, 512])
        xbc = RB(18 * KB, BF16, [8, 516]); xc = RB(27 * KB, BF16, [8, 512]); tmpE = RB(35 * KB, F32, [2, 512])
        xtok = RB(39 * KB, BF16, [512]); btok = RB(40 * KB, BF16, [256]); Lhi = RB(41 * KB, BF16, [1024])
        decT = RB(43 * KB, F32, [512]); cbm = RB(45 * KB, F32, [2, 128]); MTb = RB(46 * KB, BF16, [8, 128])
        xdt = RB(48 * KB, BF16, [512]); xddz = RB(49 * KB, BF16, [2, 512]); t1 = RB(51 * KB, F32, [512])
        t2 = RB(53 * KB, F32, [512]); yb = RB(55 * KB, F32, [512]); ynb = RB(57 * KB, BF16, [512])
        rden = RB(58 * KB, F32, [512]); Llo = RB(61 * KB, BF16, [1024]); cacd = RB(61 * KB, F32, [512])
        ckf = RB(0, F32, [4, 512]); ckb = RB(8 * KB, BF16, [4, 512]); cvf = RB(12 * KB, F32, [4, 512])
        ssf = RB(20 * KB, F32, [4, 128]); scdf = RB(60 * KB, F32, [8, 3])

        bank_ctr = [0]

        ring = list(range(8))

        def bank():
            b = ring[bank_ctr[0] % len(ring)]
            bank_ctr[0] += 1
            return b

        def MM(out, lhsT, rhs, r, w, start=True, stop=True):
            P.op("pe", lambda e: e.matmul(out, lhsT=lhsT, rhs=rhs, start=start, stop=stop), reads=r, writes=w)

        def TR(out, in_, ident, r, w):
            P.op("pe", lambda e: e.transpose(out, in_, ident), reads=r, writes=w)

        def ACT(out, in_, func, r, w, **kw):
            P.op("act", lambda e: e.activation(out=out, in_=in_, func=func, **kw), reads=r, writes=w)

        def TT(eng, out, in0, in1, op, r, w):
            P.op(eng, lambda e: e.tensor_tensor(out=out, in0=in0, in1=in1, op=op), reads=r, writes=w)

        def TS(eng, out, in0, s1, s2, op0, op1, r, w):
            if s2 is None:
                P.op(eng, lambda e: e.tensor_scalar(out=out, in0=in0, scalar1=s1, scalar2=None, op0=op0), reads=r, writes=w)
            else:
                P.op(eng, lambda e: e.tensor_scalar(out=out, in0=in0, scalar1=s1, scalar2=s2, op0=op0, op1=op1), reads=r, writes=w)

        def STT(out, in0, sc, in1, op0, op1, r, w):
            P.op("dve", lambda e: e.scalar_tensor_tensor(out=out, in0=in0, scalar=sc, in1=in1, op0=op0, op1=op1), reads=r, writes=w)

        def CP(eng, out, in_, r, w):
            if eng == "act":
                P.op("act", lambda e: e.copy(out=out, in_=in_), reads=r, writes=w)
            else:
                P.op(eng, lambda e: e.tensor_copy(out=out, in_=in_), reads=r, writes=w)

        def RCP(out, in_, r, w):
            P.op("dve", lambda e: e.reciprocal(out=out, in_=in_), reads=r, writes=w)

        def MS(eng, ap, val, r, w):
            P.op(eng, lambda e: e.memset(ap, val), reads=r, writes=w)

        def DMA(eng, out, in_, r, w, semkey, slow=False):
            def f(e):
                if slow:
                    with nc.allow_non_contiguous_dma(reason="tiny strided"):
                        return e.dma_start(out=out, in_=in_)
                return e.dma_start(out=out, in_=in_)
            P.op(eng, f, reads=r, writes=w, dma=True, semkey=semkey)

        def ASEL(out, pattern, cmp, fill, base, cm, r, w):
            P.op("pool", lambda e: e.affine_select(out=out, in_=out, pattern=pattern, compare_op=cmp, fill=fill, base=base, channel_multiplier=cm), reads=r, writes=w)

        CK = "const"
        MS("pool", wdt[:], 0.0, [], ["wdt"])
        DMA("pool", wdt[:, :, 0:8], w_in_cd[:, 3072:3080].rearrange("(k p) c -> p k c", p=128), ["wdt"], ["wdt"], "const", slow=True)
        MS("pool", identf[:], 0.0, [], ["identf"])
        ASEL(identf[:], [[-1, 128]], ALU.not_equal, 1.0, 0, 1, ["identf"], ["identf"])
        CP("pool", identb[:], identf[:], ["identf"], ["identb"])
        MS("pool", onesD[:], 1.0 / D, [], ["onesD"])
        MS("pool", onesf[:], 1.0, [], ["onesf"])
        MS("pool", Ublk[:], 1.0, [], ["Ublk"])
        ASEL(Ublk[:], [[1, 128]], ALU.is_ge, 0.0, 0, -1, ["Ublk"], ["Ublk"])
        MS("pool", Ublk[0:64, 64:128], 0.0, ["Ublk"], ["Ublk"])
        MS("pool", SLblk[:], 1.0, [], ["SLblk"])
        ASEL(SLblk[:], [[-1, 128]], ALU.is_gt, 0.0, 0, 1, ["SLblk"], ["SLblk"])
        MS("pool", SLblk[64:128, 0:64], 0.0, ["SLblk"], ["SLblk"])
        CP("pool", Uc[0:64, :], Ublk[0:64, 0:64], ["Ublk"], ["Uc"])
        CP("pool", Uc[64:128, :], Ublk[64:128, 64:128], ["Ublk", "Uc"], ["Uc"])
        CP("pool", SLc[0:64, :], SLblk[0:64, 0:64], ["SLblk"], ["SLc"])
        CP("pool", SLc[64:128, :], SLblk[64:128, 64:128], ["SLblk", "SLc"], ["SLc"])
        Ucb = T("Ucb", [128, 64], BF16); Ublkb = T("Ublkb", [128, 128], BF16); onesb = T("onesb", [128, 128], BF16)
        dtb = T("dtb", [128, 2, 32], BF16)
        SLblkb = T("SLblkb", [128, 128], BF16); onesAB = T("onesAB", [128, 2, 128], BF16)
        CP("pool", SLblkb[:], SLblk[:], ["SLblk"], ["SLblkb"])
        MS("pool", onesAB[:], 0.0, [], ["onesAB"])
        MS("pool", onesAB[0:64, 0, :], 1.0, ["onesAB"], ["onesAB"])
        MS("pool", onesAB[64:128, 1, :], 1.0, ["onesAB"], ["onesAB"])
        CP("pool", Ucb[:], Uc[:], ["Uc"], ["Ucb"])
        CP("pool", Ublkb[:], Ublk[:], ["Ublk"], ["Ublkb"])
        MS("pool", onesb[:], 1.0, [], ["onesb"])
        for i in range(2):
            MS("pool", Vx[i][:], 1.0, [], [("Vx", i, b) for b in range(4)])
        for n, src in enumerate([norm_mix[0:1], norm_ffn[0:1], norm_mix[1:2], norm_ffn[1:2], norm_final]):
            DMA("sp", gcol[:, n, :], src[0].rearrange("(k p) -> p k", p=128), [], [("gcol", n)], CK, slow=True)
        GC = [("gcol", n) for n in range(5)]
        for c in range(4):
            DMA("sp", cwa[:, c, :], conv_w_a[:, c * 128:(c + 1) * 128].rearrange("k p -> p k"), [], [("cwa", c)], CK, slow=True)
        for c in range(8):
            DMA("sp", cwd[:, c, :], conv_w_d[:, c * 128:(c + 1) * 128].rearrange("k p -> p k"), [], [("cwd", c)], CK, slow=True)
        DMA("sp", cbd[:], conv_b_d.rearrange("(k p) -> p k", p=128), [], ["cbd"], CK, slow=True)
        CW = [("cwa", c) for c in range(4)] + [("cwd", c) for c in range(8)] + ["cbd"]
        DMA("sp", lngB[:], ln_g_b.partition_broadcast(128), [], ["lngB"], CK)
        DMA("sp", lnbB[:], ln_b_b.partition_broadcast(128), [], ["lnbB"], CK)
        DMA("sp", ngdB[:], norm_g_d.partition_broadcast(128), [], ["ngdB"], CK)
        DMA("sp", bsB[:], b_s_b.partition_broadcast(128), [], ["bsB"], CK)
        DMA("sp", dtbB[:], dt_bias.partition_broadcast(128), [], ["dtbB"], CK)
        DMA("sp", AB[:], a_log.partition_broadcast(128), [], ["AB"], CK)
        DMA("sp", dskB[:], d_skip.partition_broadcast(128), [], ["dskB"], CK)
        ACT(AB[:], AB[:], AF.Exp, ["AB"], ["AB"])
        P.op("act", lambda e: e.mul(out=AB[:], in_=AB[:], mul=-1.0), reads=["AB"], writes=["AB"])
        wsfb = RB(24 * KB, F32, [4, 128])
        wsf = wsfb.ap
        DMA("sp", wsf, w_s_b.rearrange("g t s -> t g s"), [], ["wsf"] + wsfb.k(), CK)
        for g in range(4):
            ASEL(wsf[:, g, :], [[-1, 128]], ALU.is_ge, 0.0, 0, 1, ["wsf"], ["wsf"])
        b0 = bank()
        for g in range(4):
            TR(ps[b0][:, g * 128:(g + 1) * 128], wsf[:, g, :], identf[:], ["wsf", "identf"] + wsfb.k(), [("ps", b0)] + wsfb.k())
        CP("act", wsT[:], ps[b0][:, :].rearrange("p (g t) -> p g t", t=128), [("ps", b0)], ["wsT"])
        tbb = RB(28 * KB, F32, [768])
        tb = tbb.ap[0:8, :]
        DMA("sp", tb[:, 0:257], rel_bias, [], ["tb"] + tbb.k(), CK)
        CP("act", tb[:, 257:768], tb[:, 256:257].broadcast_to([8, 511]), ["tb"], ["tb2"])
        DMA("sp", tpad, tb, ["tb", "tb2"], ["tpadA", "tpadB"] + tbb.k(), "tpad")
        bm = RB(0, F32, [8, 640])
        MS("pool", bm.ap, -30000.0, [], bm.k())
        for p in range(128):
            a, j = p // 64, p % 64
            DMA("sp" if p % 2 == 0 else "pool", bm.ap[p:p + 1, :, 64 * a:64 * a + 576], tpad[:, 128 - j:128 - j + 576].unsqueeze(0),
                ["tpadA", "tpadB"] + bm.k(), [("bmrow", p)], "bm")
        P.total_keys.add("bm")
        CP("act", EB[:], bm.ap, [("bmrow", p) for p in range(128)] + bm.k(), ["EB"] + bm.k())

        for gi, (src, nkc, cols, grp) in enumerate(gran):
            dst = scr[gi][:, 0:nkc * cols].rearrange("p (k c) -> p k c", c=cols)
            DMA("pool", dst, src.rearrange("(k p) c -> p k c", p=128), [], [("scr", gi)], ("scr", grp))

        slot_ctr = [0]

        def wload(gi):
            s = slot_ctr[0] % NSLOT
            slot_ctr[0] += 1
            src, nkc, cols, grp = gran[gi]
            n = nkc * cols
            DMA("sp", slots[s][:, 0:n], scr[gi][:, 0:n], [("scr", gi)], [("slot", s)], ("slot", s))
            return slots[s][:, 0:n].rearrange("p (k c) -> p k c", c=cols), ("slot", s)

        def mixk(kc):
            return [("mixT", kc, 0), ("mixT", kc, 1)]

        def hk(kc, nb):
            return [("hT", kc, b) for b in range(nb)]

        def norm(nidx, nt, nb, final_out=None):
            pb = bank()
            for kc in range(8):
                s = sq[kc % 2]
                ACT(s[:, :nt], hT[:, kc, :nt], AF.Square, hk(kc, nb), [("sq", kc % 2)])
                MM(ps[pb][:, :nt], onesD[:], s[:, :nt], ["onesD", ("sq", kc % 2)], [("ps", pb)], start=(kc == 0), stop=(kc == 7))
            ACT(rstd[:, :nt], ps[pb][:, :nt], AF.Ln, [("ps", pb)], ["rstd"], bias=EPS, scale=1.0)
            ACT(rstd[:, :nt], rstd[:, :nt], AF.Exp, ["rstd"], ["rstd"], scale=-0.5)
            for kc in range(8):
                if final_out is None:
                    STT(hnT[:, kc, :nt], hT[:, kc, :nt], gcol[:, nidx, kc:kc + 1], rstd[:, :nt], ALU.mult, ALU.mult,
                        hk(kc, nb) + ["rstd"] + GC, [("hnT", kc)])
                else:
                    STT(hT[:, kc, :nt], hT[:, kc, :nt], gcol[:, nidx, kc:kc + 1], rstd[:, :nt], ALU.mult, ALU.mult,
                        hk(kc, nb) + ["rstd"] + GC, hk(kc, nb))

        HN = [("hnT", kc) for kc in range(8)]

        def proj_fm(slot, sk, mc, nt, rhsbuf=None, rk=None, nk=8):
            pb = bank()
            for kc in range(nk):
                if rhsbuf is None:
                    rhs, rkey = hnT[:, kc, :nt], [("hnT", kc)]
                else:
                    rhs, rkey = rhsbuf(kc), rk(kc)
                MM(ps[pb][:, :nt], slot[:, kc, mc * 128:(mc + 1) * 128], rhs, [sk] + rkey, [("ps", pb)], start=(kc == 0), stop=(kc == nk - 1))
            return pb

        def proj_tm(slot, sk, b, cols=512):
            pb = bank()
            for kc in range(8):
                MM(ps[pb][:, :cols], hnT[:, kc, b * 128:(b + 1) * 128], slot[:, kc, 0:cols], [sk, ("hnT", kc)], [("ps", pb)], start=(kc == 0), stop=(kc == 7))
            return pb

        def out_proj(gids, src_ap, src_keys, nt, nb, nk):
            for gi_i, gi in enumerate(gids):
                slot, sk = wload(gi)
                ncm = gran[gi][2] // 128
                for m in range(ncm):
                    mc = gi_i * ncm + m
                    pb = proj_fm(slot, sk, m, nt, rhsbuf=lambda kc: src_ap(kc, nt), rk=src_keys, nk=nk)
                    TT("dve", hT[:, mc, :nt], hT[:, mc, :nt], ps[pb][:, :nt], ALU.add, hk(mc, nb) + [("ps", pb)], hk(mc, nb))

        def ffn(l, nt, nb):
            norm(1 + 2 * l, nt, nb)
            for i in range(6):
                sg, skg = wload(G["gate%d" % l][i])
                su, sku = wload(G["up%d" % l][i])
                for m in range(gran[G["gate%d" % l][i]][2] // 128):
                    j = i * 4 + m
                    pg = proj_fm(sg, skg, m, nt)
                    pu = proj_fm(su, sku, m, nt)
                    ACT(sgb.ap[:, j % 2, :nt], ps[pg][:, :nt], AF.Silu, [("ps", pg)], sgb.kr(j % 2))
                    TT("dve", actT.ap[:, j, :nt], sgb.ap[:, j % 2, :nt], ps[pu][:, :nt], ALU.mult, sgb.kr(j % 2) + [("ps", pu)], actT.kr(j))
            out_proj(G["down%d" % l], lambda kc, nt: actT.ap[:, kc, :nt], lambda kc: actT.kr(kc), nt, nb, 22)

        small_ctr = [0]

        def sm(n):
            o = small_ctr[0] % (64 // 8) * 8
            small_ctr[0] += 1
            return small[:, o:o + n], ("small", o)

        def layer0(ti, nt, nb, first, last, sample, seq):
            norm(0, nt, nb)
            if first and not sample:
                MS("dve", uu.ap[:, :, 0:2], 0.0, [], uu.k())
            elif first and sample:
                for c in range(4):
                    DMA("sp", uu.ap[:, c, 0:2], sca[:, c * 128:(c + 1) * 128].rearrange("k p -> p k"), [], uu.kr(c, 0, 2), "haloA", slow=True)
            else:
                CP("dve", uu.ap[:, :, 0:2], haloA[:], ["haloA"], uu.k())
            for gi_i, nm in enumerate(["xa", "gb", "gc", "u"]):
                slot, sk = wload(G["in_ab"][gi_i])
                for mc in range(4):
                    pb = proj_fm(slot, sk, mc, nt)
                    if nm == "xa":
                        CP("act", uu.ap[:, mc, 2:2 + nt], ps[pb][:, :nt], [("ps", pb)], uu.kr(mc))
                    elif nm == "gb":
                        CP("act", gbuf.ap[:, mc, :nt], ps[pb][:, :nt], [("ps", pb)], gbuf.kr(mc))
                    elif nm == "gc":
                        TT("dve", uu.ap[:, mc, 2:2 + nt], uu.ap[:, mc, 2:2 + nt], ps[pb][:, :nt], ALU.mult, uu.kr(mc) + [("ps", pb)], uu.kr(mc))
                    else:
                        ACT(gu.ap[:, mc, :nt], ps[pb][:, :nt], AF.Gelu, [("ps", pb)], gu.kr(mc))
            nv = 64 if sample else nt
            CP("dve", haloA[:], uu.ap[:, :, nv:nv + 2], uu.k(), ["haloA"])
            for c in range(4):
                a = cacc.ap[:, c % 2, :nt]
                ak = cacc.kr(c % 2)
                TS("dve", a, uu.ap[:, c, 0:nt], cwa[:, c, 0:1], None, ALU.mult, None, uu.kr(c) + CW, ak)
                STT(a, uu.ap[:, c, 1:1 + nt], cwa[:, c, 1:2], a, ALU.mult, ALU.add, uu.kr(c) + CW + ak, ak)
                STT(a, uu.ap[:, c, 2:2 + nt], cwa[:, c, 2:3], a, ALU.mult, ALU.add, uu.kr(c) + CW + ak, ak)
                TT("dve", mixT[:, c, :nt], a, gbuf.ap[:, c, :nt], ALU.mult, ak + gbuf.kr(c), mixk(c))
            slot, sk = wload(G["in_ab"][4])
            for b in range(nb):
                pb = proj_tm(slot, sk, b)
                vs, vsk = sm(4)
                MS("dve", vs, 0.0, [], [vsk])
                ACT(vtok.ap[:, b, :], ps[pb][:, :], AF.Gelu, [("ps", pb), vsk], vtok.kr(b) + [vsk], accum_out=vs[:, 0:1])
                TS("dve", vs[:, 1:2], vs[:, 0:1], -1.0 / 512, None, ALU.mult, None, [vsk], [vsk])
                ACT(ftmp.ap[:, b % 2, :], vtok.ap[:, b, :], AF.Square, vtok.kr(b) + [vsk], ftmp.kr(b % 2) + [vsk], bias=vs[:, 1:2], scale=1.0, accum_out=vs[:, 2:3])
                ACT(vs[:, 3:4], vs[:, 2:3], AF.Sqrt, [vsk], [vsk], bias=EPS, scale=1.0 / 512)
                RCP(vs[:, 3:4], vs[:, 3:4], [vsk], [vsk])
                TS("dve", vtok.ap[:, b, :], vtok.ap[:, b, :], vs[:, 1:2], vs[:, 3:4], ALU.add, ALU.mult, vtok.kr(b) + [vsk], vtok.kr(b))
                TT("dve", vtok.ap[:, b, :], vtok.ap[:, b, :], lngB[:], ALU.mult, vtok.kr(b) + ["lngB"], vtok.kr(b))
                TT("dve", vtok.ap[:, b, :], vtok.ap[:, b, :], lnbB[:], ALU.add, vtok.kr(b) + ["lnbB"], vtok.kr(b))
                CP("act", vln.ap[:, b, :], vtok.ap[:, b, :], vtok.kr(b), vln.kr(b))
                if sample:
                    DMA("act", vbs, vtok.ap[0:64, 0, :], vtok.kr(0), [], "o_vbs")
            for g in range(4):
                pb = bank()
                for b in range(nb):
                    MM(ps[pb][:, b * 128:(b + 1) * 128], vln.ap[:, b, g * 128:(g + 1) * 128], wsT[:, g, :], vln.kr(b) + ["wsT"], [("ps", pb)])
                f = ftmp.ap[:, g % 2, :nt]
                TT("dve", f.rearrange("p (b t) -> p b t", t=128), ps[pb][:, :nt].rearrange("p (b t) -> p b t", t=128),
                   bsB[:, g * 128:(g + 1) * 128].unsqueeze(1).broadcast_to([128, nb, 128]), ALU.add, [("ps", pb), "bsB"], ftmp.kr(g % 2))
                TT("dve", mixT[:, 4 + g, :nt], f, gu.ap[:, g, :nt], ALU.mult, ftmp.kr(g % 2) + gu.kr(g), mixk(4 + g))
            if last or sample:
                CP("dve", oca[:], haloA[:], ["haloA"], ["oca"])
                dst = cas[0] if sample else cap[seq]
                for c in range(4):
                    DMA("sp", dst[:, c * 128:(c + 1) * 128].rearrange("k p -> p k"), oca[:, c, :], ["oca"], [], "o_oca", slow=True)
            out_proj(G["out_ab"], lambda kc, nt: mixT[:, kc, :nt], mixk, nt, nb, 8)

        def layer1(ti, nt, nb, first, last, sample, seq):
            import os
            cur, prv = ti % 2, (ti + 1) % 2
            norm(2, nt, nb)
            has_hist = (not first) or sample
            slot, sk = wload(G["in_cd"][0])
            for mc in range(4):
                pb = proj_fm(slot, sk, mc, nt)
                ACT(qT.ap[:, mc, :nt], ps[pb][:, :nt], AF.Copy, [("ps", pb)], qT.kr(mc), scale=0.125)
            slot, sk = wload(G["in_cd"][1])
            for mc in range(4):
                pb = proj_fm(slot, sk, mc, nt)
                CP("act", kT[cur][:, mc, :nt], ps[pb][:, :nt], [("ps", pb)], [("kT", cur, mc)])
            if last or sample:
                for b in range(nb):
                    pb = proj_tm(slot, sk, b)
                    kv = kvst[b % 2]
                    CP("act", kv[:], ps[pb][:, :], [("ps", pb)], [("kvst", b % 2)])
                    if sample:
                        DMA("act", kcs, kv[0:64, :], [("kvst", b % 2)], [], ("o_kv", b % 2))
                    else:
                        DMA("act", kcp[seq * 512 + b * 128: seq * 512 + (b + 1) * 128, :], kv[:], [("kvst", b % 2)], [], ("o_kv", b % 2))
            slot, sk = wload(G["in_cd"][2])
            for b in range(nb):
                pb = proj_tm(slot, sk, b)
                CP("dve", bass.AP(Vx[cur], b * 768, [[4 * 768, 128], [192, 4], [128, 2], [1, 64]]),
                   ps[pb][:, :].rearrange("p (r two c) -> p r two c", two=2, c=64), [("ps", pb)], [("Vx", cur, b)])
                if last or sample:
                    kv = kvst[b % 2]
                    CP("act", kv[:], ps[pb][:, :], [("ps", pb)], [("kvst", b % 2)])
                    if sample:
                        DMA("act", vcs, kv[0:64, :], [("kvst", b % 2)], [], ("o_kv", b % 2))
                    else:
                        DMA("act", vcp[seq * 512 + b * 128: seq * 512 + (b + 1) * 128, :], kv[:], [("kvst", b % 2)], [], ("o_kv", b % 2))
            slot, sk = wload(G["in_cd"][3])
            for b in range(nb):
                pb = proj_tm(slot, sk, b)
                ACT(zs.ap[:, b, :], ps[pb][:, :], AF.Silu, [("ps", pb)], zs.kr(b))
            if first and not sample:
                MS("dve", xbc.ap[:, :, 0:3], 0.0, [], xbc.k())
            elif first and sample:
                for c in range(8):
                    DMA("sp", scdf.ap[:, c, :], scd[:, c * 128:(c + 1) * 128].rearrange("k p -> p k"), [], [("scdf", c)], "haloD", slow=True)
                CP("dve", xbc.ap[:, :, 0:3], scdf.ap, [("scdf", c) for c in range(8)], xbc.k())
            else:
                CP("dve", xbc.ap[:, :, 0:3], haloD[:], ["haloD"], xbc.k())
            for gg in range(2):
                slot, sk = wload(G["in_cd"][4 + gg])
                for mc in range(4):
                    pb = proj_fm(slot, sk, mc, nt)
                    CP("act", xbc.ap[:, gg * 4 + mc, 3:3 + nt], ps[pb][:, :nt], [("ps", pb)], xbc.kr(gg * 4 + mc))
            nv = 64 if sample else nt
            CP("dve", haloD[:], xbc.ap[:, :, nv:nv + 3], xbc.k(), ["haloD"])
            if last or sample:
                CP("dve", ocd[:], xbc.ap[:, :, nv:nv + 3], xbc.k(), ["ocd"])
                dst = cds[0] if sample else cdp[seq]
                for c in range(8):
                    DMA("sp", dst[:, c * 128:(c + 1) * 128].rearrange("k p -> p k"), ocd[:, c, :], ["ocd"], [], "o_ocd", slow=True)
            if os.environ.get("KPRE"):
                for _i in range(4):
                    _pb = bank()
                    MM(ps[_pb][:, :512], hnT[:, 0, 0:128], hnT[:, 1, 0:512], [("hnT", 0), ("hnT", 1)], [("ps", _pb)])
            pd = bank()
            for b in range(nb):
                for kc in range(8):
                    MM(ps[pd][:, b * 128:(b + 1) * 128], hnT[:, kc, b * 128:(b + 1) * 128], wdt[:, kc, :], [("hnT", kc), "wdt"], [("ps", pd)], start=(kc == 0), stop=(kc == 7))
            if os.environ.get("KPOST") and ti == int(os.environ.get("KTILE", ti)):
                for _i in range(4):
                    _pb = bank()
                    MM(ps[_pb][:, :512], hnT[:, 0, 0:128], hnT[:, 1, 0:512], [("hnT", 0), ("hnT", 1)], [("ps", _pb)])
            nb8 = nb * 8
            xd = dts[:, 0, :nb8]; ax = dts[:, 1, :nb8]; dtv = dts[:, 2, :nb8]; dta = dts[:, 3, :nb8]
            DK = ["dts"]
            TT("dve", xd.rearrange("p (b h) -> p b h", h=8), ps[pd][:, :nb * 128].rearrange("p (b h) -> p b h", h=128)[:, :, 0:8],
               dtbB[:].unsqueeze(1).broadcast_to([128, nb, 8]), ALU.add, [("ps", pd), "dtbB"], DK)
            TS("dve", ax, xd, -1.0, None, ALU.mult, None, DK, DK)
            TT("dve", ax, ax, xd, ALU.max, DK, DK)
            ACT(ax, ax, AF.Exp, DK, DK, scale=-1.0)
            ACT(ax, ax, AF.Ln, DK, DK, bias=1.0, scale=1.0)
            TS("dve", dtv, xd, 0.0, None, ALU.max, None, DK, DK)
            TT("dve", dtv, dtv, ax, ALU.add, DK, DK)
            TT("dve", dta.rearrange("p (b h) -> p b h", h=8), dtv.rearrange("p (b h) -> p b h", h=8),
               AB[:].unsqueeze(1).broadcast_to([128, nb, 8]), ALU.mult, DK + ["AB"], DK)
            CP("dve", dtb[:, 0, :nb8], dta, DK, ["dtb"])
            TT("dve", dtb[:, 1, :nb8], dta, dtb[:, 0, :nb8], ALU.subtract, DK + ["dtb"], ["dtb"])
            import os
            st1 = int(os.environ.get('KSTOP1', '99'))
            if st1 < 1: return
            for c in range(8):
                a = cacd.ap[:, :nt]
                ak = cacd.k()
                TS("dve", a, xbc.ap[:, c, 0:nt], cwd[:, c, 0:1], None, ALU.mult, None, xbc.kr(c) + CW, ak)
                for k in range(1, 4):
                    STT(a, xbc.ap[:, c, k:k + nt], cwd[:, c, k:k + 1], a, ALU.mult, ALU.add, xbc.kr(c) + CW + ak, ak)
                ACT(xc.ap[:, c, :nt], a, AF.Silu, ak + CW, xc.kr(c), bias=cbd[:, c:c + 1], scale=1.0)

            if st1 < 2: return
            nqc = 1 if sample else 8
            att_state = {}

            def attA(h):
                pr, hh = h // 2, h % 2
                rows = slice(hh * 64, hh * 64 + 64)
                pt = PT.ap[:, h % 2, :]
                segs = {}
                off = 0
                for bb in range(-4, nb):
                    if bb < 0 and not has_hist:
                        continue
                    qlo, qhi = max(0, 2 * bb), min(nqc - 1, 2 * bb + 9)
                    if qhi < qlo:
                        continue
                    ncols = 128 if sample else (qhi - qlo + 1) * 64
                    kbuf, kkey, kcol = (kT[prv], ("kT", prv, pr), (4 + bb) * 128) if bb < 0 else (kT[cur], ("kT", cur, pr), bb * 128)
                    pb = bank()
                    for kh in range(2):
                        MM(ps[pb][kh * 64:(kh + 1) * 64, :ncols], kbuf[rows, pr, kcol + kh * 64:kcol + kh * 64 + 64],
                           qT.ap[rows, pr, qlo * 64:qlo * 64 + ncols], [kkey] + qT.kr(pr), [("ps", pb)], start=True, stop=False)
                    qrel = qlo - 2 * bb
                    assert qrel * 64 + ncols <= 640
                    MM(ps[pb][:, :ncols], identb[:], EB[:, h, qrel * 64:qrel * 64 + ncols], ["identb", "EB"], [("ps", pb)], start=False, stop=True)
                    ACT(pt[:, off:off + ncols], ps[pb][:, :ncols], AF.Exp, [("ps", pb)], PT.kr(h % 2))
                    segs[bb] = (off, qlo)
                    off += ncols
                att_state[h] = segs

            def attB(h):
                pr, hh = h // 2, h % 2
                rows = slice(hh * 64, hh * 64 + 64)
                pt = PT.ap[:, h % 2, :]
                segs = att_state[h]
                po = bank()
                for j in range(max(1, nqc // 2)):
                    bbs = [bb for bb in range(j - 4, j + 1) if bb in segs]
                    for i, bb in enumerate(bbs):
                        o, qlo = segs[bb]
                        vbuf, vkey = (Vx[prv], ("Vx", prv, 4 + bb)) if bb < 0 else (Vx[cur], ("Vx", cur, bb))
                        vb = (4 + bb) if bb < 0 else bb
                        MM(ps[po][:, j * 128:(j + 1) * 128], vbuf[:, vb, pr, hh * 64:hh * 64 + 128], pt[:, o + (2 * j - qlo) * 64:o + (2 * j - qlo) * 64 + 128],
                           [vkey] + PT.kr(h % 2), [("ps", po)], start=(i == 0), stop=(i == len(bbs) - 1))
                drows = slice(64, 128) if hh == 0 else slice(0, 64)
                ACT(rden.ap[rows, :nt], ps[po][drows, :nt], AF.Ln, [("ps", po)], rden.k())
                ACT(rden.ap[rows, :nt], rden.ap[rows, :nt], AF.Exp, rden.k(), rden.k(), scale=-1.0)
                TT("dve", mixT[rows, pr, :nt], ps[po][rows, :nt], rden.ap[rows, :nt], ALU.mult, [("ps", po)] + rden.k(), [("mixT", pr, hh)])

            if st1 < 3: return
            if first and not sample:
                MS("dve", H[:], 0.0, [], ["H"])
                MS("dve", Hbf[0][:], 0.0, [], [("Hbf", 0)])
            hb = [0]
            kssd = int(os.environ.get('KSSD', '99'))

            def ssd_block(b):
                tok = slice(b * 128, (b + 1) * 128)
                pbt = 4
                for c in range(4):
                    TR(psb[pbt][:, c * 128:(c + 1) * 128], xc.ap[:, c, tok], identb[:], xc.kr(c) + ["identb"], [("ps", pbt)])
                CP("act", xtok.ap, psb[pbt][:, 0:512], [("ps", pbt)], xtok.k())
                pbt = 4
                for g in range(2):
                    TR(psb[pbt][:, g * 128:(g + 1) * 128], xc.ap[:, 4 + g, tok], identb[:], xc.kr(4 + g) + ["identb"], [("ps", pbt)])
                CP("act", btok.ap, psb[pbt][:, 0:256], [("ps", pbt)], btok.k())
                yield
                dt_b = dts[:, 2, b * 8:(b + 1) * 8]
                DK = ["dts"]
                Lh3 = Lhi.ap.rearrange("p (h s) -> p h s", s=128)
                Ll3 = Llo.ap.rearrange("p (h s) -> p h s", s=128)
                TT("dve", Lh3, dtb[:, 0, b * 8:(b + 1) * 8].unsqueeze(2).broadcast_to([128, 8, 128]), SLblkb[:].unsqueeze(1).broadcast_to([128, 8, 128]),
                   ALU.mult, ["dtb", "SLblkb"], Lhi.k())
                TT("dve", Ll3, dtb[:, 1, b * 8:(b + 1) * 8].unsqueeze(2).broadcast_to([128, 8, 128]), SLblkb[:].unsqueeze(1).broadcast_to([128, 8, 128]),
                   ALU.mult, ["dtb", "SLblkb"], Llo.k())
                yield
                pbs = 5
                for h in range(8):
                    MM(ps[pbs][:, h * 64:(h + 1) * 64], Lh3[:, h, :], Ucb[:], Lhi.k() + ["Ucb"], [("ps", pbs)], start=True, stop=False)
                    MM(ps[pbs][:, h * 64:(h + 1) * 64], Ll3[:, h, :], Ucb[:], Llo.k() + ["Ucb"], [("ps", pbs)], start=False, stop=True)
                yield
                pbc = 6
                MM(ps[pbc][:, 256:264], Ublkb[:], dtb[:, 0, b * 8:(b + 1) * 8], ["dtb", "Ublkb"], [("ps", pbc)], start=True, stop=False)
                MM(ps[pbc][:, 256:264], Ublkb[:], dtb[:, 1, b * 8:(b + 1) * 8], ["dtb", "Ublkb"], [("ps", pbc)], start=False, stop=True)
                for ch in range(2):
                    MM(ps[pbc][:, 264 + ch * 8:272 + ch * 8], onesAB[:, ch, :], dtb[:, 0, b * 8:(b + 1) * 8], ["dtb", "onesAB"], [("ps", pbc)], start=True, stop=False)
                    MM(ps[pbc][:, 264 + ch * 8:272 + ch * 8], onesAB[:, ch, :], dtb[:, 1, b * 8:(b + 1) * 8], ["dtb", "onesAB"], [("ps", pbc)], start=False, stop=True)
                for g in range(2):
                    MM(ps[pbc][:, g * 128:(g + 1) * 128], xc.ap[:, 4 + g, tok], xc.ap[:, 6 + g, tok], xc.kr(4 + g) + xc.kr(6 + g), [("ps", pbc)])
                ACT(decT.ap, ps[pbs][:, :], AF.Exp, [("ps", pbs), ("ps", pbc)], decT.k())
                ACT(ecd[:, 0:24], ps[pbc][:, 256:280], AF.Exp, [("ps", pbc)], ["ecd"])
                TT("dve", cbm.ap, ps[pbc][:, 0:256].rearrange("p (g t) -> p g t", t=128),
                   Ublk[:].unsqueeze(1).broadcast_to([128, 2, 128]), ALU.mult, [("ps", pbc), "Ublk"], cbm.k())
                yield
                P.op("act", lambda e: e.memzero(MTb.ap), reads=[], writes=MTb.k())
                for ch in range(2):
                    rw = slice(ch * 64, ch * 64 + 64)
                    for g in range(2):
                        TT("dve", MTb.ap[rw, g * 4:(g + 1) * 4, ch * 64:(ch + 1) * 64],
                           decT.ap[rw, g * 256:(g + 1) * 256].rearrange("p (h t) -> p h t", t=64),
                           cbm.ap[rw, g, ch * 64:(ch + 1) * 64].unsqueeze(1).broadcast_to([64, 4, 64]), ALU.mult, decT.k() + cbm.k() + MTb.k(), MTb.k())
                dd, ddk = sm(8)
                TT("dve", dd, dt_b, decT.ap.rearrange("p (h t) -> p h t", t=64)[:, :, 63], ALU.mult, DK + decT.k(), [ddk])
                x3 = xtok.ap.rearrange("p (h q) -> p h q", q=64)
                TT("dve", xdt.ap.rearrange("p (h q) -> p h q", q=64), x3, dt_b.unsqueeze(2).broadcast_to([128, 8, 64]), ALU.mult, xtok.k() + DK, xdt.k())
                P.op("act", lambda e: e.memzero(xddz.ap), reads=[], writes=xddz.k())
                for ch in range(2):
                    rw = slice(ch * 64, ch * 64 + 64)
                    TT("dve", xddz.ap[rw, ch, :].rearrange("p (h q) -> p h q", q=64), x3[rw], dd[rw].unsqueeze(2).broadcast_to([64, 8, 64]), ALU.mult,
                       xtok.k() + [ddk] + xddz.k(), xddz.k())
                yield
                pby = 5
                for h in range(8):
                    MM(ps[pby][:, h * 64:(h + 1) * 64], MTb.ap[:, h, :], xdt.ap[:, h * 64:(h + 1) * 64], MTb.k() + xdt.k(), [("ps", pby)])
                yield
                pbi = [None, None]
                for ch in range(2):
                    hcur = hb[0] % 2
                    pbi[ch] = 6 if ch == 0 else 7
                    for g in range(2):
                        MM(ps[pbi[ch]][:, g * 256:(g + 1) * 256], xc.ap[:, 6 + g, tok], Hbf[hcur][:, g * 256:(g + 1) * 256], xc.kr(6 + g) + [("Hbf", hcur)], [("ps", pbi[ch])])
                    rw = slice(ch * 64, ch * 64 + 64)
                    TT("dve", t1.ap[rw].rearrange("p (h q) -> p h q", q=64), ps[pbi[ch]][rw, :].rearrange("p (h q) -> p h q", q=64),
                       ecd[rw, 0:8].unsqueeze(2).broadcast_to([64, 8, 64]), ALU.mult, [("ps", pbi[ch]), "ecd"] + t1.k(), t1.k())
                    if sample and ch == 1:
                        continue
                    pS = 4
                    for g in range(2):
                        MM(ps[pS][:, g * 256:(g + 1) * 256], btok.ap[:, g * 128:(g + 1) * 128], xddz.ap[:, ch, g * 256:(g + 1) * 256], btok.k() + xddz.k(), [("ps", pS)])
                    H3 = H[:].rearrange("p (h q) -> p h q", q=64)
                    TT("dve", H3, H3, ecd[:, 8 + ch * 8:16 + ch * 8].unsqueeze(2).broadcast_to([128, 8, 64]), ALU.mult, ["H", "ecd"], ["H"])
                    TT("dve", H[:], H[:], ps[pS][:, :], ALU.add, ["H", ("ps", pS)], ["H"])
                    hb[0] += 1
                    CP("act", Hbf[hb[0] % 2][:], H[:], ["H"], [("Hbf", hb[0] % 2)])
                yield
                TT("dve", t2.ap.rearrange("p (h q) -> p h q", q=64), x3, dskB[:].unsqueeze(2).broadcast_to([128, 8, 64]), ALU.mult, xtok.k() + ["dskB"], t2.k())
                TT("dve", t1.ap, t1.ap, t2.ap, ALU.add, t1.k() + t2.k(), t1.k())
                TT("dve", yb.ap, ps[pby][:, :], t1.ap, ALU.add, [("ps", pby), ("ps", pbi[1])] + t1.k(), yb.k())
                TT("dve", yb.ap, yb.ap, zs.ap[:, b, :], ALU.mult, yb.k() + zs.kr(b), yb.k())
                yield
                ss_, ssk = sm(4)
                MS("dve", ss_, 0.0, [], [ssk])
                for g in range(2):
                    ACT(t2.ap[:, g * 256:(g + 1) * 256], yb.ap[:, g * 256:(g + 1) * 256], AF.Square, yb.k() + [ssk], t2.k() + [ssk], accum_out=ss_[:, g:g + 1])
                ACT(ss_[:, 2:4], ss_[:, 0:2], AF.Sqrt, [ssk], [ssk], bias=EPS, scale=1.0 / 256)
                RCP(ss_[:, 2:4], ss_[:, 2:4], [ssk], [ssk])
                TT("dve", yb.ap.rearrange("p (g q) -> p g q", q=256), yb.ap.rearrange("p (g q) -> p g q", q=256),
                   ss_[:, 2:4].unsqueeze(2).broadcast_to([128, 2, 256]), ALU.mult, yb.k() + [ssk], yb.k())
                TT("dve", ynb.ap, yb.ap, ngdB[:], ALU.mult, yb.k() + ["ngdB"], ynb.k())
                pbt = 4
                for c in range(4):
                    TR(psb[pbt][:, c * 128:(c + 1) * 128], ynb.ap[:, c * 128:(c + 1) * 128], identb[:], ynb.k() + ["identb"], [("ps", pbt)])
                CP("act", mixT[:, 4:8, tok], psb[pbt][:, 0:512].rearrange("p (c t) -> p c t", t=128), [("ps", pbt)], [k_ for c_ in range(4, 8) for k_ in mixk(c_)])
            att_units = [("A", 0), ("A", 1)]
            for h in range(8):
                att_units.append(("B", h))
                if h + 2 < 8:
                    att_units.append(("A", h + 2))

            def ssd_all():
                for b_ in range(nb):
                    yield from ssd_block(b_)
                    yield

            ring[:] = [0, 1, 2, 3]
            gen = ssd_all()
            ssd_done = False
            ai = 0
            while ai < len(att_units) or not ssd_done:
                for _ in range(2):
                    if not ssd_done:
                        try:
                            next(gen)
                        except StopIteration:
                            ssd_done = True
                if ai < len(att_units):
                    kind, i = att_units[ai]
                    ai += 1
                    (attA if kind == "A" else attB)(i)
            ring[:] = list(range(8))
            if st1 < 4: return
            if last or sample:
                pbt = bank()
                for c in range(4):
                    TR(ps[pbt][:, c * 128:(c + 1) * 128], H[:, c * 128:(c + 1) * 128], identf[:], ["H", "identf"], [("ps", pbt)])
                CP("act", ost[:], ps[pbt][:, :].rearrange("p (c n) -> p c n", n=128), [("ps", pbt)], ["ost"])
                dst = sss if sample else ssp[seq * 512:(seq + 1) * 512, :]
                DMA("act", dst.rearrange("(c p) n -> p c n", p=128), ost[:], ["ost"], [], "o_ost")
            out_proj(G["out_cd"], lambda kc, nt: mixT[:, kc, :nt], mixk, nt, nb, 8)

        dts = T("dts", [128, 4, 32])
        ecd = T("ecd", [128, 24])

        def load_x(src_rows, nb, sample):
            for b in range(nb):
                xs_ = xin[b % 2]
                if sample:
                    MS("dve", xs_[:], 0.0, [], [("xin", b % 2)])
                    DMA("sp", xs_[0:64, :], src_rows, [("xin", b % 2)], [("xin", b % 2)], ("xin", b % 2))
                else:
                    DMA("sp", xs_[:], src_rows[b * 128:(b + 1) * 128, :], [], [("xin", b % 2)], ("xin", b % 2))
                for half in range(2):
                    pb = bank()
                    for j in range(4):
                        kc = half * 4 + j
                        TR(ps[pb][:, j * 128:(j + 1) * 128], xs_[:, kc * 128:(kc + 1) * 128], identf[:], [("xin", b % 2), "identf"], [("ps", pb)])
                    CP("act", hT[:, half * 4:half * 4 + 4, b * 128:(b + 1) * 128], ps[pb][:, :].rearrange("p (j t) -> p j t", t=128),
                       [("ps", pb)], [("hT", half * 4 + j, b) for j in range(4)])

        def store_y(dst_rows, nt, nb, sample):
            norm(4, nt, nb, final_out=True)
            for b in range(nb):
                y_ = yst[b % 2]
                for half in range(2):
                    pb = bank()
                    for j in range(4):
                        kc = half * 4 + j
                        TR(ps[pb][:, j * 128:(j + 1) * 128], hT[:, kc, b * 128:(b + 1) * 128], identf[:], hk(kc, nb) + ["identf"], [("ps", pb)])
                    CP("act", y_[:, half * 512:(half + 1) * 512], ps[pb][:, :], [("ps", pb)], [("yst", b % 2, half)])
                yk = [("yst", b % 2, 0), ("yst", b % 2, 1)]
                if sample:
                    DMA("act", dst_rows, y_[0:64, :], yk, [], ("o_y", b % 2))
                else:
                    DMA("act", dst_rows[b * 128:(b + 1) * 128, :], y_[:], yk, [], ("o_y", b % 2))

        def tile(ti, x_rows, y_rows, nt, nb, first, last, sample, seq):
            import os
            stop = int(os.environ.get("KSTOP", "99"))
            if stop < 1: return
            load_x(x_rows, nb, sample)
            if stop < 2: return
            layer0(ti, nt, nb, first, last, sample, seq)
            if stop < 3: return
            ffn(0, nt, nb)
            if stop < 4: return
            layer1(ti, nt, nb, first, last, sample, seq)
            if stop < 5: return
            ffn(1, nt, nb)
            if stop < 6: return
            store_y(y_rows, nt, nb, sample)

        gt = 0
        try:
          for s in range(NPS):
              for t in range(NTILE):
                  r0 = s * SEQ + t * 512
                  tile(gt, xp[r0:r0 + 512, :], yp[r0:r0 + 512, :], 512, 4, t == 0, t == NTILE - 1, False, s)
                  gt += 1
          cur, prv = gt % 2, (gt + 1) % 2
          DMA("sp", ckf.ap, ck.rearrange("(b p) c -> p b c", p=128), [], ckf.k(), "smp0")
          CP("dve", ckb.ap, ckf.ap, ckf.k(), ckb.k())
          for b in range(4):
              pbt = bank()
              for pr in range(4):
                  TR(psb[pbt][:, pr * 128:(pr + 1) * 128], ckb.ap[:, b, pr * 128:(pr + 1) * 128], identb[:], ckb.k() + ["identb"], [("ps", pbt)])
              CP("act", kT[prv][:, :, b * 128:(b + 1) * 128], psb[pbt][:, 0:512].rearrange("p (r t) -> p r t", t=128), [("ps", pbt)], [("kT", prv, pr) for pr in range(4)])
          DMA("sp", cvf.ap, cv.rearrange("(b p) c -> p b c", p=128), [], cvf.k(), "smp1")
          for b in range(4):
              CP("dve", bass.AP(Vx[prv], b * 768, [[4 * 768, 128], [192, 4], [128, 2], [1, 64]]),
                 cvf.ap[:, b, :].rearrange("p (r two c) -> p r two c", two=2, c=64), cvf.k(), [("Vx", prv, b)])
          DMA("sp", ssf.ap, ssm.rearrange("(c p) n -> p c n", p=128), [], ssf.k(), "smp2")
          pbt = bank()
          for c in range(4):
              TR(ps[pbt][:, c * 128:(c + 1) * 128], ssf.ap[:, c, :], identf[:], ssf.k() + ["identf"], [("ps", pbt)])
          CP("act", H[:], ps[pbt][:, :], [("ps", pbt)], ["H"])
          CP("act", Hbf[0][:], H[:], ["H"], [("Hbf", 0)])
          tile(gt, xs, ys, 128, 1, True, True, True, 0)


        except StopIteration:
            pass
        print('n_ops', len(P.ops))
        P.emit(st)
    return nc


_CACHE = {}


def _get_nc(NPS, SEQ):
    key = (NPS, SEQ)
    if key not in _CACHE:
        _CACHE[key] = build(NPS, SEQ)
    return _CACHE[key]


def run_cores(inputs, n_cores, NPS, SEQ):
    f = lambda a: np.ascontiguousarray(np.asarray(a, dtype=np.float32))
    I = {k: f(v) for k, v in inputs.items()}
    nc = _get_nc(NPS, SEQ)
    in_maps = []
    for c in range(n_cores):
        m = {
            "xp": I["x_prompt"][c * NPS:(c + 1) * NPS].reshape(NPS * SEQ, D),
            "xs": I["x_sample"][c],
            "ck": I["cache_k_c"][0, c].reshape(512, 512), "cv": I["cache_v_c"][0, c].reshape(512, 512),
            "sca": I["state_conv_a"][0, c], "scd": I["state_conv_d"][0, c], "ssm": I["state_ssm_d"][0, c].reshape(512, 128),
            "norm_mix": I["norm_mix"], "norm_ffn": I["norm_ffn"], "norm_final": I["norm_final"].reshape(1, D),
            "w_in_ab": I["w_in_ab"][0], "conv_w_a": I["conv_w_a"][0], "ln_g_b": I["ln_g_b"][0], "ln_b_b": I["ln_b_b"][0],
            "w_s_b": I["w_s_b"][0], "b_s_b": I["b_s_b"][0].reshape(512), "w_out_ab": I["w_out_ab"][0], "w_in_cd": I["w_in_cd"][0],
            "rel_bias_c": I["rel_bias_c"][0], "conv_w_d": I["conv_w_d"][0], "conv_b_d": I["conv_b_d"][0],
            "dt_bias_d": I["dt_bias_d"][0], "a_log_d": I["a_log_d"][0], "d_skip_d": I["d_skip_d"][0], "norm_g_d": I["norm_g_d"][0],
            "w_out_cd": I["w_out_cd"][0], "w_gate": I["w_gate"], "w_up": I["w_up"], "w_down": I["w_down"],
        }
        in_maps.append({k: np.ascontiguousarray(v) for k, v in m.items()})
    res = run_bass_kernel_spmd(nc, in_maps, core_ids=list(range(n_cores)))
    R_ = res.results
    cat = lambda k: np.concatenate([r[k] for r in R_], axis=0)
    B = n_cores * NPS
    outs = (
        cat("yp").reshape(B, SEQ, D),
        cat("ys").reshape(n_cores, 64, D),
        cat("cap").reshape(1, B, 2, 512),
        cat("cas").reshape(1, n_cores, 2, 512),
        cat("vbs").reshape(1, n_cores, 64, 512),
        cat("kcp").reshape(1, B, 512, 8, 64),
        cat("vcp").reshape(1, B, 512, 8, 64),
        cat("kcs").reshape(1, n_cores, 64, 8, 64),
        cat("vcs").reshape(1, n_cores, 64, 8, 64),
        cat("cdp").reshape(1, B, 3, 1024),
        cat("cds").reshape(1, n_cores, 3, 1024),
        cat("ssp").reshape(1, B, 8, 64, 128),
        cat("sss").reshape(1, n_cores, 8, 64, 128),
    )
    return tuple(np.ascontiguousarray(o, dtype=np.float32) for o in outs)


def kernel(**inputs):
    return run_cores(inputs, 8, 2, 4096)
```

```python
import bisect
from contextlib import ExitStack
import numpy as np
import concourse.bass as bass
import concourse.mybir as mybir
from concourse.bass_utils import run_bass_kernel_spmd

F32 = mybir.dt.float32
BF16 = mybir.dt.bfloat16
AF = mybir.ActivationFunctionType
ALU = mybir.AluOpType

D = 1024
FF = 2816
EPS = 1e-5
NSLOT = 4
SLOT_E = 4096


class _Op:
    __slots__ = ("eng", "fn", "deps", "dma", "semkey", "sig", "need_sig", "idx")


class Prog:
    ENGS = ("pe", "act", "dve", "pool", "sp")

    def __init__(self, nc):
        self.nc = nc
        self.ops = []
        self.last_w = {}
        self.readers = {}
        self.total_keys = set()
        self.sig_idx = {}

    def op(self, eng, fn, reads=(), writes=(), dma=False, semkey=None):
        o = _Op()
        o.eng, o.fn, o.dma, o.semkey = eng, fn, dma, ((semkey, eng) if dma else None)
        o.sig = None
        o.need_sig = dma
        o.idx = len(self.ops)
        psr = [b for b in reads if isinstance(b, tuple) and b[0] == "ps"]
        if psr:
            writes = list(writes) + [b for b in psr if b not in writes]
        deps = {}
        for b in reads:
            w = self.last_w.get(b)
            if w is not None:
                deps[w.idx] = w
        for b in writes:
            w = self.last_w.get(b)
            if w is not None:
                deps[w.idx] = w
            for r in self.readers.get(b, ()):
                if r.eng != eng or r.dma:
                    deps[r.idx] = r
        pruned = {}
        for di in sorted(deps, reverse=True):
            d = deps[di]
            if d.eng == eng and not d.dma and not dma and eng == "pe":
                continue
            if not d.dma:
                lst = self.sig_idx.setdefault(d.eng, [])
                j = bisect.bisect_left(lst, d.idx)
                if j < len(lst):
                    d = self.ops[lst[j]]
                else:
                    lst.append(d.idx)
                    d.need_sig = True
            else:
                d.need_sig = True
            pruned[d.idx] = d
        o.deps = list(pruned.values())
        for b in reads:
            self.readers.setdefault(b, []).append(o)
        for b in writes:
            self.last_w[b] = o
            self.readers[b] = []
        self.ops.append(o)
        return o

    def emit(self, stack, final_wait_eng="sp"):
        nc = self.nc
        eng_sem = {e: stack.enter_context(nc.semaphore("s_" + e)) for e in self.ENGS}
        dma_sem, cnt = {}, {}
        for o in self.ops:
            if o.dma:
                k = o.semkey
                if k not in dma_sem:
                    dma_sem[k] = stack.enter_context(nc.semaphore("d_%d" % len(dma_sem)))
                cnt[k] = cnt.get(k, 0) + 16
                o.sig = (dma_sem[k], cnt[k])
            elif o.need_sig:
                k = ("e", o.eng)
                cnt[k] = cnt.get(k, 0) + 1
                o.sig = (eng_sem[o.eng], cnt[k])
        for o in self.ops:
            if o.dma and o.semkey[0] in self.total_keys:
                o.sig = (dma_sem[o.semkey], cnt[o.semkey])
        self.n_sems = len(dma_sem) + len(eng_sem)
        print('sems', self.n_sems, {k: v for k, v in cnt.items() if isinstance(k, tuple) and k[0] == 'e'}, 'maxdma', max(v for k, v in cnt.items()))
        final = [(dma_sem[k], cnt[k]) for k in dma_sem]
        per_eng = {e: [o for o in self.ops if o.eng == e] for e in self.ENGS}
        block = stack.enter_context(nc.Block())

        def run(e, h):
            waited = {}
            for o in per_eng[e]:
                need = {}
                for d in o.deps:
                    s, v = d.sig
                    if need.get(id(s), (None, 0))[1] < v:
                        need[id(s)] = (s, v)
                for s, v in need.values():
                    if waited.get(id(s), 0) >= v:
                        continue
                    waited[id(s)] = v
                    h.wait_ge(s, v)
                ins = o.fn(h)
                if o.sig is not None:
                    ins.then_inc(o.sig[0], 16 if o.dma else 1)
            if e == final_wait_eng:
                for s, v in final:
                    if waited.get(id(s), 0) < v:
                        h.wait_ge(s, v)

        block.tensor(lambda h: run("pe", h))
        block.scalar(lambda h: run("act", h))
        block.vector(lambda h: run("dve", h))
        block.gpsimd(lambda h: run("pool", h))
        block.sync(lambda h: run("sp", h))


def build(NPS, SEQ):
    NTILE = SEQ // 512
    nc = bass.Bass("TRN2", target_bir_lowering=False)

    def din(name, shape):
        return nc.dram_tensor(name, list(shape), F32, kind="ExternalInput").ap()

    def dout(name, shape):
        return nc.dram_tensor(name, list(shape), F32, kind="ExternalOutput").ap()

    xp = din("xp", [NPS * SEQ, D]); xs = din("xs", [64, D])
    ck = din("ck", [512, 512]); cv = din("cv", [512, 512])
    sca = din("sca", [2, 512]); scd = din("scd", [3, 1024]); ssm = din("ssm", [512, 128])
    norm_mix = din("norm_mix", [2, D]); norm_ffn = din("norm_ffn", [2, D]); norm_final = din("norm_final", [1, D])
    w_in_ab = din("w_in_ab", [D, 2560]); conv_w_a = din("conv_w_a", [3, 512])
    ln_g_b = din("ln_g_b", [512]); ln_b_b = din("ln_b_b", [512])
    w_s_b = din("w_s_b", [4, 128, 128]); b_s_b = din("b_s_b", [512])
    w_out_ab = din("w_out_ab", [D, D]); w_in_cd = din("w_in_cd", [D, 3080])
    rel_bias = din("rel_bias_c", [8, 257]); conv_w_d = din("conv_w_d", [4, 1024]); conv_b_d = din("conv_b_d", [1024])
    dt_bias = din("dt_bias_d", [8]); a_log = din("a_log_d", [8]); d_skip = din("d_skip_d", [8])
    norm_g_d = din("norm_g_d", [512]); w_out_cd = din("w_out_cd", [D, D])
    w_gate = din("w_gate", [2, D, FF]); w_up = din("w_up", [2, D, FF]); w_down = din("w_down", [2, FF, D])

    yp = dout("yp", [NPS * SEQ, D]); ys = dout("ys", [64, D])
    cap = dout("cap", [NPS, 2, 512]); cas = dout("cas", [1, 2, 512]); vbs = dout("vbs", [64, 512])
    kcp = dout("kcp", [NPS * 512, 512]); vcp = dout("vcp", [NPS * 512, 512])
    kcs = dout("kcs", [64, 512]); vcs = dout("vcs", [64, 512])
    cdp = dout("cdp", [NPS, 3, 1024]); cds = dout("cds", [1, 3, 1024])
    ssp = dout("ssp", [NPS * 512, 128]); sss = dout("sss", [512, 128])

    gran = []

    def add_gran(w2d, K, c0, cols, grp):
        gran.append((w2d[:, c0:c0 + cols], K // 128, cols, grp))
        return len(gran) - 1

    G = {}
    for l in range(2):
        if l == 0:
            G["in_ab"] = [add_gran(w_in_ab, D, i * 512, 512, "in_ab") for i in range(5)]
            G["out_ab"] = [add_gran(w_out_ab, D, i * 512, 512, "out_ab") for i in range(2)]
        else:
            G["in_cd"] = [add_gran(w_in_cd, D, i * 512, 512, "in_cd") for i in range(6)]
            G["out_cd"] = [add_gran(w_out_cd, D, i * 512, 512, "out_cd") for i in range(2)]
        G["gate%d" % l] = [add_gran(w_gate[l], D, i * 512, min(512, FF - i * 512), "gate%d" % l) for i in range(6)]
        G["up%d" % l] = [add_gran(w_up[l], D, i * 512, min(512, FF - i * 512), "up%d" % l) for i in range(6)]
        G["down%d" % l] = [add_gran(w_down[l], FF, i * 128, 128, "down%d" % l) for i in range(8)]
    scr = nc.dram_tensor("wscr", [len(gran), 128, SLOT_E], BF16, kind="Internal").ap()
    tpad = nc.dram_tensor("tpad", [8, 768], F32, kind="Internal").ap()

    st = ExitStack()
    with st:
        P = Prog(nc)
        P.total_keys.add("const")
        P.total_keys.add("haloA")
        P.total_keys.add("haloD")
        for g in gran:
            P.total_keys.add(("scr", g[3]))

        def T(name, shape, dt=F32):
            return st.enter_context(nc.sbuf_tensor(name, list(shape), dt))

        hT = T("hT", [128, 8, 512]); hnT = T("hnT", [128, 8, 512], BF16); mixT = T("mixT", [128, 8, 512], BF16)
        slots = [T("slot%d" % i, [128, SLOT_E], BF16) for i in range(NSLOT)]
        xin = [T("xin%d" % i, [128, D]) for i in range(2)]
        yst = [T("yst%d" % i, [128, D]) for i in range(2)]
        sq = [T("sq%d" % i, [128, 512], BF16) for i in range(2)]
        rstd = T("rstd", [128, 512])
        identb = T("identb", [128, 128], BF16); identf = T("identf", [128, 128])
        onesD = T("onesD", [128, 128], BF16); onesf = T("onesf", [128, 128])
        gcol = T("gcol", [128, 5, 8])
        cwa = T("cwa", [128, 4, 3]); cwd = T("cwd", [128, 8, 4]); cbd = T("cbd", [128, 8])
        lngB = T("lngB", [128, 512]); lnbB = T("lnbB", [128, 512]); ngdB = T("ngdB", [128, 512])
        wsT = T("wsT", [128, 4, 128], BF16); bsB = T("bsB", [128, 512])
        EB = T("EB", [128, 8, 640], BF16)
        Ublk = T("Ublk", [128, 128]); SLblk = T("SLblk", [128, 128]); Uc = T("Uc", [128, 64]); SLc = T("SLc", [128, 64])
        dtbB = T("dtbB", [128, 8]); AB = T("AB", [128, 8]); dskB = T("dskB", [128, 8])
        wdt = T("wdt", [128, 8, 128], BF16)
        haloA = T("haloA", [128, 4, 2]); haloD = T("haloD", [128, 8, 3], BF16)
        kT = [T("kT%d" % i, [128, 4, 512], BF16) for i in range(2)]
        Vx = [T("Vx%d" % i, [128, 4, 4, 192], BF16) for i in range(2)]
        H = T("H", [128, 512]); Hbf = [T("Hbf%d" % i, [128, 512], BF16) for i in range(2)]
        small = T("small", [128, 64])
        oca = T("oca", [128, 4, 2]); ocd = T("ocd", [128, 8, 3]); ost = T("ost", [128, 4, 128])
        kvst = [T("kvst%d" % i, [128, 512]) for i in range(2)]
        RW = 16128
        R = T("R", [128, RW])
        R16 = R.bitcast(BF16)
        ps = [st.enter_context(nc.psum_tensor("ps%d" % i, [128, 512], F32)) for i in range(8)]
        psb = [p.bitcast(BF16) for p in ps]

        class RB:
            def __init__(self, off_b, dt, shape):
                self.esz = 2 if dt == BF16 else 4
                self.off = off_b
                n = int(np.prod(shape))
                base = R16 if dt == BF16 else R
                e0 = off_b // self.esz
                v = base[:, e0:e0 + n]
                if len(shape) == 2:
                    v = v.rearrange("p (a b) -> p a b", b=shape[1])
                elif len(shape) == 3:
                    v = v.rearrange("p (a b c) -> p a b c", b=shape[1], c=shape[2])
                self.ap = v
                self.shape = shape
                self.n = n
                assert off_b + n * self.esz <= RW * 4, (off_b, n)

            def k(self, lo=0, hi=None):
                hi = self.n if hi is None else hi
                b0 = (self.off + lo * self.esz) // 1024
                b1 = (self.off + hi * self.esz - 1) // 1024
                return [("R", u) for u in range(b0, b1 + 1)]

            def kr(self, a, lo=0, hi=None):
                w = int(np.prod(self.shape[1:]))
                hi = w if hi is None else hi
                return self.k(a * w + lo, a * w + hi)

        KB = 1024
        uu = RB(0, F32, [4, 516]); gbuf = RB(9 * KB, BF16, [4, 512]); gu = RB(13 * KB, BF16, [4, 512])
        vtok = RB(17 * KB, F32, [4, 512]); vln = RB(25 * KB, BF16, [4, 512]); ftmp = RB(29 * KB, F32, [2, 512])
        cacc = RB(33 * KB, F32, [2, 512])
        actT = RB(0, BF16, [22, 512]); sgb = RB(22 * KB, BF16, [2, 512])
        qT = RB(0, BF16, [4, 512]); PT = RB(4 * KB, BF16, [2, 2560]); zs = RB(14 * KB, BF16, [4, 512])
        xbc = RB(18 * KB, BF16, [8, 516]); xc = RB(27 * KB, BF16, [8, 512]); tmpE = RB(35 * KB, F32, [2, 512])
        xtok = RB(39 * KB, BF16, [512]); btok = RB(40 * KB, BF16, [256]); Lhi = RB(41 * KB, BF16, [1024])
        decT = RB(43 * KB, F32, [512]); cbm = RB(45 * KB, F32, [2, 128]); MTb = RB(46 * KB, BF16, [8, 128])
        xdt = RB(48 * KB, BF16, [512]); xddz = RB(49 * KB, BF16, [2, 512]); t1 = RB(51 * KB, F32, [512])
        t2 = RB(53 * KB, F32, [512]); yb = RB(55 * KB, F32, [512]); ynb = RB(57 * KB, BF16, [512])
        rden = RB(58 * KB, F32, [512]); Llo = RB(61 * KB, BF16, [1024]); cacd = RB(61 * KB, F32, [512])
        ckf = RB(0, F32, [4, 512]); ckb = RB(8 * KB, BF16, [4, 512]); cvf = RB(12 * KB, F32, [4, 512])
        ssf = RB(20 * KB, F32, [4, 128]); scdf = RB(60 * KB, F32, [8, 3])

        bank_ctr = [0]

        ring = list(range(8))

        def bank():
            b = ring[bank_ctr[0] % len(ring)]
            bank_ctr[0] += 1
            return b

        def MM(out, lhsT, rhs, r, w, start=True, stop=True):
            P.op("pe", lambda e: e.matmul(out, lhsT=lhsT, rhs=rhs, start=start, stop=stop), reads=r, writes=w)

        def TR(out, in_, ident, r, w):
            P.op("pe", lambda e: e.transpose(out, in_, ident), reads=r, writes=w)

        def ACT(out, in_, func, r, w, **kw):
            P.op("act", lambda e: e.activation(out=out, in_=in_, func=func, **kw), reads=r, writes=w)

        def TT(eng, out, in0, in1, op, r, w):
            P.op(eng, lambda e: e.tensor_tensor(out=out, in0=in0, in1=in1, op=op), reads=r, writes=w)

        def TS(eng, out, in0, s1, s2, op0, op1, r, w):
            if s2 is None:
                P.op(eng, lambda e: e.tensor_scalar(out=out, in0=in0, scalar1=s1, scalar2=None, op0=op0), reads=r, writes=w)
            else:
                P.op(eng, lambda e: e.tensor_scalar(out=out, in0=in0, scalar1=s1, scalar2=s2, op0=op0, op1=op1), reads=r, writes=w)

        def STT(out, in0, sc, in1, op0, op1, r, w):
            P.op("dve", lambda e: e.scalar_tensor_tensor(out=out, in0=in0, scalar=sc, in1=in1, op0=op0, op1=op1), reads=r, writes=w)

        def CP(eng, out, in_, r, w):
            if eng == "act":
                P.op("act", lambda e: e.copy(out=out, in_=in_), reads=r, writes=w)
            else:
                P.op(eng, lambda e: e.tensor_copy(out=out, in_=in_), reads=r, writes=w)

        def RCP(out, in_, r, w):
            P.op("dve", lambda e: e.reciprocal(out=out, in_=in_), reads=r, writes=w)

        def MS(eng, ap, val, r, w):
            P.op(eng, lambda e: e.memset(ap, val), reads=r, writes=w)

        def DMA(eng, out, in_, r, w, semkey, slow=False):
            def f(e):
                if slow:
                    with nc.allow_non_contiguous_dma(reason="tiny strided"):
                        return e.dma_start(out=out, in_=in_)
                return e.dma_start(out=out, in_=in_)
            P.op(eng, f, reads=r, writes=w, dma=True, semkey=semkey)

        def ASEL(out, pattern, cmp, fill, base, cm, r, w):
            P.op("pool", lambda e: e.affine_select(out=out, in_=out, pattern=pattern, compare_op=cmp, fill=fill, base=base, channel_multiplier=cm), reads=r, writes=w)

        CK = "const"
        MS("pool", wdt[:], 0.0, [], ["wdt"])
        DMA("pool", wdt[:, :, 0:8], w_in_cd[:, 3072:3080].rearrange("(k p) c -> p k c", p=128), ["wdt"], ["wdt"], "const", slow=True)
        MS("pool", identf[:], 0.0, [], ["identf"])
        ASEL(identf[:], [[-1, 128]], ALU.not_equal, 1.0, 0, 1, ["identf"], ["identf"])
        CP("pool", identb[:], identf[:], ["identf"], ["identb"])
        MS("pool", onesD[:], 1.0 / D, [], ["onesD"])
        MS("pool", onesf[:], 1.0, [], ["onesf"])
        MS("pool", Ublk[:], 1.0, [], ["Ublk"])
        ASEL(Ublk[:], [[1, 128]], ALU.is_ge, 0.0, 0, -1, ["Ublk"], ["Ublk"])
        MS("pool", Ublk[0:64, 64:128], 0.0, ["Ublk"], ["Ublk"])
        MS("pool", SLblk[:], 1.0, [], ["SLblk"])
        ASEL(SLblk[:], [[-1, 128]], ALU.is_gt, 0.0, 0, 1, ["SLblk"], ["SLblk"])
        MS("pool", SLblk[64:128, 0:64], 0.0, ["SLblk"], ["SLblk"])
        CP("pool", Uc[0:64, :], Ublk[0:64, 0:64], ["Ublk"], ["Uc"])
        CP("pool", Uc[64:128, :], Ublk[64:128, 64:128], ["Ublk", "Uc"], ["Uc"])
        CP("pool", SLc[0:64, :], SLblk[0:64, 0:64], ["SLblk"], ["SLc"])
        CP("pool", SLc[64:128, :], SLblk[64:128, 64:128], ["SLblk", "SLc"], ["SLc"])
        Ucb = T("Ucb", [128, 64], BF16); Ublkb = T("Ublkb", [128, 128], BF16); onesb = T("onesb", [128, 128], BF16)
        dtb = T("dtb", [128, 2, 32], BF16)
        SLblkb = T("SLblkb", [128, 128], BF16); onesAB = T("onesAB", [128, 2, 128], BF16)
        CP("pool", SLblkb[:], SLblk[:], ["SLblk"], ["SLblkb"])
        MS("pool", onesAB[:], 0.0, [], ["onesAB"])
        MS("pool", onesAB[0:64, 0, :], 1.0, ["onesAB"], ["onesAB"])
        MS("pool", onesAB[64:128, 1, :], 1.0, ["onesAB"], ["onesAB"])
        CP("pool", Ucb[:], Uc[:], ["Uc"], ["Ucb"])
        CP("pool", Ublkb[:], Ublk[:], ["Ublk"], ["Ublkb"])
        MS("pool", onesb[:], 1.0, [], ["onesb"])
        for i in range(2):
            MS("pool", Vx[i][:], 1.0, [], [("Vx", i, b) for b in range(4)])
        for n, src in enumerate([norm_mix[0:1], norm_ffn[0:1], norm_mix[1:2], norm_ffn[1:2], norm_final]):
            DMA("sp", gcol[:, n, :], src[0].rearrange("(k p) -> p k", p=128), [], [("gcol", n)], CK, slow=True)
        GC = [("gcol", n) for n in range(5)]
        for c in range(4):
            DMA("sp", cwa[:, c, :], conv_w_a[:, c * 128:(c + 1) * 128].rearrange("k p -> p k"), [], [("cwa", c)], CK, slow=True)
        for c in range(8):
            DMA("sp", cwd[:, c, :], conv_w_d[:, c * 128:(c + 1) * 128].rearrange("k p -> p k"), [], [("cwd", c)], CK, slow=True)
        DMA("sp", cbd[:], conv_b_d.rearrange("(k p) -> p k", p=128), [], ["cbd"], CK, slow=True)
        CW = [("cwa", c) for c in range(4)] + [("cwd", c) for c in range(8)] + ["cbd"]
        DMA("sp", lngB[:], ln_g_b.partition_broadcast(128), [], ["lngB"], CK)
        DMA("sp", lnbB[:], ln_b_b.partition_broadcast(128), [], ["lnbB"], CK)
        DMA("sp", ngdB[:], norm_g_d.partition_broadcast(128), [], ["ngdB"], CK)
        DMA("sp", bsB[:], b_s_b.partition_broadcast(128), [], ["bsB"], CK)
        DMA("sp", dtbB[:], dt_bias.partition_broadcast(128), [], ["dtbB"], CK)
        DMA("sp", AB[:], a_log.partition_broadcast(128), [], ["AB"], CK)
        DMA("sp", dskB[:], d_skip.partition_broadcast(128), [], ["dskB"], CK)
        ACT(AB[:], AB[:], AF.Exp, ["AB"], ["AB"])
        P.op("act", lambda e: e.mul(out=AB[:], in_=AB[:], mul=-1.0), reads=["AB"], writes=["AB"])
        wsfb = RB(24 * KB, F32, [4, 128])
        wsf = wsfb.ap
        DMA("sp", wsf, w_s_b.rearrange("g t s -> t g s"), [], ["wsf"] + wsfb.k(), CK)
        for g in range(4):
            ASEL(wsf[:, g, :], [[-1, 128]], ALU.is_ge, 0.0, 0, 1, ["wsf"], ["wsf"])
        b0 = bank()
        for g in range(4):
            TR(ps[b0][:, g * 128:(g + 1) * 128], wsf[:, g, :], identf[:], ["wsf", "identf"] + wsfb.k(), [("ps", b0)] + wsfb.k())
        CP("act", wsT[:], ps[b0][:, :].rearrange("p (g t) -> p g t", t=128), [("ps", b0)], ["wsT"])
        tbb = RB(28 * KB, F32, [768])
        tb = tbb.ap[0:8, :]
        DMA("sp", tb[:, 0:257], rel_bias, [], ["tb"] + tbb.k(), CK)
        CP("act", tb[:, 257:768], tb[:, 256:257].broadcast_to([8, 511]), ["tb"], ["tb2"])
        DMA("sp", tpad, tb, ["tb", "tb2"], ["tpadA", "tpadB"] + tbb.k(), "tpad")
        bm = RB(0, F32, [8, 640])
        MS("pool", bm.ap, -30000.0, [], bm.k())
        for p in range(128):
            a, j = p // 64, p % 64
            DMA("sp" if p % 2 == 0 else "pool", bm.ap[p:p + 1, :, 64 * a:64 * a + 576], tpad[:, 128 - j:128 - j + 576].unsqueeze(0),
                ["tpadA", "tpadB"] + bm.k(), [("bmrow", p)], "bm")
        P.total_keys.add("bm")
        CP("act", EB[:], bm.ap, [("bmrow", p) for p in range(128)] + bm.k(), ["EB"] + bm.k())

        for gi, (src, nkc, cols, grp) in enumerate(gran):
            dst = scr[gi][:, 0:nkc * cols].rearrange("p (k c) -> p k c", c=cols)
            DMA("pool", dst, src.rearrange("(k p) c -> p k c", p=128), [], [("scr", gi)], ("scr", grp))

        slot_ctr = [0]

        def wload(gi):
            s = slot_ctr[0] % NSLOT
            slot_ctr[0] += 1
            src, nkc, cols, grp = gran[gi]
            n = nkc * cols
            DMA("sp", slots[s][:, 0:n], scr[gi][:, 0:n], [("scr", gi)], [("slot", s)], ("slot", s))
            return slots[s][:, 0:n].rearrange("p (k c) -> p k c", c=cols), ("slot", s)

        def mixk(kc):
            return [("mixT", kc, 0), ("mixT", kc, 1)]

        def hk(kc, nb):
            return [("hT", kc, b) for b in range(nb)]

        def norm(nidx, nt, nb, final_out=None):
            pb = bank()
            for kc in range(8):
                s = sq[kc % 2]
                ACT(s[:, :nt], hT[:, kc, :nt], AF.Square, hk(kc, nb), [("sq", kc % 2)])
                MM(ps[pb][:, :nt], onesD[:], s[:, :nt], ["onesD", ("sq", kc % 2)], [("ps", pb)], start=(kc == 0), stop=(kc == 7))
            ACT(rstd[:, :nt], ps[pb][:, :nt], AF.Ln, [("ps", pb)], ["rstd"], bias=EPS, scale=1.0)
            ACT(rstd[:, :nt], rstd[:, :nt], AF.Exp, ["rstd"], ["rstd"], scale=-0.5)
            for kc in range(8):
                if final_out is None:
                    STT(hnT[:, kc, :nt], hT[:, kc, :nt], gcol[:, nidx, kc:kc + 1], rstd[:, :nt], ALU.mult, ALU.mult,
                        hk(kc, nb) + ["rstd"] + GC, [("hnT", kc)])
                else:
                    STT(hT[:, kc, :nt], hT[:, kc, :nt], gcol[:, nidx, kc:kc + 1], rstd[:, :nt], ALU.mult, ALU.mult,
                        hk(kc, nb) + ["rstd"] + GC, hk(kc, nb))

        HN = [("hnT", kc) for kc in range(8)]

        def proj_fm(slot, sk, mc, nt, rhsbuf=None, rk=None, nk=8):
            pb = bank()
            for kc in range(nk):
                if rhsbuf is None:
                    rhs, rkey = hnT[:, kc, :nt], [("hnT", kc)]
                else:
                    rhs, rkey = rhsbuf(kc), rk(kc)
                MM(ps[pb][:, :nt], slot[:, kc, mc * 128:(mc + 1) * 128], rhs, [sk] + rkey, [("ps", pb)], start=(kc == 0), stop=(kc == nk - 1))
            return pb

        def proj_tm(slot, sk, b, cols=512):
            pb = bank()
            for kc in range(8):
                MM(ps[pb][:, :cols], hnT[:, kc, b * 128:(b + 1) * 128], slot[:, kc, 0:cols], [sk, ("hnT", kc)], [("ps", pb)], start=(kc == 0), stop=(kc == 7))
            return pb

        def out_proj(gids, src_ap, src_keys, nt, nb, nk):
            for gi_i, gi in enumerate(gids):
                slot, sk = wload(gi)
                ncm = gran[gi][2] // 128
                for m in range(ncm):
                    mc = gi_i * ncm + m
                    pb = proj_fm(slot, sk, m, nt, rhsbuf=lambda kc: src_ap(kc, nt), rk=src_keys, nk=nk)
                    TT("dve", hT[:, mc, :nt], hT[:, mc, :nt], ps[pb][:, :nt], ALU.add, hk(mc, nb) + [("ps", pb)], hk(mc, nb))

        def ffn(l, nt, nb):
            norm(1 + 2 * l, nt, nb)
            for i in range(6):
                sg, skg = wload(G["gate%d" % l][i])
                su, sku = wload(G["up%d" % l][i])
                for m in range(gran[G["gate%d" % l][i]][2] // 128):
                    j = i * 4 + m
                    pg = proj_fm(sg, skg, m, nt)
                    pu = proj_fm(su, sku, m, nt)
                    ACT(sgb.ap[:, j % 2, :nt], ps[pg][:, :nt], AF.Silu, [("ps", pg)], sgb.kr(j % 2))
                    TT("dve", actT.ap[:, j, :nt], sgb.ap[:, j % 2, :nt], ps[pu][:, :nt], ALU.mult, sgb.kr(j % 2) + [("ps", pu)], actT.kr(j))
            out_proj(G["down%d" % l], lambda kc, nt: actT.ap[:, kc, :nt], lambda kc: actT.kr(kc), nt, nb, 22)

        small_ctr = [0]

        def sm(n):
            o = small_ctr[0] % (64 // 8) * 8
            small_ctr[0] += 1
            return small[:, o:o + n], ("small", o)

        def layer0(ti, nt, nb, first, last, sample, seq):
            norm(0, nt, nb)
            if first and not sample:
                MS("dve", uu.ap[:, :, 0:2], 0.0, [], uu.k())
            elif first and sample:
                for c in range(4):
                    DMA("sp", uu.ap[:, c, 0:2], sca[:, c * 128:(c + 1) * 128].rearrange("k p -> p k"), [], uu.kr(c, 0, 2), "haloA", slow=True)
            else:
                CP("dve", uu.ap[:, :, 0:2], haloA[:], ["haloA"], uu.k())
            for gi_i, nm in enumerate(["xa", "gb", "gc", "u"]):
                slot, sk = wload(G["in_ab"][gi_i])
                for mc in range(4):
                    pb = proj_fm(slot, sk, mc, nt)
                    if nm == "xa":
                        CP("act", uu.ap[:, mc, 2:2 + nt], ps[pb][:, :nt], [("ps", pb)], uu.kr(mc))
                    elif nm == "gb":
                        CP("act", gbuf.ap[:, mc, :nt], ps[pb][:, :nt], [("ps", pb)], gbuf.kr(mc))
                    elif nm == "gc":
                        TT("dve", uu.ap[:, mc, 2:2 + nt], uu.ap[:, mc, 2:2 + nt], ps[pb][:, :nt], ALU.mult, uu.kr(mc) + [("ps", pb)], uu.kr(mc))
                    else:
                        ACT(gu.ap[:, mc, :nt], ps[pb][:, :nt], AF.Gelu, [("ps", pb)], gu.kr(mc))
            nv = 64 if sample else nt
            CP("dve", haloA[:], uu.ap[:, :, nv:nv + 2], uu.k(), ["haloA"])
            for c in range(4):
                a = cacc.ap[:, c % 2, :nt]
                ak = cacc.kr(c % 2)
                TS("dve", a, uu.ap[:, c, 0:nt], cwa[:, c, 0:1], None, ALU.mult, None, uu.kr(c) + CW, ak)
                STT(a, uu.ap[:, c, 1:1 + nt], cwa[:, c, 1:2], a, ALU.mult, ALU.add, uu.kr(c) + CW + ak, ak)
                STT(a, uu.ap[:, c, 2:2 + nt], cwa[:, c, 2:3], a, ALU.mult, ALU.add, uu.kr(c) + CW + ak, ak)
                TT("dve", mixT[:, c, :nt], a, gbuf.ap[:, c, :nt], ALU.mult, ak + gbuf.kr(c), mixk(c))
            slot, sk = wload(G["in_ab"][4])
            for b in range(nb):
                pb = proj_tm(slot, sk, b)
                vs, vsk = sm(4)
                MS("dve", vs, 0.0, [], [vsk])
                ACT(vtok.ap[:, b, :], ps[pb][:, :], AF.Gelu, [("ps", pb), vsk], vtok.kr(b) + [vsk], accum_out=vs[:, 0:1])
                TS("dve", vs[:, 1:2], vs[:, 0:1], -1.0 / 512, None, ALU.mult, None, [vsk], [vsk])
                ACT(ftmp.ap[:, b % 2, :], vtok.ap[:, b, :], AF.Square, vtok.kr(b) + [vsk], ftmp.kr(b % 2) + [vsk], bias=vs[:, 1:2], scale=1.0, accum_out=vs[:, 2:3])
                ACT(vs[:, 3:4], vs[:, 2:3], AF.Sqrt, [vsk], [vsk], bias=EPS, scale=1.0 / 512)
                RCP(vs[:, 3:4], vs[:, 3:4], [vsk], [vsk])
                TS("dve", vtok.ap[:, b, :], vtok.ap[:, b, :], vs[:, 1:2], vs[:, 3:4], ALU.add, ALU.mult, vtok.kr(b) + [vsk], vtok.kr(b))
                TT("dve", vtok.ap[:, b, :], vtok.ap[:, b, :], lngB[:], ALU.mult, vtok.kr(b) + ["lngB"], vtok.kr(b))
                TT("dve", vtok.ap[:, b, :], vtok.ap[:, b, :], lnbB[:], ALU.add, vtok.kr(b) + ["lnbB"], vtok.kr(b))
                CP("act", vln.ap[:, b, :], vtok.ap[:, b, :], vtok.kr(b), vln.kr(b))
                if sample:
                    DMA("act", vbs, vtok.ap[0:64, 0, :], vtok.kr(0), [], "o_vbs")
            for g in range(4):
                pb = bank()
                for b in range(nb):
                    MM(ps[pb][:, b * 128:(b + 1) * 128], vln.ap[:, b, g * 128:(g + 1) * 128], wsT[:, g, :], vln.kr(b) + ["wsT"], [("ps", pb)])
                f = ftmp.ap[:, g % 2, :nt]
                TT("dve", f.rearrange("p (b t) -> p b t", t=128), ps[pb][:, :nt].rearrange("p (b t) -> p b t", t=128),
                   bsB[:, g * 128:(g + 1) * 128].unsqueeze(1).broadcast_to([128, nb, 128]), ALU.add, [("ps", pb), "bsB"], ftmp.kr(g % 2))
                TT("dve", mixT[:, 4 + g, :nt], f, gu.ap[:, g, :nt], ALU.mult, ftmp.kr(g % 2) + gu.kr(g), mixk(4 + g))
            if last or sample:
                CP("dve", oca[:], haloA[:], ["haloA"], ["oca"])
                dst = cas[0] if sample else cap[seq]
                for c in range(4):
                    DMA("sp", dst[:, c * 128:(c + 1) * 128].rearrange("k p -> p k"), oca[:, c, :], ["oca"], [], "o_oca", slow=True)
            out_proj(G["out_ab"], lambda kc, nt: mixT[:, kc, :nt], mixk, nt, nb, 8)

        def layer1(ti, nt, nb, first, last, sample, seq):
            import os
            cur, prv = ti % 2, (ti + 1) % 2
            norm(2, nt, nb)
            has_hist = (not first) or sample
            slot, sk = wload(G["in_cd"][0])
            for mc in range(4):
                pb = proj_fm(slot, sk, mc, nt)
                ACT(qT.ap[:, mc, :nt], ps[pb][:, :nt], AF.Copy, [("ps", pb)], qT.kr(mc), scale=0.125)
            slot, sk = wload(G["in_cd"][1])
            for mc in range(4):
                pb = proj_fm(slot, sk, mc, nt)
                CP("act", kT[cur][:, mc, :nt], ps[pb][:, :nt], [("ps", pb)], [("kT", cur, mc)])
            if last or sample:
                for b in range(nb):
                    pb = proj_tm(slot, sk, b)
                    kv = kvst[b % 2]
                    CP("act", kv[:], ps[pb][:, :], [("ps", pb)], [("kvst", b % 2)])
                    if sample:
                        DMA("act", kcs, kv[0:64, :], [("kvst", b % 2)], [], ("o_kv", b % 2))
                    else:
                        DMA("act", kcp[seq * 512 + b * 128: seq * 512 + (b + 1) * 128, :], kv[:], [("kvst", b % 2)], [], ("o_kv", b % 2))
            slot, sk = wload(G["in_cd"][2])
            for b in range(nb):
                pb = proj_tm(slot, sk, b)
                CP("dve", bass.AP(Vx[cur], b * 768, [[4 * 768, 128], [192, 4], [128, 2], [1, 64]]),
                   ps[pb][:, :].rearrange("p (r two c) -> p r two c", two=2, c=64), [("ps", pb)], [("Vx", cur, b)])
                if last or sample:
                    kv = kvst[b % 2]
                    CP("act", kv[:], ps[pb][:, :], [("ps", pb)], [("kvst", b % 2)])
                    if sample:
                        DMA("act", vcs, kv[0:64, :], [("kvst", b % 2)], [], ("o_kv", b % 2))
                    else:
                        DMA("act", vcp[seq * 512 + b * 128: seq * 512 + (b + 1) * 128, :], kv[:], [("kvst", b % 2)], [], ("o_kv", b % 2))
            slot, sk = wload(G["in_cd"][3])
            for b in range(nb):
                pb = proj_tm(slot, sk, b)
                ACT(zs.ap[:, b, :], ps[pb][:, :], AF.Silu, [("ps", pb)], zs.kr(b))
            if first and not sample:
                MS("dve", xbc.ap[:, :, 0:3], 0.0, [], xbc.k())
            elif first and sample:
                for c in range(8):
                    DMA("sp", scdf.ap[:, c, :], scd[:, c * 128:(c + 1) * 128].rearrange("k p -> p k"), [], [("scdf", c)], "haloD", slow=True)
                CP("dve", xbc.ap[:, :, 0:3], scdf.ap, [("scdf", c) for c in range(8)], xbc.k())
            else:
                CP("dve", xbc.ap[:, :, 0:3], haloD[:], ["haloD"], xbc.k())
            for gg in range(2):
                slot, sk = wload(G["in_cd"][4 + gg])
                for mc in range(4):
                    pb = proj_fm(slot, sk, mc, nt)
                    CP("act", xbc.ap[:, gg * 4 + mc, 3:3 + nt], ps[pb][:, :nt], [("ps", pb)], xbc.kr(gg * 4 + mc))
            nv = 64 if sample else nt
            CP("dve", haloD[:], xbc.ap[:, :, nv:nv + 3], xbc.k(), ["haloD"])
            if last or sample:
                CP("dve", ocd[:], xbc.ap[:, :, nv:nv + 3], xbc.k(), ["ocd"])
                dst = cds[0] if sample else cdp[seq]
                for c in range(8):
                    DMA("sp", dst[:, c * 128:(c + 1) * 128].rearrange("k p -> p k"), ocd[:, c, :], ["ocd"], [], "o_ocd", slow=True)
            if os.environ.get("KPRE"):
                for _i in range(4):
                    _pb = bank()
                    MM(ps[_pb][:, :512], hnT[:, 0, 0:128], hnT[:, 1, 0:512], [("hnT", 0), ("hnT", 1)], [("ps", _pb)])
            pd = bank()
            for b in range(nb):
                for kc in range(8):
                    MM(ps[pd][:, b * 128:(b + 1) * 128], hnT[:, kc, b * 128:(b + 1) * 128], wdt[:, kc, :], [("hnT", kc), "wdt"], [("ps", pd)], start=(kc == 0), stop=(kc == 7))
            if os.environ.get("KPOST") and ti == int(os.environ.get("KTILE", ti)):
                for _i in range(4):
                    _pb = bank()
                    MM(ps[_pb][:, :512], hnT[:, 0, 0:128], hnT[:, 1, 0:512], [("hnT", 0), ("hnT", 1)], [("ps", _pb)])
            nb8 = nb * 8
            xd = dts[:, 0, :nb8]; ax = dts[:, 1, :nb8]; dtv = dts[:, 2, :nb8]; dta = dts[:, 3, :nb8]
            DK = ["dts"]
            TT("dve", xd.rearrange("p (b h) -> p b h", h=8), ps[pd][:, :nb * 128].rearrange("p (b h) -> p b h", h=128)[:, :, 0:8],
               dtbB[:].unsqueeze(1).broadcast_to([128, nb, 8]), ALU.add, [("ps", pd), "dtbB"], DK)
            TS("dve", ax, xd, -1.0, None, ALU.mult, None, DK, DK)
            TT("dve", ax, ax, xd, ALU.max, DK, DK)
            ACT(ax, ax, AF.Exp, DK, DK, scale=-1.0)
            ACT(ax, ax, AF.Ln, DK, DK, bias=1.0, scale=1.0)
            TS("dve", dtv, xd, 0.0, None, ALU.max, None, DK, DK)
            TT("dve", dtv, dtv, ax, ALU.add, DK, DK)
            TT("dve", dta.rearrange("p (b h) -> p b h", h=8), dtv.rearrange("p (b h) -> p b h", h=8),
               AB[:].unsqueeze(1).broadcast_to([128, nb, 8]), ALU.mult, DK + ["AB"], DK)
            CP("dve", dtb[:, 0, :nb8], dta, DK, ["dtb"])
            TT("dve", dtb[:, 1, :nb8], dta, dtb[:, 0, :nb8], ALU.subtract, DK + ["dtb"], ["dtb"])
            import os
            st1 = int(os.environ.get('KSTOP1', '99'))
            if st1 < 1: return
            for c in range(8):
                a = cacd.ap[:, :nt]
                ak = cacd.k()
                TS("dve", a, xbc.ap[:, c, 0:nt], cwd[:, c, 0:1], None, ALU.mult, None, xbc.kr(c) + CW, ak)
                for k in range(1, 4):
                    STT(a, xbc.ap[:, c, k:k + nt], cwd[:, c, k:k + 1], a, ALU.mult, ALU.add, xbc.kr(c) + CW + ak, ak)
                ACT(xc.ap[:, c, :nt], a, AF.Silu, ak + CW, xc.kr(c), bias=cbd[:, c:c + 1], scale=1.0)

            if st1 < 2: return
            nqc = 1 if sample else 8
            att_state = {}

            def attA(h):
                pr, hh = h // 2, h % 2
                rows = slice(hh * 64, hh * 64 + 64)
                pt = PT.ap[:, h % 2, :]
                segs = {}
                off = 0
                for bb in range(-4, nb):
                    if bb < 0 and not has_hist:
                        continue
                    qlo, qhi = max(0, 2 * bb), min(nqc - 1, 2 * bb + 9)
                    if qhi < qlo:
                        continue
                    ncols = 128 if sample else (qhi - qlo + 1) * 64
                    kbuf, kkey, kcol = (kT[prv], ("kT", prv, pr), (4 + bb) * 128) if bb < 0 else (kT[cur], ("kT", cur, pr), bb * 128)
                    pb = bank()
                    for kh in range(2):
                        MM(ps[pb][kh * 64:(kh + 1) * 64, :ncols], kbuf[rows, pr, kcol + kh * 64:kcol + kh * 64 + 64],
                           qT.ap[rows, pr, qlo * 64:qlo * 64 + ncols], [kkey] + qT.kr(pr), [("ps", pb)], start=True, stop=False)
                    qrel = qlo - 2 * bb
                    assert qrel * 64 + ncols <= 640
                    MM(ps[pb][:, :ncols], identb[:], EB[:, h, qrel * 64:qrel * 64 + ncols], ["identb", "EB"], [("ps", pb)], start=False, stop=True)
                    ACT(pt[:, off:off + ncols], ps[pb][:, :ncols], AF.Exp, [("ps", pb)], PT.kr(h % 2))
                    segs[bb] = (off, qlo)
                    off += ncols
                att_state[h] = segs

            def attB(h):
                pr, hh = h // 2, h % 2
                rows = slice(hh * 64, hh * 64 + 64)
                pt = PT.ap[:, h % 2, :]
                segs = att_state[h]
                po = bank()
                for j in range(max(1, nqc // 2)):
                    bbs = [bb for bb in range(j - 4, j + 1) if bb in segs]
                    for i, bb in enumerate(bbs):
                        o, qlo = segs[bb]
                        vbuf, vkey = (Vx[prv], ("Vx", prv, 4 + bb)) if bb < 0 else (Vx[cur], ("Vx", cur, bb))
                        vb = (4 + bb) if bb < 0 else bb
                        MM(ps[po][:, j * 128:(j + 1) * 128], vbuf[:, vb, pr, hh * 64:hh * 64 + 128], pt[:, o + (2 * j - qlo) * 64:o + (2 * j - qlo) * 64 + 128],
                           [vkey] + PT.kr(h % 2), [("ps", po)], start=(i == 0), stop=(i == len(bbs) - 1))
                drows = slice(64, 128) if hh == 0 else slice(0, 64)
                ACT(rden.ap[rows, :nt], ps[po][drows, :nt], AF.Ln, [("ps", po)], rden.k())
                ACT(rden.ap[rows, :nt], rden.ap[rows, :nt], AF.Exp, rden.k(), rden.k(), scale=-1.0)
                TT("dve", mixT[rows, pr, :nt], ps[po][rows, :nt], rden.ap[rows, :nt], ALU.mult, [("ps", po)] + rden.k(), [("mixT", pr, hh)])

            if st1 < 3: return
            if first and not sample:
                MS("dve", H[:], 0.0, [], ["H"])
                MS("dve", Hbf[0][:], 0.0, [], [("Hbf", 0)])
            hb = [0]
            kssd = int(os.environ.get('KSSD', '99'))

            def ssd_block(b):
                tok = slice(b * 128, (b + 1) * 128)
                pbt = 4
                for c in range(4):
                    TR(psb[pbt][:, c * 128:(c + 1) * 128], xc.ap[:, c, tok], identb[:], xc.kr(c) + ["identb"], [("ps", pbt)])
                CP("act", xtok.ap, psb[pbt][:, 0:512], [("ps", pbt)], xtok.k())
                pbt = 4
                for g in range(2):
                    TR(psb[pbt][:, g * 128:(g + 1) * 128], xc.ap[:, 4 + g, tok], identb[:], xc.kr(4 + g) + ["identb"], [("ps", pbt)])
                CP("act", btok.ap, psb[pbt][:, 0:256], [("ps", pbt)], btok.k())
                yield
                dt_b = dts[:, 2, b * 8:(b + 1) * 8]
                DK = ["dts"]
                Lh3 = Lhi.ap.rearrange("p (h s) -> p h s", s=128)
                Ll3 = Llo.ap.rearrange("p (h s) -> p h s", s=128)
                TT("dve", Lh3, dtb[:, 0, b * 8:(b + 1) * 8].unsqueeze(2).broadcast_to([128, 8, 128]), SLblkb[:].unsqueeze(1).broadcast_to([128, 8, 128]),
                   ALU.mult, ["dtb", "SLblkb"], Lhi.k())
                TT("dve", Ll3, dtb[:, 1, b * 8:(b + 1) * 8].unsqueeze(2).broadcast_to([128, 8, 128]), SLblkb[:].unsqueeze(1).broadcast_to([128, 8, 128]),
                   ALU.mult, ["dtb", "SLblkb"], Llo.k())
                yield
                pbs = 5
                for h in range(8):
                    MM(ps[pbs][:, h * 64:(h + 1) * 64], Lh3[:, h, :], Ucb[:], Lhi.k() + ["Ucb"], [("ps", pbs)], start=True, stop=False)
                    MM(ps[pbs][:, h * 64:(h + 1) * 64], Ll3[:, h, :], Ucb[:], Llo.k() + ["Ucb"], [("ps", pbs)], start=False, stop=True)
                yield
                pbc = 6
                MM(ps[pbc][:, 256:264], Ublkb[:], dtb[:, 0, b * 8:(b + 1) * 8], ["dtb", "Ublkb"], [("ps", pbc)], start=True, stop=False)
                MM(ps[pbc][:, 256:264], Ublkb[:], dtb[:, 1, b * 8:(b + 1) * 8], ["dtb", "Ublkb"], [("ps", pbc)], start=False, stop=True)
                for ch in range(2):
                    MM(ps[pbc][:, 264 + ch * 8:272 + ch * 8], onesAB[:, ch, :], dtb[:, 0, b * 8:(b + 1) * 8], ["dtb", "onesAB"], [("ps", pbc)], start=True, stop=False)
                    MM(ps[pbc][:, 264 + ch * 8:272 + ch * 8], onesAB[:, ch, :], dtb[:, 1, b * 8:(b + 1) * 8], ["dtb", "onesAB"], [("ps", pbc)], start=False, stop=True)
                for g in range(2):
                    MM(ps[pbc][:, g * 128:(g + 1) * 128], xc.ap[:, 4 + g, tok], xc.ap[:, 6 + g, tok], xc.kr(4 + g) + xc.kr(6 + g), [("ps", pbc)])
                ACT(decT.ap, ps[pbs][:, :], AF.Exp, [("ps", pbs), ("ps", pbc)], decT.k())
                ACT(ecd[:, 0:24], ps[pbc][:, 256:280], AF.Exp, [("ps", pbc)], ["ecd"])
                TT("dve", cbm.ap, ps[pbc][:, 0:256].rearrange("p (g t) -> p g t", t=128),
                   Ublk[:].unsqueeze(1).broadcast_to([128, 2, 128]), ALU.mult, [("ps", pbc), "Ublk"], cbm.k())
                yield
                P.op("act", lambda e: e.memzero(MTb.ap), reads=[], writes=MTb.k())
                for ch in range(2):
                    rw = slice(ch * 64, ch * 64 + 64)
                    for g in range(2):
                        TT("dve", MTb.ap[rw, g * 4:(g + 1) * 4, ch * 64:(ch + 1) * 64],
                           decT.ap[rw, g * 256:(g + 1) * 256].rearrange("p (h t) -> p h t", t=64),
                           cbm.ap[rw, g, ch * 64:(ch + 1) * 64].unsqueeze(1).broadcast_to([64, 4, 64]), ALU.mult, decT.k() + cbm.k() + MTb.k(), MTb.k())
                dd, ddk = sm(8)
                TT("dve", dd, dt_b, decT.ap.rearrange("p (h t) -> p h t", t=64)[:, :, 63], ALU.mult, DK + decT.k(), [ddk])
                x3 = xtok.ap.rearrange("p (h q) -> p h q", q=64)
                TT("dve", xdt.ap.rearrange("p (h q) -> p h q", q=64), x3, dt_b.unsqueeze(2).broadcast_to([128, 8, 64]), ALU.mult, xtok.k() + DK, xdt.k())
                P.op("act", lambda e: e.memzero(xddz.ap), reads=[], writes=xddz.k())
                for ch in range(2):
                    rw = slice(ch * 64, ch * 64 + 64)
                    TT("dve", xddz.ap[rw, ch, :].rearrange("p (h q) -> p h q", q=64), x3[rw], dd[rw].unsqueeze(2).broadcast_to([64, 8, 64]), ALU.mult,
                       xtok.k() + [ddk] + xddz.k(), xddz.k())
                yield
                pby = 5
                for h in range(8):
                    MM(ps[pby][:, h * 64:(h + 1) * 64], MTb.ap[:, h, :], xdt.ap[:, h * 64:(h + 1) * 64], MTb.k() + xdt.k(), [("ps", pby)])
                yield
                pbi = [None, None]
                for ch in range(2):
                    hcur = hb[0] % 2
                    pbi[ch] = 6 if ch == 0 else 7
                    for g in range(2):
                        MM(ps[pbi[ch]][:, g * 256:(g + 1) * 256], xc.ap[:, 6 + g, tok], Hbf[hcur][:, g * 256:(g + 1) * 256], xc.kr(6 + g) + [("Hbf", hcur)], [("ps", pbi[ch])])
                    rw = slice(ch * 64, ch * 64 + 64)
                    TT("dve", t1.ap[rw].rearrange("p (h q) -> p h q", q=64), ps[pbi[ch]][rw, :].rearrange("p (h q) -> p h q", q=64),
                       ecd[rw, 0:8].unsqueeze(2).broadcast_to([64, 8, 64]), ALU.mult, [("ps", pbi[ch]), "ecd"] + t1.k(), t1.k())
                    if sample and ch == 1:
                        continue
                    pS = 4
                    for g in range(2):
                        MM(ps[pS][:, g * 256:(g + 1) * 256], btok.ap[:, g * 128:(g + 1) * 128], xddz.ap[:, ch, g * 256:(g + 1) * 256], btok.k() + xddz.k(), [("ps", pS)])
                    H3 = H[:].rearrange("p (h q) -> p h q", q=64)
                    TT("dve", H3, H3, ecd[:, 8 + ch * 8:16 + ch * 8].unsqueeze(2).broadcast_to([128, 8, 64]), ALU.mult, ["H", "ecd"], ["H"])
                    TT("dve", H[:], H[:], ps[pS][:, :], ALU.add, ["H", ("ps", pS)], ["H"])
                    hb[0] += 1
                    CP("act", Hbf[hb[0] % 2][:], H[:], ["H"], [("Hbf", hb[0] % 2)])
                yield
                TT("dve", t2.ap.rearrange("p (h q) -> p h q", q=64), x3, dskB[:].unsqueeze(2).broadcast_to([128, 8, 64]), ALU.mult, xtok.k() + ["dskB"], t2.k())
                TT("dve", t1.ap, t1.ap, t2.ap, ALU.add, t1.k() + t2.k(), t1.k())
                TT("dve", yb.ap, ps[pby][:, :], t1.ap, ALU.add, [("ps", pby), ("ps", pbi[1])] + t1.k(), yb.k())
                TT("dve", yb.ap, yb.ap, zs.ap[:, b, :], ALU.mult, yb.k() + zs.kr(b), yb.k())
                yield
                ss_, ssk = sm(4)
                MS("dve", ss_, 0.0, [], [ssk])
                for g in range(2):
                    ACT(t2.ap[:, g * 256:(g + 1) * 256], yb.ap[:, g * 256:(g + 1) * 256], AF.Square, yb.k() + [ssk], t2.k() + [ssk], accum_out=ss_[:, g:g + 1])
                ACT(ss_[:, 2:4], ss_[:, 0:2], AF.Sqrt, [ssk], [ssk], bias=EPS, scale=1.0 / 256)
                RCP(ss_[:, 2:4], ss_[:, 2:4], [ssk], [ssk])
                TT("dve", yb.ap.rearrange("p (g q) -> p g q", q=256), yb.ap.rearrange("p (g q) -> p g q", q=256),
                   ss_[:, 2:4].unsqueeze(2).broadcast_to([128, 2, 256]), ALU.mult, yb.k() + [ssk], yb.k())
                TT("dve", ynb.ap, yb.ap, ngdB[:], ALU.mult, yb.k() + ["ngdB"], ynb.k())
                pbt = 4
                for c in range(4):
                    TR(psb[pbt][:, c * 128:(c + 1) * 128], ynb.ap[:, c * 128:(c + 1) * 128], identb[:], ynb.k() + ["identb"], [("ps", pbt)])
                CP("act", mixT[:, 4:8, tok], psb[pbt][:, 0:512].rearrange("p (c t) -> p c t", t=128), [("ps", pbt)], [k_ for c_ in range(4, 8) for k_ in mixk(c_)])
            att_units = [("A", 0), ("A", 1)]
            for h in range(8):
                att_units.append(("B", h))
                if h + 2 < 8:
                    att_units.append(("A", h + 2))

            def ssd_all():
                for b_ in range(nb):
                    yield from ssd_block(b_)
                    yield

            ring[:] = [0, 1, 2, 3]
            gen = ssd_all()
            ssd_done = False
            ai = 0
            while ai < 3:
                kind, i = att_units[ai]
                ai += 1
                (attA if kind == "A" else attB)(i)
            while ai < len(att_units) or not ssd_done:
                for _ in range(2):
                    if not ssd_done:
                        try:
                            next(gen)
                        except StopIteration:
                            ssd_done = True
                if ai < len(att_units):
                    kind, i = att_units[ai]
                    ai += 1
                    (attA if kind == "A" else attB)(i)
            ring[:] = list(range(8))
            if st1 < 4: return
            if last or sample:
                pbt = bank()
                for c in range(4):
                    TR(ps[pbt][:, c * 128:(c + 1) * 128], H[:, c * 128:(c + 1) * 128], identf[:], ["H", "identf"], [("ps", pbt)])
                CP("act", ost[:], ps[pbt][:, :].rearrange("p (c n) -> p c n", n=128), [("ps", pbt)], ["ost"])
                dst = sss if sample else ssp[seq * 512:(seq + 1) * 512, :]
                DMA("act", dst.rearrange("(c p) n -> p c n", p=128), ost[:], ["ost"], [], "o_ost")
            out_proj(G["out_cd"], lambda kc, nt: mixT[:, kc, :nt], mixk, nt, nb, 8)

        dts = T("dts", [128, 4, 32])
        ecd = T("ecd", [128, 24])

        def load_x(src_rows, nb, sample):
            for b in range(nb):
                xs_ = xin[b % 2]
                if sample:
                    MS("dve", xs_[:], 0.0, [], [("xin", b % 2)])
                    DMA("sp", xs_[0:64, :], src_rows, [("xin", b % 2)], [("xin", b % 2)], ("xin", b % 2))
                else:
                    DMA("sp", xs_[:], src_rows[b * 128:(b + 1) * 128, :], [], [("xin", b % 2)], ("xin", b % 2))
                for half in range(2):
                    pb = bank()
                    for j in range(4):
                        kc = half * 4 + j
                        TR(ps[pb][:, j * 128:(j + 1) * 128], xs_[:, kc * 128:(kc + 1) * 128], identf[:], [("xin", b % 2), "identf"], [("ps", pb)])
                    CP("act", hT[:, half * 4:half * 4 + 4, b * 128:(b + 1) * 128], ps[pb][:, :].rearrange("p (j t) -> p j t", t=128),
                       [("ps", pb)], [("hT", half * 4 + j, b) for j in range(4)])

        def store_y(dst_rows, nt, nb, sample):
            norm(4, nt, nb, final_out=True)
            for b in range(nb):
                y_ = yst[b % 2]
                for half in range(2):
                    pb = bank()
                    for j in range(4):
                        kc = half * 4 + j
                        TR(ps[pb][:, j * 128:(j + 1) * 128], hT[:, kc, b * 128:(b + 1) * 128], identf[:], hk(kc, nb) + ["identf"], [("ps", pb)])
                    CP("act", y_[:, half * 512:(half + 1) * 512], ps[pb][:, :], [("ps", pb)], [("yst", b % 2, half)])
                yk = [("yst", b % 2, 0), ("yst", b % 2, 1)]
                if sample:
                    DMA("act", dst_rows, y_[0:64, :], yk, [], ("o_y", b % 2))
                else:
                    DMA("act", dst_rows[b * 128:(b + 1) * 128, :], y_[:], yk, [], ("o_y", b % 2))

        def tile(ti, x_rows, y_rows, nt, nb, first, last, sample, seq):
            import os
            stop = int(os.environ.get("KSTOP", "99"))
            if stop < 1: return
            load_x(x_rows, nb, sample)
            if stop < 2: return
            layer0(ti, nt, nb, first, last, sample, seq)
            if stop < 3: return
            ffn(0, nt, nb)
            if stop < 4: return
            layer1(ti, nt, nb, first, last, sample, seq)
            if stop < 5: return
            ffn(1, nt, nb)
            if stop < 6: return
            store_y(y_rows, nt, nb, sample)

        gt = 0
        try:
          for s in range(NPS):
              for t in range(NTILE):
                  r0 = s * SEQ + t * 512
                  tile(gt, xp[r0:r0 + 512, :], yp[r0:r0 + 512, :], 512, 4, t == 0, t == NTILE - 1, False, s)
                  gt += 1
          cur, prv = gt % 2, (gt + 1) % 2
          DMA("sp", ckf.ap, ck.rearrange("(b p) c -> p b c", p=128), [], ckf.k(), "smp0")
          CP("dve", ckb.ap, ckf.ap, ckf.k(), ckb.k())
          for b in range(4):
              pbt = bank()
              for pr in range(4):
                  TR(psb[pbt][:, pr * 128:(pr + 1) * 128], ckb.ap[:, b, pr * 128:(pr + 1) * 128], identb[:], ckb.k() + ["identb"], [("ps", pbt)])
              CP("act", kT[prv][:, :, b * 128:(b + 1) * 128], psb[pbt][:, 0:512].rearrange("p (r t) -> p r t", t=128), [("ps", pbt)], [("kT", prv, pr) for pr in range(4)])
          DMA("sp", cvf.ap, cv.rearrange("(b p) c -> p b c", p=128), [], cvf.k(), "smp1")
          for b in range(4):
              CP("dve", bass.AP(Vx[prv], b * 768, [[4 * 768, 128], [192, 4], [128, 2], [1, 64]]),
                 cvf.ap[:, b, :].rearrange("p (r two c) -> p r two c", two=2, c=64), cvf.k(), [("Vx", prv, b)])
          DMA("sp", ssf.ap, ssm.rearrange("(c p) n -> p c n", p=128), [], ssf.k(), "smp2")
          pbt = bank()
          for c in range(4):
              TR(ps[pbt][:, c * 128:(c + 1) * 128], ssf.ap[:, c, :], identf[:], ssf.k() + ["identf"], [("ps", pbt)])
          CP("act", H[:], ps[pbt][:, :], [("ps", pbt)], ["H"])
          CP("act", Hbf[0][:], H[:], ["H"], [("Hbf", 0)])
          tile(gt, xs, ys, 128, 1, True, True, True, 0)


        except StopIteration:
            pass
        print('n_ops', len(P.ops))
        P.emit(st)
    return nc


_CACHE = {}


def _get_nc(NPS, SEQ):
    key = (NPS, SEQ)
    if key not in _CACHE:
        _CACHE[key] = build(NPS, SEQ)
    return _CACHE[key]


def run_cores(inputs, n_cores, NPS, SEQ):
    f = lambda a: np.ascontiguousarray(np.asarray(a, dtype=np.float32))
    I = {k: f(v) for k, v in inputs.items()}
    nc = _get_nc(NPS, SEQ)
    in_maps = []
    for c in range(n_cores):
        m = {
            "xp": I["x_prompt"][c * NPS:(c + 1) * NPS].reshape(NPS * SEQ, D),
            "xs": I["x_sample"][c],
            "ck": I["cache_k_c"][0, c].reshape(512, 512), "cv": I["cache_v_c"][0, c].reshape(512, 512),
            "sca": I["state_conv_a"][0, c], "scd": I["state_conv_d"][0, c], "ssm": I["state_ssm_d"][0, c].reshape(512, 128),
            "norm_mix": I["norm_mix"], "norm_ffn": I["norm_ffn"], "norm_final": I["norm_final"].reshape(1, D),
            "w_in_ab": I["w_in_ab"][0], "conv_w_a": I["conv_w_a"][0], "ln_g_b": I["ln_g_b"][0], "ln_b_b": I["ln_b_b"][0],
            "w_s_b": I["w_s_b"][0], "b_s_b": I["b_s_b"][0].reshape(512), "w_out_ab": I["w_out_ab"][0], "w_in_cd": I["w_in_cd"][0],
            "rel_bias_c": I["rel_bias_c"][0], "conv_w_d": I["conv_w_d"][0], "conv_b_d": I["conv_b_d"][0],
            "dt_bias_d": I["dt_bias_d"][0], "a_log_d": I["a_log_d"][0], "d_skip_d": I["d_skip_d"][0], "norm_g_d": I["norm_g_d"][0],
            "w_out_cd": I["w_out_cd"][0], "w_gate": I["w_gate"], "w_up": I["w_up"], "w_down": I["w_down"],
        }
        in_maps.append({k: np.ascontiguousarray(v) for k, v in m.items()})
    res = run_bass_kernel_spmd(nc, in_maps, core_ids=list(range(n_cores)))
    R_ = res.results
    cat = lambda k: np.concatenate([r[k] for r in R_], axis=0)
    B = n_cores * NPS
    outs = (
        cat("yp").reshape(B, SEQ, D),
        cat("ys").reshape(n_cores, 64, D),
        cat("cap").reshape(1, B, 2, 512),
        cat("cas").reshape(1, n_cores, 2, 512),
        cat("vbs").reshape(1, n_cores, 64, 512),
        cat("kcp").reshape(1, B, 512, 8, 64),
        cat("vcp").reshape(1, B, 512, 8, 64),
        cat("kcs").reshape(1, n_cores, 64, 8, 64),
        cat("vcs").reshape(1, n_cores, 64, 8, 64),
        cat("cdp").reshape(1, B, 3, 1024),
        cat("cds").reshape(1, n_cores, 3, 1024),
        cat("ssp").reshape(1, B, 8, 64, 128),
        cat("sss").reshape(1, n_cores, 8, 64, 128),
    )
    return tuple(np.ascontiguousarray(o, dtype=np.float32) for o in outs)


def kernel(**inputs):
    return run_cores(inputs, 8, 2, 4096)
```
